# Optimizing a Trainium2 kernel written in Bass

```python
import math
import jax, jax.numpy as jnp
from jax import lax
import numpy as np

D_MODEL = 1024
BATCH = 16
SEQ = 256
DEPTH = 2
DEC_BATCH = 4
DEC_SEQ = 4096
PAST_LEN = 256

GRID_W = 64
POOL_WINDOWS = (2, 4, 8, 16)
POOL_WIDTH = D_MODEL // 4
POOL_GROUP = POOL_WIDTH // len(POOL_WINDOWS)
HEAD_DIM = 64
N_HEADS = (D_MODEL // 2) // HEAD_DIM
N_KV_HEADS = 2
GQA_GROUP = N_HEADS // N_KV_HEADS
ATTN_WIDTH = N_HEADS * HEAD_DIM
KV_WIDTH = N_KV_HEADS * HEAD_DIM
WINDOW = 128
BLOCK = 128
ROPE_THETA = 10000.0
HY_WIDTH = D_MODEL - POOL_WIDTH - ATTN_WIDTH
HY_ORDER = 2
HY_EMB_DIM = 33
HY_FILTER_HIDDEN = 64
HY_MOD_SHIFT = 0.05
D_FF = 2816
IN_WIDTH = POOL_WIDTH + ATTN_WIDTH + 2 * KV_WIDTH + (HY_ORDER + 1) * HY_WIDTH
N_MOD = 9
NORM_EPS = 1e-6
NEG_INF = -1e30

kernel_name = "hybrid_pool_swa_hyena_diffusion_step"

F32 = jnp.float32


def rmsnorm(x, g):
    xf = x.astype(F32)
    y = xf * lax.rsqrt(jnp.mean(xf * xf, axis=-1, keepdims=True) + NORM_EPS)
    return (y * g.astype(F32)).astype(x.dtype)


def swiglu(h, wg, wu, wd):
    return (jax.nn.silu(h @ wg) * (h @ wu)) @ wd


def pool_mixer(u, w, scale):
    B, L, _ = u.shape
    uf = u.astype(F32)
    cs = jnp.concatenate([jnp.zeros((B, 1, POOL_WIDTH), F32), jnp.cumsum(uf, axis=1)], axis=1)
    t = jnp.arange(L)
    outs = []
    for gi, win in enumerate(POOL_WINDOWS):
        lo = jnp.clip(t - win // 2, 0, L)
        hi = jnp.clip(t + win // 2, 0, L)
        csg = cs[..., gi * POOL_GROUP:(gi + 1) * POOL_GROUP]
        mean = (csg[:, hi] - csg[:, lo]) / (hi - lo).astype(F32)[None, :, None]
        outs.append(mean - uf[..., gi * POOL_GROUP:(gi + 1) * POOL_GROUP])
    d = jnp.stack(outs, axis=2)
    y = jnp.einsum('blgc,gcd->blgd', d, w.astype(F32)).reshape(B, L, POOL_WIDTH)
    return (y * scale.astype(F32)).astype(u.dtype)


def axial_rope(x, pos_row, pos_col):
    half = HEAD_DIM // 2
    quarter = half // 2
    inv = ROPE_THETA ** (-jnp.arange(quarter, dtype=F32) / quarter)

    def rot(xa, pos):
        ang = pos.astype(F32)[:, None] * inv[None, :]
        cos = jnp.cos(ang)[None, :, None, :]
        sin = jnp.sin(ang)[None, :, None, :]
        x1, x2 = xa[..., :quarter], xa[..., quarter:]
        return jnp.concatenate([x1 * cos - x2 * sin, x2 * cos + x1 * sin], axis=-1)

    xf = x.astype(F32)
    return jnp.concatenate([rot(xf[..., :half], pos_row), rot(xf[..., half:], pos_col)], axis=-1).astype(x.dtype)


def context_attention(q, k, v, sink):
    B, Lc = q.shape[:2]
    nb = Lc // BLOCK
    qb = q.reshape(B, nb, BLOCK, N_KV_HEADS, GQA_GROUP, HEAD_DIM).transpose(1, 0, 2, 3, 4, 5)
    sink_b = jnp.broadcast_to(sink.astype(F32).reshape(1, N_KV_HEADS, GQA_GROUP, 1, 1),
                              (B, N_KV_HEADS, GQA_GROUP, BLOCK, 1))
    scale = HEAD_DIM ** -0.5

    def one_block(qi):
        s = jnp.einsum('bqkgd,bckd->bkgqc', qi, k, preferred_element_type=F32) * scale
        p = jax.nn.softmax(jnp.concatenate([s, sink_b], axis=-1), axis=-1)[..., :Lc]
        return jnp.einsum('bkgqc,bckd->bqkgd', p.astype(v.dtype), v, preferred_element_type=F32)

    out = lax.map(one_block, qb)
    return out.transpose(1, 0, 2, 3, 4, 5).reshape(B, Lc, ATTN_WIDTH).astype(q.dtype)


def latent_attention(q, k, v, k_ctx, v_ctx, sink):
    B, L = q.shape[:2]
    Lc = k_ctx.shape[1]
    nb = L // BLOCK
    qb = q.reshape(B, nb, BLOCK, N_KV_HEADS, GQA_GROUP, HEAD_DIM).transpose(1, 0, 2, 3, 4, 5)

    def windows(a):
        ap = jnp.pad(a, ((0, 0), (BLOCK, BLOCK), (0, 0), (0, 0))).reshape(B, nb + 2, BLOCK, N_KV_HEADS, HEAD_DIM)
        w = jnp.concatenate([ap[:, :-2], ap[:, 1:-1], ap[:, 2:]], axis=2)
        return w.transpose(1, 0, 2, 3, 4)

    kw, vw = windows(k), windows(v)
    nidx = jnp.arange(nb)[:, None, None]
    qpos = nidx * BLOCK + jnp.arange(BLOCK)[None, :, None]
    kpos = (nidx - 1) * BLOCK + jnp.arange(3 * BLOCK)[None, None, :]
    valid = (jnp.abs(kpos - qpos) <= WINDOW) & (kpos >= 0) & (kpos < L)
    sink_b = jnp.broadcast_to(sink.astype(F32).reshape(1, N_KV_HEADS, GQA_GROUP, 1, 1),
                              (B, N_KV_HEADS, GQA_GROUP, BLOCK, 1))
    scale = HEAD_DIM ** -0.5

    def one_block(args):
        qi, ki, vi, mi = args
        s_loc = jnp.einsum('bqkgd,bjkd->bkgqj', qi, ki, preferred_element_type=F32) * scale
        s_loc = jnp.where(mi, s_loc, NEG_INF)
        s_ctx = jnp.einsum('bqkgd,bckd->bkgqc', qi, k_ctx, preferred_element_type=F32) * scale
        p = jax.nn.softmax(jnp.concatenate([s_loc, s_ctx, sink_b], axis=-1), axis=-1).astype(v.dtype)
        o_loc = jnp.einsum('bkgqj,bjkd->bqkgd', p[..., :3 * BLOCK], vi, preferred_element_type=F32)
        o_ctx = jnp.einsum('bkgqc,bckd->bqkgd', p[..., 3 * BLOCK:3 * BLOCK + Lc], v_ctx, preferred_element_type=F32)
        return o_loc + o_ctx

    out = lax.map(one_block, (qb, kw, vw, valid))
    return out.transpose(1, 0, 2, 3, 4, 5).reshape(B, L, ATTN_WIDTH).astype(q.dtype)


def hyena_filters(L, w1, b1, w2, b2, w3, b3, freq, deltas):
    t = jnp.linspace(0.0, 1.0, L, dtype=F32)[:, None]
    bands = (HY_EMB_DIM - 1) // 2
    f = jnp.linspace(1e-4, bands - 1, bands, dtype=F32)[None, :]
    w = 2.0 * math.pi * jnp.arange(L, dtype=F32)[:, None] / L
    z = jnp.concatenate([t, jnp.cos(f * w), -jnp.sin(f * w)], axis=-1)
    fr = freq.astype(F32)
    h = jnp.sin(fr[0] * (z @ w1.astype(F32) + b1.astype(F32)))
    h = jnp.sin(fr[1] * (h @ w2.astype(F32) + b2.astype(F32)))
    h = (h @ w3.astype(F32) + b3.astype(F32)).reshape(L, 2, HY_ORDER, HY_WIDTH)
    decay = jnp.exp(-t[:, :, None, None] * jnp.abs(deltas.astype(F32))[None, None])
    h = h * (decay + HY_MOD_SHIFT)
    fwd, bwd = h[:, 0], h[:, 1]
    k = jnp.concatenate([fwd, jnp.zeros((1, HY_ORDER, HY_WIDTH), F32), jnp.flip(bwd[1:], axis=0)], axis=0)
    return k / (jnp.sum(jnp.abs(k), axis=0, keepdims=True) + 1e-6)


def short_conv(u, w, b):
    up = jnp.pad(u, ((0, 0), (1, 1), (0, 0)))
    return up[:, :-2] * w[0] + up[:, 1:-1] * w[1] + up[:, 2:] * w[2] + b


def fft_conv(z, k):
    L = z.shape[1]
    zf = jnp.fft.rfft(z, n=2 * L, axis=1)
    kf = jnp.fft.rfft(k, n=2 * L, axis=0)
    return jnp.fft.irfft(zf * kf[None], n=2 * L, axis=1)[:, :L]


def hyena_mixer(u, sw, sb, filt, bias_d):
    uc = short_conv(u, sw, sb).astype(F32)
    v, x1, x2 = jnp.split(uc, 3, axis=-1)
    bd = bias_d.astype(F32)
    z = v
    for o, gate in enumerate((x1, x2)):
        z = gate * (fft_conv(z, filt[:, o]) + bd[o] * z)
    return z.astype(u.dtype)


def trunk_layer(x, cond, p, pos, ctx_kv):
    B, L, _ = x.shape
    mod = (jax.nn.silu(cond.astype(F32)) @ p['ada_w'].astype(F32) + p['ada_b'].astype(F32)).astype(x.dtype)
    sh1, sc1, g1, sh2, sc2, g2, sh3, sc3, g3 = jnp.split(mod[:, None, :], N_MOD, axis=-1)

    h = rmsnorm(x, p['norm'][0]) * (1 + sc1) + sh1
    x = x + 0.5 * g1 * swiglu(h, p['wg'][0], p['wu'][0], p['wd'][0])

    h = rmsnorm(x, p['norm'][1]) * (1 + sc2) + sh2
    u = h @ p['w_in']
    s1 = POOL_WIDTH
    s2 = s1 + ATTN_WIDTH
    s3 = s2 + KV_WIDTH
    s4 = s3 + KV_WIDTH
    u_pool, q, k, v, u_hy = jnp.split(u, [s1, s2, s3, s4], axis=-1)
    q = rmsnorm(q.reshape(B, L, N_HEADS, HEAD_DIM), p['q_norm'])
    k = rmsnorm(k.reshape(B, L, N_KV_HEADS, HEAD_DIM), p['k_norm'])
    v = v.reshape(B, L, N_KV_HEADS, HEAD_DIM)
    if ctx_kv is None:
        a = context_attention(q, k, v, p['sink'])
        new_kv = (k, v)
    else:
        q = axial_rope(q, pos[0], pos[1])
        k = axial_rope(k, pos[0], pos[1])
        a = latent_attention(q, k, v, ctx_kv[0], ctx_kv[1], p['sink'])
        new_kv = None
    y_pool = pool_mixer(u_pool, p['pool_w'], p['pool_scale'])
    filt = hyena_filters(L, p['f_w1'], p['f_b1'], p['f_w2'], p['f_b2'], p['f_w3'], p['f_b3'], p['f_freq'], p['decay'])
    y_hy = hyena_mixer(u_hy, p['short_w'], p['short_b'], filt, p['hy_bias'])
    mix = jnp.concatenate([y_pool, a, y_hy], axis=-1) @ p['w_out']
    x = x + g2 * mix

    h = rmsnorm(x, p['norm'][2]) * (1 + sc3) + sh3
    x = x + 0.5 * g3 * swiglu(h, p['wg'][1], p['wu'][1], p['wd'][1])
    return x, new_kv


def setup_inputs(seed: int = 0) -> dict:
    key = jax.random.key(seed)
    ks = jax.random.split(key, 40)

    def nrm(k, shape, s):
        return jax.random.normal(k, shape, F32) * s

    base_decay = jnp.linspace(abs(math.log(1e-2) / 1.5), abs(math.log(1e-2) / 0.3), HY_WIDTH, dtype=F32)
    return {
        "x_prompt": nrm(ks[0], (BATCH, SEQ, D_MODEL), 1.0),
        "x_sample": nrm(ks[1], (DEC_BATCH, DEC_SEQ, D_MODEL), 1.0),
        "cache_k": nrm(ks[2], (DEC_BATCH, DEPTH, PAST_LEN, N_KV_HEADS, HEAD_DIM), 1.0),
        "cache_v": nrm(ks[3], (DEC_BATCH, DEPTH, PAST_LEN, N_KV_HEADS, HEAD_DIM), 1.0),
        "c": nrm(ks[4], (DEC_BATCH, D_MODEL), 1.0),
        "c_ctx": nrm(ks[5], (D_MODEL,), 1.0),
        "ada_w": nrm(ks[6], (DEPTH, D_MODEL, N_MOD * D_MODEL), 0.5 * D_MODEL ** -0.5),
        "ada_b": nrm(ks[7], (DEPTH, N_MOD * D_MODEL), 0.02),
        "norm_w": 1.0 + nrm(ks[8], (DEPTH, 3, D_MODEL), 0.05),
        "ffn_wg": nrm(ks[9], (DEPTH, 2, D_MODEL, D_FF), D_MODEL ** -0.5),
        "ffn_wu": nrm(ks[10], (DEPTH, 2, D_MODEL, D_FF), D_MODEL ** -0.5),
        "ffn_wd": nrm(ks[11], (DEPTH, 2, D_FF, D_MODEL), D_FF ** -0.5),
        "w_in": nrm(ks[12], (DEPTH, D_MODEL, IN_WIDTH), D_MODEL ** -0.5),
        "w_out": nrm(ks[13], (DEPTH, D_MODEL, D_MODEL), D_MODEL ** -0.5),
        "pool_w": nrm(ks[14], (DEPTH, len(POOL_WINDOWS), POOL_GROUP, POOL_GROUP), POOL_GROUP ** -0.5),
        "pool_scale": 1.0 + nrm(ks[15], (DEPTH, POOL_WIDTH), 0.1),
        "q_norm": 1.0 + nrm(ks[16], (DEPTH, HEAD_DIM), 0.05),
        "k_norm": 1.0 + nrm(ks[17], (DEPTH, HEAD_DIM), 0.05),
        "attn_sink": nrm(ks[18], (DEPTH, N_HEADS), 0.5),
        "hy_short_w": nrm(ks[19], (DEPTH, 3, (HY_ORDER + 1) * HY_WIDTH), 3 ** -0.5),
        "hy_short_b": nrm(ks[20], (DEPTH, (HY_ORDER + 1) * HY_WIDTH), 0.02),
        "hy_f_w1": nrm(ks[21], (DEPTH, HY_EMB_DIM, HY_FILTER_HIDDEN), HY_EMB_DIM ** -0.5),
        "hy_f_b1": nrm(ks[22], (DEPTH, HY_FILTER_HIDDEN), 0.1),
        "hy_f_w2": nrm(ks[23], (DEPTH, HY_FILTER_HIDDEN, HY_FILTER_HIDDEN), HY_FILTER_HIDDEN ** -0.5),
        "hy_f_b2": nrm(ks[24], (DEPTH, HY_FILTER_HIDDEN), 0.1),
        "hy_f_w3": nrm(ks[25], (DEPTH, HY_FILTER_HIDDEN, 2 * HY_ORDER * HY_WIDTH), HY_FILTER_HIDDEN ** -0.5),
        "hy_f_b3": nrm(ks[26], (DEPTH, 2 * HY_ORDER * HY_WIDTH), 0.02),
        "hy_sin_freq": 1.0 + nrm(ks[27], (DEPTH, 2, HY_FILTER_HIDDEN), 0.1),
        "hy_decay": base_decay[None, None, :] + nrm(ks[28], (DEPTH, HY_ORDER, HY_WIDTH), 0.1),
        "hy_bias": nrm(ks[29], (DEPTH, HY_ORDER, HY_WIDTH), 1.0),
    }


def reference(x_prompt, x_sample, cache_k, cache_v, c, c_ctx, ada_w, ada_b, norm_w,
              ffn_wg, ffn_wu, ffn_wd, w_in, w_out, pool_w, pool_scale, q_norm, k_norm,
              attn_sink, hy_short_w, hy_short_b, hy_f_w1, hy_f_b1, hy_f_w2, hy_f_b2,
              hy_f_w3, hy_f_b3, hy_sin_freq, hy_decay, hy_bias):
    L_lat = x_sample.shape[1]
    n_rows = L_lat // GRID_W
    t = jnp.arange(n_rows * GRID_W)
    pos = (t // GRID_W, t % GRID_W)
    cond_ctx = c_ctx[None, :]
    yp, ys = x_prompt, x_sample
    ks, vs = [], []
    for l in range(DEPTH):
        p = {
            'ada_w': ada_w[l], 'ada_b': ada_b[l], 'norm': norm_w[l],
            'wg': ffn_wg[l], 'wu': ffn_wu[l], 'wd': ffn_wd[l],
            'w_in': w_in[l], 'w_out': w_out[l],
            'pool_w': pool_w[l], 'pool_scale': pool_scale[l],
            'q_norm': q_norm[l], 'k_norm': k_norm[l], 'sink': attn_sink[l],
            'short_w': hy_short_w[l], 'short_b': hy_short_b[l],
            'f_w1': hy_f_w1[l], 'f_b1': hy_f_b1[l], 'f_w2': hy_f_w2[l], 'f_b2': hy_f_b2[l],
            'f_w3': hy_f_w3[l], 'f_b3': hy_f_b3[l], 'f_freq': hy_sin_freq[l],
            'decay': hy_decay[l], 'hy_bias': hy_bias[l],
        }
        yp, kv = trunk_layer(yp, cond_ctx, p, None, None)
        ks.append(kv[0])
        vs.append(kv[1])
        ys, _ = trunk_layer(ys, c, p, pos, (cache_k[:, l], cache_v[:, l]))
    new_cache_k = jnp.stack(ks, axis=1)
    new_cache_v = jnp.stack(vs, axis=1)
    return (yp, ys, new_cache_k, new_cache_v)
```

```python
import math
from contextlib import ExitStack
import numpy as np
import concourse.bass as bass
import concourse.mybir as mybir
from concourse.bass_utils import run_bass_kernel_spmd

F32 = mybir.dt.float32
BF16 = mybir.dt.bfloat16
ALU = mybir.AluOpType
AF = mybir.ActivationFunctionType
AX = mybir.AxisListType

D = 1024
DFF = 2816
NFF = 22
TS = 4096
TPR = 512
NPS = TPR // 256
T = TS + TPR
NBLK = T // 128
EPS = 1e-6
TWO_PI = 2.0 * math.pi


class Buf:
    def __init__(self, name=""):
        self.name = name
        self.w = None
        self.r = []
        self.dsem = None
        self.dsem_sw = None


class Eng:
    def __init__(self, kb, name, h, is_pe=False):
        self.kb, self.name, self.h, self.is_pe = kb, name, h, is_pe
        self.sem = kb.nc.alloc_semaphore("e_" + name)
        self.semid = id(self)
        self.n = 0
        self.waited = {}


class KB:
    def __init__(self, nc):
        self.nc = nc
        self.pe = Eng(self, "pe", nc.tensor, True)
        self.act = Eng(self, "act", nc.scalar)
        self.dve = Eng(self, "dve", nc.vector)
        self.pool = Eng(self, "pool", nc.gpsimd)
        self.sp = Eng(self, "sp", nc.sync)
        self.dsems = []
        self.free_dsems = {False: [], True: []}
        self.outstanding = []

    def _deps(self, reads, writes):
        toks = []
        for b in reads:
            if b.w is not None:
                toks.append(b.w)
        for b in writes:
            if b.w is not None:
                toks.append(b.w)
            toks.extend(b.r)
        return toks

    def _wait(self, eng, toks):
        mx = {}
        for (sem, key, val) in toks:
            if key == eng.semid and eng.is_pe:
                continue
            if key not in mx or mx[key][1] < val:
                mx[key] = (sem, val)
        for key, (sem, val) in mx.items():
            if eng.waited.get(key, 0) < val:
                eng.h.wait_ge(sem, val)
                eng.waited[key] = val

    def _update(self, tok, reads, writes):
        for b in reads:
            b.r.append(tok)
        for b in writes:
            b.w = tok
            b.r = []

    def op(self, eng, fn, reads=(), writes=(), sig=True):
        self._wait(eng, self._deps(reads, writes))
        ins = fn(eng.h)
        if sig or not eng.is_pe:
            eng.n += 1
            ins.then_inc(eng.sem, 1)
            tok = (eng.sem, eng.semid, eng.n)
        else:
            tok = (eng.sem, eng.semid, eng.n + 1)
        self._update(tok, reads, writes)
        return tok

    def get_dsem(self, b, sw):
        attr = "dsem_sw" if sw else "dsem"
        if getattr(b, attr, None) is None:
            if self.free_dsems[sw]:
                setattr(b, attr, self.free_dsems[sw].pop())
            else:
                sm = self.nc.alloc_semaphore("d%d" % len(self.dsems))
                ds = [sm, 0, len(self.dsems) + 1000]
                self.dsems.append(ds)
                setattr(b, attr, ds)
        return getattr(b, attr)

    def lasttok(self, b, sw):
        ds = b.dsem_sw if sw else b.dsem
        return (ds[0], ds[2], ds[1])

    def dma(self, q, out, in_, reads=(), writes=(), sembuf=None):
        self._wait(q, self._deps(reads, writes))
        ds = self.get_dsem(sembuf, q is self.pool)
        ds[1] += 16
        q.h.dma_start(out=out, in_=in_).then_inc(ds[0], 16)
        tok = (ds[0], ds[2], ds[1])
        self._update(tok, reads, writes)
        self.outstanding.append(tok)
        return tok

    def dma2(self, q, out, in_, n, reads=(), writes=(), sembuf=None, maxel=2048):
        toks = None
        for a in range(0, n, maxel):
            b = min(n, a + maxel)
            toks = self.dma(q, out[:, a:b], in_[:, a:b], reads=reads, writes=(writes if a == 0 else ()), sembuf=sembuf)
        if writes:
            for bb in writes:
                bb.w = toks
        return toks

    def release(self, bufs):
        for b in bufs:
            if b.dsem is not None:
                self.free_dsems[False].append(b.dsem)
                b.dsem = None
            if b.dsem_sw is not None:
                self.free_dsems[True].append(b.dsem_sw)
                b.dsem_sw = None

    def barrier(self):
        toks = list(self.outstanding)
        for e in (self.pe, self.act, self.dve, self.pool, self.sp):
            if e.n > 0:
                toks.append((e.sem, e.semid, e.n))
        for e in (self.pe, self.act, self.dve, self.pool, self.sp):
            mx = {}
            for (sem, key, val) in toks:
                if key == e.semid:
                    continue
                if key not in mx or mx[key][2] < val:
                    mx[key] = (sem, key, val)
            for tk in mx.values():
                if e.waited.get(tk[1], 0) < tk[2]:
                    e.h.wait_ge(tk[0], tk[2])
                    e.waited[tk[1]] = tk[2]
        self.outstanding = []


def build_program(stages=None, debug=False):
    nc = bass.Bass("TRN2", target_bir_lowering=False)
    kb = KB(nc)
    PE, ACT, DVE, POOL, SP = kb.pe, kb.act, kb.dve, kb.pool, kb.sp

    def din(name, shape, dt=F32):
        return nc.dram_tensor(name, list(shape), dt, kind="ExternalInput")

    def dout(name, shape, dt=F32):
        return nc.dram_tensor(name, list(shape), dt, kind="ExternalOutput")

    def dscr(name, shape, dt):
        return nc.dram_tensor(name, list(shape), dt, kind="ExternalOutput" if debug else "Internal")

    xT_in = din("xT", [128, 8, T])
    condT = din("condT", [128, 8, 2])
    adaw = din("adaw", [2, 72, 128, 8 * 128])
    adab = din("adab", [2, 128, 72])
    normw = din("normw", [2, 128, 24])
    wg = din("wg", [2, 2, NFF, 128, 8 * 128])
    wu = din("wu", [2, 2, NFF, 128, 8 * 128])
    wd = din("wd", [2, 2, 128, NFF * 1024])
    win = din("win", [2, 128, 8 * 1664])
    winv = din("winv", [2, 128, 8 * 128])
    wout = din("wout", [2, 128, 8 * 1024])
    poolw = din("poolw", [2, 2, 128, 128])
    poolsc = din("poolsc", [2, 128, 2])
    qkn = din("qkn", [2, 128, 2])
    sinkb = din("sinkb", [2, 2, 64, 4])
    ckT = din("ckT", [2, 128, 256])
    cv = din("cv", [2, 256, 128])
    shw = din("shw", [2, 128, 6, 4])
    fw1 = din("fw1", [2, 33, 64])
    fw2 = din("fw2", [2, 64, 64])
    fw3 = din("fw3", [2, 64, 1024])
    fsm = din("fsm", [2, 64, 8])
    fb3 = din("fb3", [2, 128, 8])
    fdec = din("fdec", [2, 128, 4])
    hyb = din("hyb", [2, 128, 4])
    c_zT_s = din("c_zT_s", [33, TS])
    c_zT_p = din("c_zT_p", [33, 256])
    c_tl_s = din("c_tl_s", [128, TS])
    c_tl_p = din("c_tl_p", [128, 256])
    c_cos = din("c_cos", [128, TS])
    c_sin = din("c_sin", [128, TS])
    c_pm = din("c_pm", [128, 128])
    c_bd = din("c_bd", [128, 128])
    c_id = din("c_id", [128, 128])
    c_mask = din("c_mask", [2, 128, 128])
    c_invc_s = din("c_invc_s", [2, 128, TS])
    c_invc_p = din("c_invc_p", [2, 128, 256])

    yT = dout("yT", [128, 8, T])
    okT = dout("okT", [2, 128, TPR])
    ov = dout("ov", [2, TPR, 128])

    qT_s = dscr("qT_s", [4, 128, T], BF16)
    kT_s = dscr("kT_s", [128, T], BF16)
    v_s = dscr("v_s", [T, 128], BF16)
    up_s = dscr("up_s", [2, 128, T], F32)
    uh_s = dscr("uh_s", [6, 128, T], F32)
    mix_s = dscr("mix_s", [8, 128, T], BF16)
    G_s = dscr("G_s", [2, 256, 2 * TS - 1], BF16)
    G_p = dscr("G_p", [2, 256, 2 * 256 - 1], BF16)

    psb = [nc.alloc_psum_tensor("ps%d" % i, [128, 512], F32) for i in range(8)]
    PSB = [Buf("ps%d" % i) for i in range(8)]

    def A(t):
        return t.ap() if hasattr(t, "ap") and callable(getattr(t, "ap")) else t

    def sbp(name, shape, dt):
        return nc.alloc_sbuf_tensor(name, list(shape), dt)

    ones_bf = sbp("ones_bf", [128, 128], BF16)
    ones64 = sbp("ones64", [128, 64], BF16)
    bd_bf = sbp("bd_bf", [128, 128], BF16)
    pm_bf = sbp("pm_bf", [128, 128], BF16)
    id_f = sbp("id_f", [128, 128], F32)
    mask_f = sbp("mask_f", [128, 2, 128], F32)
    scT = sbp("scT", [128, 8, 2], F32)
    mod = sbp("mod", [128, 72, 2], F32)
    nrm = sbp("nrm", [128, 24], F32)
    Aab = sbp("Aab", [128, 3, 8, 2], F32)
    Gg = sbp("Gg", [128, 3, 8, 2], F32)
    adab_sb = sbp("adab_sb", [128, 72], F32)
    B_const = Buf("const")
    B_mod = Buf("mod")

    kb.op(DVE, lambda e: e.memset(ones_bf[:, :], 1.0 / 1024.0), writes=[B_const])
    kb.op(DVE, lambda e: e.memset(ones64[:, :], 1.0), writes=[B_const])
    kb.dma(POOL, bd_bf[:, :], A(c_bd)[:, :], writes=[B_const], sembuf=B_const)
    kb.dma(POOL, pm_bf[:, :], A(c_pm)[:, :], writes=[B_const], sembuf=B_const)
    kb.dma(SP, id_f[:, :], A(c_id)[:, :], writes=[B_const], sembuf=B_const)
    kb.dma(SP, mask_f[:, 0, :], A(c_mask)[0], writes=[B_const], sembuf=B_const)
    kb.dma(SP, mask_f[:, 1, :], A(c_mask)[1], writes=[B_const], sembuf=B_const)
    kb.dma(SP, scT[:, :, :], A(condT)[:, :, :], writes=[B_const], sembuf=B_const)
    kb.op(ACT, lambda e: e.activation(out=scT[:, :, :], in_=scT[:, :, :], func=AF.Silu),
          reads=[], writes=[B_const])

    _ctr = [0]

    def stage_sb(es, name, shape, dt):
        _ctr[0] += 1
        return es.enter_context(nc.sbuf_tensor("%s_%d" % (name, _ctr[0]), list(shape), dt))

    def mod_stage(l):
        with ExitStack() as es:
            ring = [stage_sb(es, "adar%d" % i, [128, 8, 128], F32) for i in range(3)]
            RB = [Buf("adar%d" % i) for i in range(3)]
            kb.dma(SP, adab_sb[:, :], A(adab)[l], writes=[B_mod], sembuf=B_mod)
            kb.dma(SP, nrm[:, :], A(normw)[l], writes=[B_mod], sembuf=B_mod)
            for fc in range(72):
                s = fc % 3
                kb.dma(SP, ring[s][:, :, :], A(adaw)[l, fc].rearrange("p (k j) -> p k j", j=128),
                       writes=[RB[s]], sembuf=RB[s])
                bank = 6 + (fc % 2)
                for kc in range(8):
                    kb.op(PE, lambda e, kc=kc, s=s, bank=bank: e.matmul(
                        psb[bank][:, 0:2], lhsT=ring[s][:, kc, :], rhs=scT[:, kc, :],
                        start=(kc == 0), stop=(kc == 7)),
                        reads=[RB[s], B_const], writes=[PSB[bank]], sig=(kc == 7))
                kb.op(DVE, lambda e, fc=fc, bank=bank: e.tensor_scalar(
                    out=mod[:, fc, :], in0=psb[bank][:, 0:2], scalar1=adab_sb[:, fc:fc + 1],
                    scalar2=None, op0=ALU.add), reads=[PSB[bank], B_mod], writes=[B_mod])
            for i in range(3):
                for ci in range(2):
                    kb.op(DVE, lambda e, i=i, ci=ci: e.scalar_tensor_tensor(
                        out=Aab[:, i, :, ci], in0=mod[:, (3 * i + 1) * 8:(3 * i + 2) * 8, ci], scalar=1.0,
                        in1=nrm[:, i * 8:(i + 1) * 8], op0=ALU.add, op1=ALU.mult),
                        reads=[B_mod], writes=[B_mod])
                    gs = 0.5 if i != 1 else 1.0
                    kb.op(DVE, lambda e, i=i, ci=ci, gs=gs: e.tensor_scalar(
                        out=Gg[:, i, :, ci], in0=mod[:, (3 * i + 2) * 8:(3 * i + 3) * 8, ci], scalar1=gs,
                        scalar2=None, op0=ALU.mult), reads=[B_mod], writes=[B_mod])
            kb.barrier()
            kb.release(RB)

    def norm_mod(xap, hap, i, ci, sq, rstd, tmpn, Bx, Bh, Bsq, Brs, Btmp, bank):
        for kc in range(8):
            kb.op(ACT, lambda e, kc=kc: e.activation(out=sq[:, kc, :], in_=xap(kc), func=AF.Square),
                  reads=[Bx[kc]], writes=[Bsq])
        for kc in range(8):
            kb.op(PE, lambda e, kc=kc: e.matmul(psb[bank][:, :], lhsT=ones_bf[:, :], rhs=sq[:, kc, :],
                                               start=(kc == 0), stop=(kc == 7)),
                  reads=[Bsq, B_const], writes=[PSB[bank]], sig=(kc == 7))
        kb.op(ACT, lambda e: e.activation(out=rstd[:, :], in_=psb[bank][:, :], func=AF.Sqrt, bias=EPS_AP[:, 0:1]),
              reads=[PSB[bank]], writes=[Brs])
        kb.op(DVE, lambda e: e.reciprocal(out=rstd[:, :], in_=rstd[:, :]), reads=[Brs], writes=[Brs])
        for kc in range(8):
            tb = kc % 2
            kb.op(DVE, lambda e, kc=kc, tb=tb: e.scalar_tensor_tensor(
                out=tmpn[tb][:, :], in0=xap(kc), scalar=Aab[:, i, kc, ci:ci + 1], in1=rstd[:, :],
                op0=ALU.mult, op1=ALU.mult), reads=[Bx[kc], Brs, B_mod], writes=[Btmp[tb]])
            kb.op(ACT, lambda e, kc=kc, tb=tb: e.activation(
                out=hap(kc), in_=tmpn[tb][:, :], func=AF.Identity,
                bias=mod[:, (3 * i) * 8 + kc, ci:ci + 1]), reads=[Btmp[tb], B_mod], writes=[Bh])

    eps_t = sbp("eps_t", [128, 1], F32)
    EPS_AP = eps_t
    kb.op(DVE, lambda e: e.memset(eps_t[:, :], EPS), writes=[B_const])

    def ffn_stage(l, f, src):
        i = 0 if f == 0 else 2
        NT = (T + 1023) // 1024

        def nhalf(tile):
            return min(2, (T - tile * 1024) // 512)
        with ExitStack() as es:
            xts = [stage_sb(es, "f_xt%d" % k, [128, 8, 1024], F32) for k in range(2)]
            hT = stage_sb(es, "f_hT", [128, 8, 1024], BF16)
            aT = stage_sb(es, "f_aT", [128, NFF, 1024], BF16)
            wds = stage_sb(es, "f_wd", [128, NFF, 1024], BF16)
            ring = [stage_sb(es, "f_r%d" % k, [128, 2, 8, 128], BF16) for k in range(3)]
            sq = stage_sb(es, "f_sq", [128, 8, 512], BF16)
            rstd = stage_sb(es, "f_rstd", [128, 512], F32)
            tmpn = [stage_sb(es, "f_tmp%d" % k, [128, 512], F32) for k in range(2)]
            sg = [stage_sb(es, "f_sg%d" % k, [128, 512], F32) for k in range(2)]
            Bxs = [[Buf("x%d_%d" % (b_, k)) for k in range(8)] for b_ in range(2)]
            BhT = [Buf("hT0"), Buf("hT1")]
            BaT = [[Buf() for _ in range(2)] for _ in range(NFF)]
            Bwd = [Buf("wd%d" % k) for k in range(NFF)]
            RB = [Buf("r%d" % k) for k in range(3)]
            Bsq, Brs = Buf("sq"), Buf("rs")
            Btmp = [Buf(), Buf()]
            Bsg = [Buf(), Buf()]

            def load_x(tile):
                xt, Bx = xts[tile % 2], Bxs[tile % 2]
                t0 = tile * 1024
                w_ = nhalf(tile) * 512
                for kc in range(8):
                    kb.dma(SP, xt[:, kc, 0:w_], A(src)[:, kc, t0:t0 + w_], writes=[Bx[kc]], sembuf=Bx[kc])

            def norm_tile(tile):
                xt, Bx = xts[tile % 2], Bxs[tile % 2]
                ci = 0 if tile < TS // 1024 else 1
                for h in range(nhalf(tile)):
                    norm_mod(lambda kc, h=h: xt[:, kc, h * 512:(h + 1) * 512],
                             lambda kc, h=h: hT[:, kc, h * 512:(h + 1) * 512],
                             i, ci, sq, rstd, tmpn, Bx, BhT[h], Bsq, Brs, Btmp, 7)

            load_x(0)
            norm_tile(0)
            rcount = 0
            for tile in range(NT):
                xt, Bx = xts[tile % 2], Bxs[tile % 2]
                ci = 0 if tile < TS // 1024 else 1
                t0 = tile * 1024
                if tile + 1 < NT:
                    load_x(tile + 1)
                for c in range(NFF):
                    s = rcount % 3
                    rcount += 1
                    kb.dma(POOL, ring[s][:, 0, :, :], A(wg)[l, f, c].rearrange("p (k j) -> p k j", j=128),
                           writes=[RB[s]], sembuf=RB[s])
                    kb.dma(POOL, ring[s][:, 1, :, :], A(wu)[l, f, c].rearrange("p (k j) -> p k j", j=128),
                           reads=[], writes=[], sembuf=RB[s])
                    RB[s].w = kb.lasttok(RB[s], True)
                    if tile == 0:
                        kb.dma(POOL, wds[:, c, :], A(wd)[l, f][:, c * 1024:(c + 1) * 1024], writes=[Bwd[c]], sembuf=Bwd[c])
                    for h in range(nhalf(tile)):
                        pb = 2 * ((2 * c + h) % 2)
                        for gu in range(2):
                            for kc in range(8):
                                kb.op(PE, lambda e, gu=gu, kc=kc, s=s, h=h, pb=pb: e.matmul(
                                    psb[pb + gu][:, :], lhsT=ring[s][:, gu, kc, :],
                                    rhs=hT[:, kc, h * 512:(h + 1) * 512], start=(kc == 0), stop=(kc == 7)),
                                    reads=[RB[s], BhT[h]], writes=[PSB[pb + gu]], sig=(kc == 7))
                        k2 = (2 * c + h) % 2
                        kb.op(ACT, lambda e, pb=pb, k2=k2: e.activation(out=sg[k2][:, :], in_=psb[pb][:, :], func=AF.Silu),
                              reads=[PSB[pb]], writes=[Bsg[k2]])
                        kb.op(DVE, lambda e, pb=pb, k2=k2, c=c, h=h: e.tensor_tensor(
                            out=aT[:, c, h * 512:(h + 1) * 512], in0=sg[k2][:, :], in1=psb[pb + 1][:, :], op=ALU.mult),
                            reads=[Bsg[k2], PSB[pb + 1]], writes=[BaT[c][h]])
                if tile + 1 < NT:
                    norm_tile(tile + 1)
                dcount = 0
                for oc in range(8):
                    for h in range(nhalf(tile)):
                        pb = 4 + (dcount % 3)
                        dcount += 1
                        for c in range(NFF):
                            kb.op(PE, lambda e, c=c, oc=oc, h=h, pb=pb: e.matmul(
                                psb[pb][:, :], lhsT=wds[:, c, oc * 128:(oc + 1) * 128],
                                rhs=aT[:, c, h * 512:(h + 1) * 512], start=(c == 0), stop=(c == NFF - 1)),
                                reads=[Bwd[c], BaT[c][h]], writes=[PSB[pb]], sig=(c == NFF - 1))
                        kb.op(DVE, lambda e, oc=oc, h=h, pb=pb, ci=ci, xt=xt: e.scalar_tensor_tensor(
                            out=xt[:, oc, h * 512:(h + 1) * 512], in0=psb[pb][:, :], scalar=Gg[:, i, oc, ci:ci + 1],
                            in1=xt[:, oc, h * 512:(h + 1) * 512], op0=ALU.mult, op1=ALU.add),
                            reads=[PSB[pb], B_mod], writes=[Bx[oc]])
                w_ = nhalf(tile) * 512
                for kc in range(8):
                    kb.dma(SP, A(yT)[:, kc, t0:t0 + w_], xt[:, kc, 0:w_], reads=[Bx[kc]], sembuf=Bx[kc])
            kb.barrier()
            kb.release(Bxs[0] + Bxs[1] + Bwd + RB)

    def mixa_stage(l):
        with ExitStack() as es:
            xts_ = [stage_sb(es, "a_xt%d" % k, [128, 8, 512], F32) for k in range(2)]
            hTs_ = [stage_sb(es, "a_hT%d" % k, [128, 8, 512], BF16) for k in range(2)]
            w_sb = stage_sb(es, "a_w", [128, 8, 1664], BF16)
            wv_sb = stage_sb(es, "a_wv", [128, 8, 128], BF16)
            sq = stage_sb(es, "a_sq", [128, 8, 512], BF16)
            rstd = stage_sb(es, "a_rstd", [128, 512], F32)
            tmpn = [stage_sb(es, "a_tmp%d" % k, [128, 512], F32) for k in range(2)]
            cos_sb = stage_sb(es, "a_cos", [128, TS], F32)
            sin_sb = stage_sb(es, "a_sin", [128, TS], F32)
            qkn_sb = stage_sb(es, "a_qkn", [128, 2], F32)
            sqh_ = [stage_sb(es, "a_sqh%d" % k, [128, 512], BF16) for k in range(2)]
            rs_ = [stage_sb(es, "a_rs%d" % k, [128, 512], F32) for k in range(2)]
            qn_ = [stage_sb(es, "a_qn%d" % k, [128, 512], F32) for k in range(2)]
            qnb_ = [stage_sb(es, "a_qnb%d" % k, [128, 512], BF16) for k in range(2)]
            t1_ = [stage_sb(es, "a_t1%d" % k, [128, 512], F32) for k in range(2)]
            t2_ = [stage_sb(es, "a_t2%d" % k, [128, 512], F32) for k in range(2)]
            ob = [stage_sb(es, "a_ob%d" % k, [128, 512], BF16) for k in range(2)]
            of = [stage_sb(es, "a_of%d" % k, [128, 512], F32) for k in range(2)]
            vb = [stage_sb(es, "a_vb%d" % k, [128, 128], BF16) for k in range(2)]
            vf = [stage_sb(es, "a_vf%d" % k, [128, 128], F32) for k in range(2)]
            Bxs_ = [[Buf() for _ in range(8)] for _ in range(2)]
            Bhs_ = [Buf(), Buf()]
            Bsq, Brs, Bw, Bc = Buf(), Buf(), Buf(), Buf()
            Btmp = [Buf(), Buf()]
            Bsqh_, Brs2_, Bqn_, Bqnb_, Bt1_, Bt2_ = [[Buf(), Buf()] for _ in range(6)]
            chain = [0]
            Bob = [Buf(), Buf()]
            Bof = [Buf(), Buf()]
            Bvb = [Buf(), Buf()]
            Bvf = [Buf(), Buf()]
            for kc in range(8):
                kb.dma(POOL, w_sb[:, kc, :], A(win)[l][:, kc * 1664:(kc + 1) * 1664], writes=([Bw] if kc == 0 else []), sembuf=Bw)
            kb.dma(POOL, wv_sb[:, :, :], A(winv)[l].rearrange("p (k j) -> p k j", j=128), writes=[], sembuf=Bw)
            Bw.w = kb.lasttok(Bw, True)
            kb.dma2(SP, cos_sb, A(c_cos), TS, writes=[Bc], sembuf=Bc)
            kb.dma2(SP, sin_sb, A(c_sin), TS, writes=[Bc], sembuf=Bc)
            kb.dma(SP, qkn_sb[:, :], A(qkn)[l], writes=[Bc], sembuf=Bc)
            oc_ = 0
            vc_ = 0
            NBK = T // 512

            def a_load(blk):
                xt_, Bx_ = xts_[blk % 2], Bxs_[blk % 2]
                for kc in range(8):
                    kb.dma(SP, xt_[:, kc, :], A(yT)[:, kc, blk * 512:(blk + 1) * 512], writes=[Bx_[kc]], sembuf=Bx_[kc])

            def a_norm(blk):
                xt_, Bx_, hT_, Bh_ = xts_[blk % 2], Bxs_[blk % 2], hTs_[blk % 2], Bhs_[blk % 2]
                norm_mod(lambda kc: xt_[:, kc, :], lambda kc: hT_[:, kc, :], 1, (0 if blk * 512 < TS else 1), sq, rstd, tmpn,
                         Bx_, Bh_, Bsq, Brs, Btmp, 7)

            a_load(0)
            a_load(1)
            a_norm(0)
            for blk in range(NBK):
                t0 = blk * 512
                isS = t0 < TS
                ci = 0 if isS else 1
                hT, Bh = hTs_[blk % 2], Bhs_[blk % 2]
                for ch in range(13):
                    pb = ch % 4
                    for kc in range(8):
                        kb.op(PE, lambda e, ch=ch, kc=kc, pb=pb: e.matmul(
                            psb[pb][:, :], lhsT=w_sb[:, kc, ch * 128:(ch + 1) * 128], rhs=hT[:, kc, :],
                            start=(kc == 0), stop=(kc == 7)), reads=[Bw, Bh], writes=[PSB[pb]], sig=(kc == 7))
                    if ch < 2 or ch >= 7:
                        k2 = oc_ % 2
                        oc_ += 1
                        kb.op(ACT, lambda e, pb=pb, k2=k2: e.activation(out=of[k2][:, :], in_=psb[pb][:, :], func=AF.Identity),
                              reads=[PSB[pb]], writes=[Bof[k2]])
                        dst = A(up_s)[ch, :, t0:t0 + 512] if ch < 2 else A(uh_s)[ch - 7, :, t0:t0 + 512]
                        kb.dma(SP, dst, of[k2][:, :], reads=[Bof[k2]], sembuf=Bof[k2])
                        continue
                    cp = chain[0] % 2
                    chain[0] += 1
                    sqh, rs, qn, qnb, t1, t2 = sqh_[cp], rs_[cp], qn_[cp], qnb_[cp], t1_[cp], t2_[cp]
                    Bsqh, Brs2, Bqn, Bqnb, Bt1, Bt2 = Bsqh_[cp], Brs2_[cp], Bqn_[cp], Bqnb_[cp], Bt1_[cp], Bt2_[cp]
                    bk = 4 + cp
                    isq = ch < 6
                    gcol = 0 if isq else 1
                    kb.op(ACT, lambda e, pb=pb, sqh=sqh: e.activation(out=sqh[:, :], in_=psb[pb][:, :], func=AF.Square),
                          reads=[PSB[pb]], writes=[Bsqh])
                    kb.op(PE, lambda e, bk=bk, sqh=sqh: e.matmul(psb[bk][:, :], lhsT=bd_bf[:, :], rhs=sqh[:, :], start=True, stop=True),
                          reads=[Bsqh, B_const], writes=[PSB[bk]])
                    kb.op(ACT, lambda e, bk=bk, rs=rs: e.activation(out=rs[:, :], in_=psb[bk][:, :], func=AF.Sqrt, bias=EPS_AP[:, 0:1]),
                          reads=[PSB[bk]], writes=[Brs2])
                    kb.op(DVE, lambda e, rs=rs: e.reciprocal(out=rs[:, :], in_=rs[:, :]), reads=[Brs2], writes=[Brs2])
                    kb.op(DVE, lambda e, pb=pb, gcol=gcol, qn=qn, rs=rs: e.scalar_tensor_tensor(
                        out=qn[:, :], in0=psb[pb][:, :], scalar=qkn_sb[:, gcol:gcol + 1], in1=rs[:, :],
                        op0=ALU.mult, op1=ALU.mult), reads=[PSB[pb], Brs2, Bc], writes=[Bqn])
                    k2 = oc_ % 2
                    oc_ += 1
                    if isS:
                        kb.op(ACT, lambda e, qnb=qnb, qn=qn: e.activation(out=qnb[:, :], in_=qn[:, :], func=AF.Identity),
                              reads=[Bqn], writes=[Bqnb])
                        kb.op(PE, lambda e, bk=bk, qnb=qnb: e.matmul(psb[bk][:, :], lhsT=pm_bf[:, :], rhs=qnb[:, :], start=True, stop=True),
                              reads=[Bqnb, B_const], writes=[PSB[bk]])
                        kb.op(DVE, lambda e, t0=t0, t1=t1, qn=qn: e.tensor_tensor(out=t1[:, :], in0=qn[:, :], in1=cos_sb[:, t0:t0 + 512], op=ALU.mult),
                              reads=[Bqn, Bc], writes=[Bt1])
                        kb.op(DVE, lambda e, t0=t0, t2=t2, bk=bk: e.tensor_tensor(out=t2[:, :], in0=psb[bk][:, :], in1=sin_sb[:, t0:t0 + 512], op=ALU.mult),
                              reads=[PSB[bk], Bc], writes=[Bt2])
                        kb.op(DVE, lambda e, k2=k2, t1=t1, t2=t2: e.tensor_tensor(out=ob[k2][:, :], in0=t1[:, :], in1=t2[:, :], op=ALU.add),
                              reads=[Bt1, Bt2], writes=[Bob[k2]])
                    else:
                        kb.op(ACT, lambda e, k2=k2, qn=qn: e.activation(out=ob[k2][:, :], in_=qn[:, :], func=AF.Identity),
                              reads=[Bqn], writes=[Bob[k2]])
                        if not isq:
                            k3 = oc_ % 2
                            oc_ += 1
                            kb.op(ACT, lambda e, k3=k3, qn=qn: e.activation(out=of[k3][:, :], in_=qn[:, :], func=AF.Identity),
                                  reads=[Bqn], writes=[Bof[k3]])
                            kb.dma(SP, A(okT)[l, :, t0 - TS:t0 - TS + 512], of[k3][:, :], reads=[Bof[k3]], sembuf=Bof[k3])
                    dst = A(qT_s)[ch - 2, :, t0:t0 + 512] if isq else A(kT_s)[:, t0:t0 + 512]
                    kb.dma(SP, dst, ob[k2][:, :], reads=[Bob[k2]], sembuf=Bob[k2])
                for tb in range(4):
                    for kc in range(8):
                        kb.op(PE, lambda e, tb=tb, kc=kc: e.matmul(
                            psb[6][:, 0:128], lhsT=hT[:, kc, tb * 128:(tb + 1) * 128], rhs=wv_sb[:, kc, :],
                            start=(kc == 0), stop=(kc == 7)), reads=[Bw, Bh], writes=[PSB[6]], sig=(kc == 7))
                    k2 = vc_ % 2
                    vc_ += 1
                    if not isS:
                        kb.op(ACT, lambda e, k2=k2: e.activation(out=vf[k2][:, :], in_=psb[6][:, 0:128], func=AF.Identity),
                              reads=[PSB[6]], writes=[Bvf[k2]])
                        r0 = t0 - TS + tb * 128
                        kb.dma(SP, A(ov)[l, r0:r0 + 128, :], vf[k2][:, :], reads=[Bvf[k2]], sembuf=Bvf[k2])
                    if not isS:
                        kb.op(DVE, lambda e, k2=k2: e.tensor_copy(out=vb[k2][:, :], in_=vf[k2][:, :]),
                              reads=[Bvf[k2]], writes=[Bvb[k2]])
                    else:
                        kb.op(DVE, lambda e, k2=k2: e.tensor_copy(out=vb[k2][:, :], in_=psb[6][:, 0:128]),
                              reads=[PSB[6]], writes=[Bvb[k2]])
                    r0 = t0 + tb * 128
                    kb.dma(SP, A(v_s)[r0:r0 + 128, :], vb[k2][:, :], reads=[Bvb[k2]], sembuf=Bvb[k2])
                if blk + 1 < NBK:
                    a_norm(blk + 1)
                if blk + 2 < NBK:
                    a_load(blk + 2)
            kb.barrier()
            kb.release(Bxs_[0] + Bxs_[1] + [Bw, Bc] + Bob + Bof + Bvb + Bvf)

    def attn_gen(l, es, Q, relbufs):
        if True:
            q_sb = [stage_sb(es, "b_q%d" % k, [128, 4, 128], BF16) for k in range(2)]
            k_sb = [stage_sb(es, "b_k%d" % k, [128, 3, 128], BF16) for k in range(2)]
            v_sb = [stage_sb(es, "b_v%d" % k, [128, 3, 128], BF16) for k in range(2)]
            ck_sb = stage_sb(es, "b_ck", [128, 256], BF16)
            cv_sb = stage_sb(es, "b_cv", [128, 2, 128], BF16)
            sk = stage_sb(es, "b_sk", [64, 2, 4], F32)
            pT = [stage_sb(es, "b_pT%d" % k, [128, 512], BF16) for k in range(3)]
            d2 = stage_sb(es, "b_d2", [64, 512], F32)
            o_sb = [stage_sb(es, "b_o%d" % k, [64, 512], BF16) for k in range(2)]
            Bq = [Buf(), Buf()]
            Bk = [Buf(), Buf()]
            Bv = [Buf(), Buf()]
            Bck, Bsk, Bd2 = Buf(), Buf(), Buf()
            BpT = [Buf(), Buf(), Buf()]
            Bo = [Buf(), Buf()]
            relbufs.extend(Bq + Bk + Bv + [Bck, Bsk] + Bo)
            kb.dma(POOL, ck_sb[:, :], A(ckT)[l], writes=[Bck], sembuf=Bck)
            kb.dma(POOL, cv_sb[:, :, :], A(cv)[l].rearrange("(b p) d -> p b d", p=128), writes=[Bck], sembuf=Bck)
            for kh in range(2):
                kb.dma(Q, sk[:, kh, :], A(sinkb)[l, kh], writes=[Bsk], sembuf=Bsk)
            kb.op(ACT, lambda e: e.activation(out=sk[:, :, :], in_=sk[:, :, :], func=AF.Exp), reads=[], writes=[Bsk])
            pcount = 0
            ocount = 0
            for qb in range(NBLK):
                isS = qb < TS // 128
                s = qb % 2
                t0 = qb * 128
                if isS:
                    kbl = [b for b in (qb - 1, qb, qb + 1) if 0 <= b < TS // 128]
                else:
                    sq0 = (qb - TS // 128) // 2 * 2 + TS // 128
                    kbl = [sq0, sq0 + 1]
                kb.dma(Q, q_sb[s][:, :, :], A(qT_s)[:, :, t0:t0 + 128].rearrange("c p t -> p c t"),
                       writes=[Bq[s]], sembuf=Bq[s])
                k0 = kbl[0] * 128
                nk = len(kbl)
                kb.dma(Q, k_sb[s][:, 0:nk, :], A(kT_s)[:, k0:k0 + nk * 128].rearrange("p (b t) -> p b t", t=128),
                       writes=[Bk[s]], sembuf=Bk[s])
                kb.dma(Q, v_sb[s][:, 0:nk, :], A(v_s)[k0:k0 + nk * 128, :].rearrange("(b p) d -> p b d", p=128),
                       writes=[Bv[s]], sembuf=Bv[s])
                for kh in range(2):
                    pr = slice(64 * kh, 64 * kh + 64)
                    keys = [("l", j, kbl[j]) for j in range(nk)]
                    if isS:
                        keys += [("c", 0, 0), ("c", 1, 0)]
                    nkeys = len(keys)
                    for idx, (kind, j, gb) in enumerate(keys):
                        if kind == "l":
                            lk = k_sb[s][pr, j, :]
                            lv = v_sb[s][:, j, 64 * kh:64 * kh + 64]
                            rk = [Bk[s]]
                            rv = [Bv[s]]
                        else:
                            lk = ck_sb[pr, j * 128:(j + 1) * 128]
                            lv = cv_sb[:, j, 64 * kh:64 * kh + 64]
                            rk = [Bck]
                            rv = [Bck]
                        sb_ = 3 + idx % 2
                        kb.op(PE, lambda e, lk=lk, s=s, pr=pr, sb_=sb_: e.matmul(
                            psb[sb_][:, :], lhsT=lk, rhs=q_sb[s][pr, :, :], start=True, stop=True),
                            reads=rk + [Bq[s]], writes=[PSB[sb_]])
                        pi = pcount % 3
                        pcount += 1
                        kb.op(ACT, lambda e, pi=pi, sb_=sb_: e.activation(out=pT[pi][:, :], in_=psb[sb_][:, :], func=AF.Exp, scale=0.125),
                              reads=[PSB[sb_]], writes=[BpT[pi]])
                        if kind == "l" and isS and gb != qb:
                            mi = 0 if gb < qb else 1
                            mh = mask_f[:, mi, :]
                            map_ = bass.AP(mask_f, mh.offset, [list(mh.ap[0]), [0, 4], [1, 128]])
                            kb.op(DVE, lambda e, pi=pi, map_=map_: e.tensor_tensor(
                                out=pT[pi][:, :].rearrange("p (c t) -> p c t", c=4),
                                in0=pT[pi][:, :].rearrange("p (c t) -> p c t", c=4), in1=map_, op=ALU.mult),
                                reads=[B_const], writes=[BpT[pi]])
                        kb.op(PE, lambda e, lv=lv, pi=pi, idx=idx, nkeys=nkeys: e.matmul(
                            psb[5][0:64, :], lhsT=lv, rhs=pT[pi][:, :], start=(idx == 0), stop=(idx == nkeys - 1)),
                            reads=rv + [BpT[pi]], writes=[PSB[5]], sig=False)
                        kb.op(PE, lambda e, pi=pi, idx=idx, nkeys=nkeys: e.matmul(
                            psb[6][0:64, :], lhsT=ones64[:, :], rhs=pT[pi][:, :], start=(idx == 0), stop=(idx == nkeys - 1)),
                            reads=[BpT[pi], B_const], writes=[PSB[6]], sig=True)
                    skb = sk[:, kh, :]
                    skap = bass.AP(sk, skb.offset, [list(skb.ap[0]), [1, 4], [0, 128]])
                    kb.op(DVE, lambda e, skap=skap: e.tensor_tensor(
                        out=d2[:, :].rearrange("p (c t) -> p c t", c=4),
                        in0=psb[6][0:64, :].rearrange("p (c t) -> p c t", c=4), in1=skap, op=ALU.add),
                        reads=[PSB[6], Bsk], writes=[Bd2])
                    kb.op(DVE, lambda e: e.reciprocal(out=d2[:, :], in_=d2[:, :]), reads=[Bd2], writes=[Bd2])
                    oi = ocount % 2
                    ocount += 1
                    kb.op(DVE, lambda e, oi=oi: e.tensor_tensor(out=o_sb[oi][:, :], in0=psb[5][0:64, :], in1=d2[:, :], op=ALU.mult),
                          reads=[PSB[5], Bd2], writes=[Bo[oi]])
                    dst = A(mix_s)[2:6, 64 * kh:64 * kh + 64, t0:t0 + 128].rearrange("c p t -> p c t")
                    kb.dma(Q, dst, o_sb[oi][:, :].rearrange("p (c t) -> p c t", c=4), reads=[Bo[oi]], sembuf=Bo[oi])
                    yield

    def attn_stage(l):
        with ExitStack() as es:
            rel = []
            for _ in attn_gen(l, es, SP, rel):
                pass
            kb.barrier()
            kb.release(rel)

    def pool_stage(l):
        for (nseq, L, tbase, invc_d) in ((1, TS, 0, c_invc_s), (NPS, 256, TS, c_invc_p)):
            with ExitStack() as es:
                LP = L + 16
                u = stage_sb(es, "c_u", [128, nseq, LP], F32)
                wa = stage_sb(es, "c_wa", [128, nseq, LP], F32)
                wb_ = stage_sb(es, "c_wb", [128, nseq, LP], F32)
                sel = stage_sb(es, "c_sel", [128, nseq, L], F32)
                invc = stage_sb(es, "c_invc", [128, L], F32)
                dT = stage_sb(es, "c_dT", [128, nseq, L], BF16)
                pw = stage_sb(es, "c_pw", [128, 128], BF16)
                psc = stage_sb(es, "c_psc", [128, 2], F32)
                yo = [stage_sb(es, "c_yo%d" % k, [128, 512], BF16) for k in range(2)]
                Bu, Bwa, Bwb, Bsel, Binv, Bd, Bpw = Buf(), Buf(), Buf(), Buf(), Buf(), Buf(), Buf()
                Byo = [Buf(), Buf()]
                Bpsc = Buf()
                kb.dma(SP, psc[:, :], A(poolsc)[l], writes=[Bpsc], sembuf=Bpsc)
                yc = 0
                for ch in range(2):
                    kb.op(DVE, lambda e: e.memset(u[:, :, :], 0.0), writes=[Bu])
                    kb.op(DVE, lambda e: e.memset(wa[:, :, :], 0.0), writes=[Bwa])
                    kb.op(DVE, lambda e: e.memset(wb_[:, :, :], 0.0), writes=[Bwb])
                    for sq_ in range(nseq):
                        kb.dma2(SP, u[:, sq_, 8:8 + L], A(up_s)[ch, :, tbase + sq_ * L:tbase + (sq_ + 1) * L], L,
                                writes=[Bu], sembuf=Bu)
                    kb.dma2(SP, invc[:, :], A(invc_d)[ch], L, writes=[Binv], sembuf=Binv)
                    kb.dma(POOL, pw[:, :], A(poolw)[l, ch], writes=[Bpw], sembuf=Bpw)
                    kb.op(DVE, lambda e: e.tensor_tensor(out=wa[:, :, 1:LP], in0=u[:, :, 0:LP - 1], in1=u[:, :, 1:LP], op=ALU.add),
                          reads=[Bu], writes=[Bwa])
                    kb.op(DVE, lambda e: e.tensor_tensor(out=wb_[:, :, 2:LP - 1], in0=wa[:, :, 1:LP - 2], in1=wa[:, :, 3:LP], op=ALU.add),
                          reads=[Bwa], writes=[Bwb])
                    if ch == 0:
                        kb.op(DVE, lambda e: e.tensor_copy(out=sel[0:64, :, :], in_=wa[0:64, :, 8:8 + L]), reads=[Bwa], writes=[Bsel])
                        kb.op(DVE, lambda e: e.tensor_copy(out=sel[64:128, :, :], in_=wb_[64:128, :, 8:8 + L]), reads=[Bwb], writes=[Bsel])
                    else:
                        kb.op(DVE, lambda e: e.tensor_tensor(out=wa[:, :, 4:LP - 3], in0=wb_[:, :, 2:LP - 5], in1=wb_[:, :, 6:LP - 1], op=ALU.add),
                              reads=[Bwb], writes=[Bwa])
                        kb.op(DVE, lambda e: e.tensor_tensor(out=wb_[:, :, 8:LP - 7], in0=wa[:, :, 4:LP - 11], in1=wa[:, :, 12:LP - 3], op=ALU.add),
                              reads=[Bwa], writes=[Bwb])
                        kb.op(DVE, lambda e: e.tensor_copy(out=sel[0:64, :, :], in_=wa[0:64, :, 8:8 + L]), reads=[Bwa], writes=[Bsel])
                        kb.op(DVE, lambda e: e.tensor_copy(out=sel[64:128, :, :], in_=wb_[64:128, :, 8:8 + L]), reads=[Bwb], writes=[Bsel])
                    for sq_ in range(nseq):
                        kb.op(DVE, lambda e, sq_=sq_: e.tensor_tensor(out=sel[:, sq_, :], in0=sel[:, sq_, :], in1=invc[:, :], op=ALU.mult),
                              reads=[Binv], writes=[Bsel])
                    kb.op(DVE, lambda e: e.tensor_tensor(out=dT[:, :, :], in0=sel[:, :, :], in1=u[:, :, 8:8 + L], op=ALU.subtract),
                          reads=[Bsel, Bu], writes=[Bd])
                    dflat = dT[:, :, :].rearrange("p s t -> p (s t)")
                    for cb in range(nseq * L // 512):
                        pb = cb % 2
                        kb.op(PE, lambda e, cb=cb, pb=pb: e.matmul(psb[pb][:, :], lhsT=pw[:, :], rhs=dflat[:, cb * 512:(cb + 1) * 512],
                                                                 start=True, stop=True), reads=[Bpw, Bd], writes=[PSB[pb]])
                        k2 = yc % 2
                        yc += 1
                        kb.op(ACT, lambda e, pb=pb, k2=k2, ch=ch: e.activation(out=yo[k2][:, :], in_=psb[pb][:, :], func=AF.Identity,
                                                                             scale=psc[:, ch:ch + 1]),
                              reads=[PSB[pb], Bpsc], writes=[Byo[k2]])
                        kb.dma(SP, A(mix_s)[ch, :, tbase + cb * 512:tbase + (cb + 1) * 512], yo[k2][:, :], reads=[Byo[k2]], sembuf=Byo[k2])
                    kb.barrier()
                kb.release([Bu, Binv, Bpw, Bpsc] + Byo)

    def sin_act(ps_ap, out_ap, fr_ap, bb_ap, tmp, Btmp_, rd, wr, tmp2=None, Btmp2_=None):
        kb.op(DVE, lambda e: e.tensor_scalar(out=tmp, in0=ps_ap, scalar1=fr_ap, scalar2=bb_ap, op0=ALU.mult, op1=ALU.add),
              reads=rd, writes=[Btmp_])
        for _r in range(2):
            kb.op(DVE, lambda e: e.tensor_scalar(out=tmp2, in0=tmp, scalar1=math.pi, scalar2=-TWO_PI, op0=ALU.is_gt, op1=ALU.mult),
                  reads=[Btmp_], writes=[Btmp2_])
            kb.op(DVE, lambda e: e.tensor_tensor(out=tmp, in0=tmp, in1=tmp2, op=ALU.add), reads=[Btmp2_], writes=[Btmp_])
            kb.op(DVE, lambda e: e.tensor_scalar(out=tmp2, in0=tmp, scalar1=-math.pi, scalar2=TWO_PI, op0=ALU.is_lt, op1=ALU.mult),
                  reads=[Btmp_], writes=[Btmp2_])
            kb.op(DVE, lambda e: e.tensor_tensor(out=tmp, in0=tmp, in1=tmp2, op=ALU.add), reads=[Btmp2_], writes=[Btmp_])
        kb.op(DVE, lambda e: e.tensor_scalar(out=tmp, in0=tmp, scalar1=-3.1415925, scalar2=3.1415925, op0=ALU.max, op1=ALU.min),
              reads=[], writes=[Btmp_])
        kb.op(ACT, lambda e: e.activation(out=out_ap, in_=tmp, func=AF.Sin), reads=[Btmp_], writes=wr)

    def filt_stage(l, L, zT_d, tl_d, G_d):
        with ExitStack() as es:
            zT = stage_sb(es, "d_zT", [33, L], F32)
            tl = stage_sb(es, "d_tl", [128, L], F32)
            w1 = stage_sb(es, "d_w1", [33, 64], F32)
            w2 = stage_sb(es, "d_w2", [64, 64], F32)
            w3 = stage_sb(es, "d_w3", [64, 1024], F32)
            sm = stage_sb(es, "d_sm", [64, 8], F32)
            b3 = stage_sb(es, "d_b3", [128, 8], F32)
            dec = stage_sb(es, "d_dec", [128, 4], F32)
            h1 = stage_sb(es, "d_h1", [64, L], F32)
            h2 = stage_sb(es, "d_h2", [64, L], F32)
            tmp = stage_sb(es, "d_tmp", [128, 512], F32)
            tmp2 = stage_sb(es, "d_tmp2", [128, 512], F32)
            Btmp2_ = Buf()
            dk = stage_sb(es, "d_dk", [128, 512], F32)
            fl = [stage_sb(es, "d_fl%d" % k, [128, L], F32) for k in range(2)]
            flb = [stage_sb(es, "d_flb%d" % k, [128, L], BF16) for k in range(2)]
            ssum = stage_sb(es, "d_ssum", [128, 4], F32)
            Bc, Bh1, Bh2, Btmp_, Bdk, Bss = Buf(), Buf(), Buf(), Buf(), Buf(), Buf()
            Bfl = [Buf(), Buf()]
            Bflb = [Buf(), Buf()]
            for (dst, srcap, nn) in ((zT[:, :], A(zT_d)[:, :], L), (tl[:, :], A(tl_d)[:, :], L), (w1[:, :], A(fw1)[l], 64),
                                     (w2[:, :], A(fw2)[l], 64), (w3[:, :], A(fw3)[l], 1024), (sm[:, :], A(fsm)[l], 8),
                                     (b3[:, :], A(fb3)[l], 8), (dec[:, :], A(fdec)[l], 4)):
                kb.dma2(SP, dst, srcap, nn, writes=[Bc], sembuf=Bc)
            for k in range(2):
                kb.op(DVE, lambda e, k=k: e.tensor_scalar(out=sm[:, 4 + k:5 + k], in0=sm[:, k:k + 1], scalar1=sm[:, 2 + k:3 + k],
                                                         scalar2=0.0, op0=ALU.mult, op1=ALU.add), reads=[Bc], writes=[Bc])
            kb.op(ACT, lambda e: e.activation(out=dec[:, :], in_=dec[:, :], func=AF.Abs), reads=[Bc], writes=[Bc])
            kb.op(DVE, lambda e: e.tensor_scalar(out=dec[:, :], in0=dec[:, :], scalar1=-1.0, scalar2=None, op0=ALU.mult),
                  reads=[Bc], writes=[Bc])
            nb5 = L // 512 if L >= 512 else 1
            W = min(L, 512)
            for cb in range(nb5):
                cs = slice(cb * W, (cb + 1) * W)
                kb.op(PE, lambda e, cs=cs: e.matmul(psb[0][0:64, 0:W], lhsT=w1[:, :], rhs=zT[:, cs], start=True, stop=True),
                      reads=[Bc], writes=[PSB[0]])
                sin_act(psb[0][0:64, 0:W], h1[:, cs], sm[:, 2:3], sm[:, 4:5], tmp[0:64, 0:W], Btmp_, [PSB[0], Bc], [Bh1], tmp2[0:64, 0:W], Btmp2_)
            for cb in range(nb5):
                cs = slice(cb * W, (cb + 1) * W)
                kb.op(PE, lambda e, cs=cs: e.matmul(psb[1][0:64, 0:W], lhsT=w2[:, :], rhs=h1[:, cs], start=True, stop=True),
                      reads=[Bc, Bh1], writes=[PSB[1]])
                sin_act(psb[1][0:64, 0:W], h2[:, cs], sm[:, 3:4], sm[:, 5:6], tmp[0:64, 0:W], Btmp_, [PSB[1], Bc], [Bh2], tmp2[0:64, 0:W], Btmp2_)
            for o in range(2):
                for half in range(2):
                    oh = o * 2 + half
                    for d in range(2):
                        fc = d * 4 + oh
                        for cb in range(nb5):
                            cs = slice(cb * W, (cb + 1) * W)
                            pb = 2 + cb % 2
                            kb.op(PE, lambda e, cs=cs, fc=fc, pb=pb: e.matmul(psb[pb][:, 0:W], lhsT=w3[:, fc * 128:(fc + 1) * 128],
                                                                          rhs=h2[:, cs], start=True, stop=True),
                                  reads=[Bc, Bh2], writes=[PSB[pb]])
                            kb.op(ACT, lambda e, cs=cs, oh=oh: e.activation(out=dk[:, 0:W], in_=tl[:, cs], func=AF.Exp, scale=dec[:, oh:oh + 1]),
                                  reads=[Bc], writes=[Bdk])
                            kb.op(DVE, lambda e: e.tensor_scalar(out=dk[:, 0:W], in0=dk[:, 0:W], scalar1=0.05, scalar2=None, op0=ALU.add),
                                  reads=[], writes=[Bdk])
                            kb.op(DVE, lambda e, cs=cs, fc=fc, pb=pb, d=d: e.scalar_tensor_tensor(
                                out=fl[d][:, cs], in0=psb[pb][:, 0:W], scalar=b3[:, fc:fc + 1], in1=dk[:, 0:W], op0=ALU.add, op1=ALU.mult),
                                reads=[PSB[pb], Bdk, Bc], writes=[Bfl[d]])
                        kb.op(DVE, lambda e, d=d: e.tensor_reduce(out=ssum[:, d:d + 1], in_=fl[d][:, :], axis=AX.X, op=ALU.add,
                                                                 apply_absolute_value=True), reads=[Bfl[d]], writes=[Bss])
                    kb.op(ACT, lambda e: e.activation(out=ssum[:, 2:3], in_=fl[1][:, 0:1], func=AF.Abs), reads=[Bfl[1]], writes=[Bss])
                    kb.op(DVE, lambda e: e.tensor_scalar(out=ssum[:, 2:3], in0=ssum[:, 2:3], scalar1=-1.0, scalar2=None, op0=ALU.mult),
                          reads=[], writes=[Bss])
                    kb.op(DVE, lambda e: e.tensor_tensor(out=ssum[:, 3:4], in0=ssum[:, 0:1], in1=ssum[:, 1:2], op=ALU.add), reads=[], writes=[Bss])
                    kb.op(DVE, lambda e: e.scalar_tensor_tensor(out=ssum[:, 3:4], in0=ssum[:, 3:4], scalar=1e-6, in1=ssum[:, 2:3], op0=ALU.add, op1=ALU.add),
                          reads=[], writes=[Bss])
                    kb.op(DVE, lambda e: e.reciprocal(out=ssum[:, 3:4], in_=ssum[:, 3:4]), reads=[], writes=[Bss])
                    kb.op(ACT, lambda e: e.activation(out=flb[0][:, :], in_=fl[0][:, :], func=AF.Identity, scale=ssum[:, 3:4]),
                          reads=[Bfl[0], Bss], writes=[Bflb[0]])
                    f1 = flb[1][:, :]
                    rev = bass.AP(flb[1], f1.offset + L - 1, [list(f1.ap[0]), [-1, L]])
                    kb.op(ACT, lambda e, rev=rev: e.activation(out=rev, in_=fl[1][:, :], func=AF.Identity, scale=ssum[:, 3:4]),
                          reads=[Bfl[1], Bss], writes=[Bflb[1]])
                    kb.dma2(SP, A(G_d)[o, half * 128:(half + 1) * 128, L - 1:2 * L - 1], flb[0][:, :], L, reads=[Bflb[0]], sembuf=Bflb[0])
                    kb.dma2(SP, A(G_d)[o, half * 128:(half + 1) * 128, 0:L - 1], flb[1][:, 0:L - 1], L - 1, reads=[Bflb[1]], sembuf=Bflb[1])
            kb.barrier()
            kb.release([Bc] + Bflb)

    def hyena_gen(l, nseq, L, tbase, G_d, es, rel, cbanks, tbank, NST, pump=None, tag="S"):
        BS = 128
        nb = L // BS
        nblk = nseq * nb
        SW = BS * (2 * nb - 1)
        GW = 2 * L - 1
        cpb = 512 // nblk
        TPB = 4
        raw = stage_sb(es, "h_raw" + tag, [128, nseq, L + 2], F32)
        cv_ = [stage_sb(es, "h_c%d" % k + tag, [128, nseq, L], F32) for k in range(3)]
        zrev = stage_sb(es, "h_zrev" + tag, [128, nblk, BS], F32)
        zf = stage_sb(es, "h_zf" + tag, [BS, nblk, 128], BF16)
        ytm = raw[:, :, :].rearrange("p s t -> p (s t)")[:, 0:nblk * 128].rearrange("p (b c) -> p b c", c=128)
        strips = [stage_sb(es, "h_st%d" % k + tag, [BS, SW], BF16) for k in range(NST)]
        swt = stage_sb(es, "h_sw" + tag, [128, 6, 4], F32)
        hb = stage_sb(es, "h_hb" + tag, [128, 4], F32)
        yo = zf[:, :, :].rearrange("p b c -> p (b c)").rearrange("p (s t) -> p s t", s=nseq)
        z1 = cv_[1]
        Braw, Bzrev, Bzf, Bsw = Buf(), Buf(), Buf(), Buf()
        Bytm, BcT, Byo = Braw, Bzrev, Bzf
        convT = zrev[:, :, :].rearrange("p (s b) t -> p s (b t)", s=nseq)
        cflat = zrev[:, :, :].rearrange("p b t -> p (b t)")
        Bcv = [Buf(), Buf(), Buf()]
        Bz1 = Bcv[1]
        Bst = [Buf() for _ in range(NST)]
        rel.extend([Braw, Bsw, Bzf] + Bst)
        kb.dma(SP, swt[:, :, :], A(shw)[l], writes=[Bsw], sembuf=Bsw)
        kb.dma(SP, hb[:, :], A(hyb)[l], writes=[Bsw], sembuf=Bsw)
        Dl = [0] + [d for d in range(-(nb - 1), nb) if d != 0]
        scount = 0
        gcount = 0
        for half in range(2):
            for part in range(3):
                chn = part * 2 + half
                kb.op(DVE, lambda e: e.memset(raw[:, :, :], 0.0), writes=[Braw])
                for sq_ in range(nseq):
                    kb.dma2(SP, raw[:, sq_, 1:L + 1], A(uh_s)[chn, :, tbase + sq_ * L:tbase + (sq_ + 1) * L], L, writes=[Braw], sembuf=Braw)
                kb.op(ACT, lambda e, part=part, chn=chn: e.activation(out=cv_[part][:, :, :], in_=raw[:, :, 1:L + 1], func=AF.Identity,
                                                                   scale=swt[:, chn, 1:2], bias=swt[:, chn, 3:4]),
                      reads=[Braw, Bsw], writes=[Bcv[part]])
                kb.op(DVE, lambda e, part=part, chn=chn: e.scalar_tensor_tensor(out=cv_[part][:, :, :], in0=raw[:, :, 0:L], scalar=swt[:, chn, 0:1],
                                                                             in1=cv_[part][:, :, :], op0=ALU.mult, op1=ALU.add),
                      reads=[Braw, Bsw], writes=[Bcv[part]])
                kb.op(DVE, lambda e, part=part, chn=chn: e.scalar_tensor_tensor(out=cv_[part][:, :, :], in0=raw[:, :, 2:L + 2], scalar=swt[:, chn, 2:3],
                                                                             in1=cv_[part][:, :, :], op0=ALU.mult, op1=ALU.add),
                      reads=[Braw, Bsw], writes=[Bcv[part]])
                yield
            for o in range(2):
                zsrc = cv_[0] if o == 0 else z1
                Bzs = Bcv[0] if o == 0 else Bz1
                zs = zsrc[:, :, :]
                pstep = list(zs.ap[0])
                revap = bass.AP(zsrc, zs.offset + BS - 1, [pstep, [BS, nblk], [-1, BS]])
                kb.op(POOL, lambda e, revap=revap: e.tensor_copy(out=zrev[:, :, :], in_=revap), reads=[Bzs], writes=[Bzrev])
                for b0 in range(0, nblk, TPB):
                    for b in range(b0, b0 + TPB):
                        kb.op(PE, lambda e, b=b, b0=b0: e.transpose(psb[tbank][0:BS, (b - b0) * 128:(b - b0 + 1) * 128], zrev[:, b, :], id_f[:, :]),
                              reads=[Bzrev, B_const], writes=[PSB[tbank]], sig=(b == b0 + TPB - 1))
                    kb.op(ACT, lambda e, b0=b0: e.activation(out=zf[:, b0:b0 + TPB, :].rearrange("p b c -> p (b c)"), in_=psb[tbank][0:BS, :], func=AF.Identity),
                          reads=[PSB[tbank]], writes=[Bzf])
                    yield
                for c0 in range(0, 128, cpb):
                    pb = cbanks[gcount % len(cbanks)]
                    gcount += 1
                    for c in range(c0, c0 + cpb):
                        s = scount % NST
                        scount += 1
                        if _DBG.get("split64") and tag == "S":
                            wfrac = _DBG.get("wfrac", 1.0)
                            SWx = int(SW * wfrac)
                            for hp in range(2):
                                src = bass.AP(G_d, (o * 256 + half * 128 + c) * GW + 64 * hp, [[1, 64], [1, SWx]])
                                kb.dma2(SP, strips[s][64 * hp:64 * hp + 64, 0:SWx], src, SWx, writes=([Bst[s]] if hp == 0 else []), sembuf=Bst[s], maxel=4096)
                            Bst[s].w = kb.lasttok(Bst[s], False)
                        else:
                            src = bass.AP(G_d, (o * 256 + half * 128 + c) * GW, [[1, BS], [1, SW]])
                            kb.dma2(SP, strips[s][:, :], src, SW, writes=[Bst[s]], sembuf=Bst[s], maxel=_DBG.get("smax", 4096))
                        col0 = (c - c0) * nblk
                        for di, Dd in enumerate(Dl):
                            J0, J1 = max(0, -Dd), min(nb, nb - Dd)
                            zfa = zf[:, :, :].rearrange("p (s b) c -> p s b c", s=nseq)[:, :, J0:J1, c]
                            oa = psb[pb][0:BS, col0:col0 + nblk].rearrange("p (s b) -> p s b", s=nseq)[:, :, J0 + Dd:J1 + Dd]
                            kb.op(PE, lambda e, s=s, Dd=Dd, zfa=zfa, oa=oa, di=di: e.matmul(
                                oa, lhsT=strips[s][:, BS * (Dd + nb - 1):BS * (Dd + nb)], rhs=zfa,
                                start=(di == 0), stop=(di == len(Dl) - 1)),
                                reads=[Bst[s], Bzf], writes=[PSB[pb]], sig=(di == len(Dl) - 1))
                        if pump is not None:
                            pump()
                        if c != c0 + cpb - 1:
                            yield
                    kb.op(ACT, lambda e, c0=c0, pb=pb: e.activation(
                        out=ytm[:, :, c0:c0 + cpb], in_=psb[pb][0:BS, :].rearrange("p (c b) -> p b c", b=nblk), func=AF.Identity),
                        reads=[PSB[pb]], writes=[Bytm])
                    yield
                for b0 in range(0, nblk, TPB):
                    for b in range(b0, b0 + TPB):
                        kb.op(PE, lambda e, b=b, b0=b0: e.transpose(psb[tbank][:, (b - b0) * BS:(b - b0 + 1) * BS], ytm[:, b, :], id_f[0:BS, 0:BS]),
                              reads=[Bytm, B_const], writes=[PSB[tbank]], sig=(b == b0 + TPB - 1))
                    kb.op(ACT, lambda e, b0=b0: e.activation(out=cflat[:, b0 * BS:(b0 + TPB) * BS], in_=psb[tbank][:, :], func=AF.Identity),
                          reads=[PSB[tbank]], writes=[BcT])
                    yield
                oh = o * 2 + half
                kb.op(DVE, lambda e, zsrc=zsrc, oh=oh: e.scalar_tensor_tensor(out=convT, in0=zsrc[:, :, :], scalar=hb[:, oh:oh + 1],
                                                                           in1=convT, op0=ALU.mult, op1=ALU.add),
                      reads=[Bzs, Bsw], writes=[BcT])
                if o == 0:
                    kb.op(DVE, lambda e: e.tensor_tensor(out=z1[:, :, :], in0=convT, in1=cv_[1][:, :, :], op=ALU.mult),
                          reads=[BcT], writes=[Bz1])
                else:
                    kb.op(DVE, lambda e: e.tensor_tensor(out=yo, in0=convT, in1=cv_[2][:, :, :], op=ALU.mult),
                          reads=[BcT, Bcv[2]], writes=[Byo])
                    for sq_ in range(nseq):
                        kb.dma2(SP, A(mix_s)[6 + half, :, tbase + sq_ * L:tbase + (sq_ + 1) * L], yo[:, sq_, :], L, reads=[Byo], sembuf=Byo)
                yield

    def hyena_all(l, with_attn):
        with ExitStack() as es:
            rel = []
            ag = attn_gen(l, es, POOL, rel) if with_attn else iter(())
            next(ag, None)
            pg = hyena_gen(l, NPS, 256, TS, G_p, es, rel, [2], 7, 3, tag="P")
            next(pg, None)
            cnt = [0]

            def pump():
                cnt[0] += 1
                next(pg, None)
                if cnt[0] % 3 == 0:
                    next(pg, None)
                if cnt[0] % 5 == 0:
                    next(ag, None)
            with ExitStack() as es2:
                for _ in hyena_gen(l, 1, TS, 0, G_s, es2, rel, [0, 1], 7, _DBG.get("nst", 4), pump=pump, tag="S"):
                    pass
                for _ in pg:
                    pass
                for _ in ag:
                    pass
                kb.barrier()
            kb.release(rel)

    def mixe_stage(l):
        NB = T // 512
        with ExitStack() as es:
            xts = [stage_sb(es, "e_xt%d" % k, [128, 8, 512], F32) for k in range(2)]
            mts = [stage_sb(es, "e_mt%d" % k, [128, 8, 512], BF16) for k in range(2)]
            wo = stage_sb(es, "e_wo", [128, 8, 1024], BF16)
            Bxs = [[Buf() for _ in range(8)] for _ in range(2)]
            Bms = [Buf(), Buf()]
            Bw = Buf()
            for kc in range(8):
                kb.dma(POOL, wo[:, kc, :], A(wout)[l][:, kc * 1024:(kc + 1) * 1024], writes=([Bw] if kc == 0 else []), sembuf=Bw)
            Bw.w = kb.lasttok(Bw, True)

            def load(blk):
                xt, Bx, mt, Bm = xts[blk % 2], Bxs[blk % 2], mts[blk % 2], Bms[blk % 2]
                t0 = blk * 512
                for kc in range(8):
                    kb.dma(SP, xt[:, kc, :], A(yT)[:, kc, t0:t0 + 512], writes=[Bx[kc]], sembuf=Bx[kc])
                kb.dma(SP, mt[:, :, :], A(mix_s)[:, :, t0:t0 + 512].rearrange("c p t -> p c t"), writes=[Bm], sembuf=Bm)

            load(0)
            for blk in range(NB):
                xt, Bx, mt, Bm = xts[blk % 2], Bxs[blk % 2], mts[blk % 2], Bms[blk % 2]
                t0 = blk * 512
                ci = 0 if t0 < TS else 1
                if blk + 1 < NB:
                    load(blk + 1)
                for oc in range(8):
                    pb = oc % 4
                    for kc in range(8):
                        kb.op(PE, lambda e, oc=oc, kc=kc, pb=pb, mt=mt: e.matmul(psb[pb][:, :], lhsT=wo[:, kc, oc * 128:(oc + 1) * 128], rhs=mt[:, kc, :],
                                                                              start=(kc == 0), stop=(kc == 7)), reads=[Bw, Bm], writes=[PSB[pb]], sig=(kc == 7))
                    kb.op(DVE, lambda e, oc=oc, pb=pb, ci=ci, xt=xt: e.scalar_tensor_tensor(out=xt[:, oc, :], in0=psb[pb][:, :], scalar=Gg[:, 1, oc, ci:ci + 1],
                                                                                         in1=xt[:, oc, :], op0=ALU.mult, op1=ALU.add),
                          reads=[PSB[pb], B_mod], writes=[Bx[oc]])
                for kc in range(8):
                    kb.dma(SP, A(yT)[:, kc, t0:t0 + 512], xt[:, kc, :], reads=[Bx[kc]], sembuf=Bx[kc])
            kb.barrier()
            kb.release(Bxs[0] + Bxs[1] + Bms + [Bw])

    def want(name):
        return stages is None or name in stages
    kb.barrier()
    for l in range(2):
        if want("mod%d" % l):
            mod_stage(l)
        if want("ffn%d0" % l):
            ffn_stage(l, 0, xT_in if l == 0 else yT)
        if want("mixa%d" % l):
            mixa_stage(l)
        if want("pool%d" % l):
            pool_stage(l)
        if want("filt%d" % l):
            filt_stage(l, TS, c_zT_s, c_tl_s, G_s)
            filt_stage(l, 256, c_zT_p, c_tl_p, G_p)
        if want("hy%d" % l):
            hyena_all(l, want("attn%d" % l))
        elif want("attn%d" % l):
            attn_stage(l)
        if want("mixe%d" % l):
            mixe_stage(l)
        if want("ffn%d1" % l):
            ffn_stage(l, 1, yT)
    kb.barrier()
    return nc


def _consts():
    c = {}
    for (L, nm) in ((TS, "s"), (256, "p")):
        t = np.linspace(0.0, 1.0, L, dtype=np.float32)[:, None]
        bands = 16
        f = np.linspace(1e-4, bands - 1, bands, dtype=np.float32)[None, :]
        w = (2.0 * math.pi * np.arange(L, dtype=np.float32)[:, None] / L).astype(np.float32)
        z = np.concatenate([t, np.cos(f * w), -np.sin(f * w)], axis=-1).astype(np.float32)
        c["c_zT_" + nm] = np.ascontiguousarray(z.T)
        c["c_tl_" + nm] = np.ascontiguousarray(np.broadcast_to(t[:, 0][None, :], (128, L))).astype(np.float32)
        tt = np.arange(L)
        invc = np.zeros((2, 128, L), np.float32)
        for gi, win in enumerate((2, 4, 8, 16)):
            lo = np.clip(tt - win // 2, 0, L)
            hi = np.clip(tt + win // 2, 0, L)
            invc[gi // 2, (gi % 2) * 64:(gi % 2) * 64 + 64, :] = (1.0 / (hi - lo).astype(np.float32))[None, :]
        c["c_invc_" + nm] = invc
    tt = np.arange(TS)
    pos_row, pos_col = tt // 64, tt % 64
    inv = (10000.0 ** (-np.arange(16, dtype=np.float32) / 16)).astype(np.float32)
    cos = np.zeros((64, TS), np.float32)
    sin = np.zeros((64, TS), np.float32)
    for d in range(64):
        pos = pos_row if d < 32 else pos_col
        dd = d % 32
        ang = pos.astype(np.float32) * inv[dd % 16]
        cos[d] = np.cos(ang)
        sin[d] = -np.sin(ang) if dd < 16 else np.sin(ang)
    c["c_cos"] = np.concatenate([cos, cos], 0)
    c["c_sin"] = np.concatenate([sin, sin], 0)
    pm = np.zeros((128, 128), np.float32)
    for m in range(128):
        dd = m % 32
        k = m + 16 if dd < 16 else m - 16
        pm[k, m] = 1.0
    c["c_pm"] = pm
    bd = np.zeros((128, 128), np.float32)
    bd[:64, :64] = 1.0 / 64
    bd[64:, 64:] = 1.0 / 64
    c["c_bd"] = bd
    c["c_id"] = np.eye(128, dtype=np.float32)
    j = np.arange(128)[:, None]
    i = np.arange(128)[None, :]
    c["c_mask"] = np.stack([(j >= i), (j <= i)]).astype(np.float32)
    return c


_NC = None
_DBG = {}


def kernel(x_prompt, x_sample, cache_k, cache_v, c, c_ctx, ada_w, ada_b, norm_w,
           ffn_wg, ffn_wu, ffn_wd, w_in, w_out, pool_w, pool_scale, q_norm, k_norm,
           attn_sink, hy_short_w, hy_short_b, hy_f_w1, hy_f_b1, hy_f_w2, hy_f_b2,
           hy_f_w3, hy_f_b3, hy_sin_freq, hy_decay, hy_bias):
    global _NC
    f32 = np.float32
    g = lambda a: np.asarray(a, dtype=f32)
    x_prompt, x_sample, cache_k, cache_v, c, c_ctx = map(g, (x_prompt, x_sample, cache_k, cache_v, c, c_ctx))
    ada_w, ada_b, norm_w, ffn_wg, ffn_wu, ffn_wd, w_in, w_out = map(g, (ada_w, ada_b, norm_w, ffn_wg, ffn_wu, ffn_wd, w_in, w_out))
    pool_w, pool_scale, q_norm, k_norm, attn_sink = map(g, (pool_w, pool_scale, q_norm, k_norm, attn_sink))
    hy_short_w, hy_short_b, hy_f_w1, hy_f_b1, hy_f_w2, hy_f_b2 = map(g, (hy_short_w, hy_short_b, hy_f_w1, hy_f_b1, hy_f_w2, hy_f_b2))
    hy_f_w3, hy_f_b3, hy_sin_freq, hy_decay, hy_bias = map(g, (hy_f_w3, hy_f_b3, hy_sin_freq, hy_decay, hy_bias))

    def fm(v, n):
        return np.ascontiguousarray(np.swapaxes(v.reshape(v.shape[:-1] + (n, 128)), -1, -2))

    shared = dict(_consts())
    shared["adaw"] = np.ascontiguousarray(ada_w.reshape(2, 8, 128, 72, 128).transpose(0, 3, 2, 1, 4)).reshape(2, 72, 128, 1024)
    shared["adab"] = fm(ada_b, 72)
    shared["normw"] = np.ascontiguousarray(norm_w.reshape(2, 3, 8, 128).transpose(0, 3, 1, 2)).reshape(2, 128, 24)
    shared["wg"] = np.ascontiguousarray(ffn_wg.reshape(2, 2, 8, 128, NFF, 128).transpose(0, 1, 4, 3, 2, 5)).reshape(2, 2, NFF, 128, 1024)
    shared["wu"] = np.ascontiguousarray(ffn_wu.reshape(2, 2, 8, 128, NFF, 128).transpose(0, 1, 4, 3, 2, 5)).reshape(2, 2, NFF, 128, 1024)
    shared["wd"] = np.ascontiguousarray(ffn_wd.reshape(2, 2, NFF, 128, 1024).transpose(0, 1, 3, 2, 4)).reshape(2, 2, 128, NFF * 1024)
    qcols = []
    for cc in range(4):
        qcols += list(range(256 + 64 * cc, 256 + 64 * cc + 64)) + list(range(256 + 64 * (4 + cc), 256 + 64 * (4 + cc) + 64))
    colsFM = list(range(256)) + qcols + list(range(768, 896)) + list(range(1024, 1792))
    shared["win"] = np.ascontiguousarray(w_in[:, :, colsFM].reshape(2, 8, 128, 1664).transpose(0, 2, 1, 3)).reshape(2, 128, 8 * 1664)
    shared["winv"] = np.ascontiguousarray(w_in[:, :, 896:1024].reshape(2, 8, 128, 128).transpose(0, 2, 1, 3)).reshape(2, 128, 1024)
    rows = list(range(256))
    for cc in range(4):
        for kh in range(2):
            rows += list(range(256 + 64 * (4 * kh + cc), 256 + 64 * (4 * kh + cc) + 64))
    rows += list(range(768, 1024))
    shared["wout"] = np.ascontiguousarray(w_out[:, rows, :].reshape(2, 8, 128, 1024).transpose(0, 2, 1, 3)).reshape(2, 128, 8192)
    pw = np.zeros((2, 2, 128, 128), f32)
    for l in range(2):
        for ch in range(2):
            pw[l, ch, :64, :64] = pool_w[l, 2 * ch]
            pw[l, ch, 64:, 64:] = pool_w[l, 2 * ch + 1]
    shared["poolw"] = pw
    shared["poolsc"] = fm(pool_scale, 2)
    shared["qkn"] = np.ascontiguousarray(np.stack([np.concatenate([q_norm, q_norm], -1), np.concatenate([k_norm, k_norm], -1)], -1))
    shared["sinkb"] = np.ascontiguousarray(np.broadcast_to(attn_sink.reshape(2, 2, 1, 4), (2, 2, 64, 4)))
    shw = np.zeros((2, 128, 6, 4), f32)
    shw[:, :, :, 0:3] = hy_short_w.reshape(2, 3, 6, 128).transpose(0, 3, 2, 1)
    shw[:, :, :, 3] = hy_short_b.reshape(2, 6, 128).transpose(0, 2, 1)
    shared["shw"] = shw
    shared["fw1"] = hy_f_w1
    shared["fw2"] = hy_f_w2
    shared["fw3"] = hy_f_w3
    fsm = np.zeros((2, 64, 8), f32)
    fsm[:, :, 0] = hy_f_b1
    fsm[:, :, 1] = hy_f_b2
    fsm[:, :, 2] = hy_sin_freq[:, 0]
    fsm[:, :, 3] = hy_sin_freq[:, 1]
    shared["fsm"] = fsm
    shared["fb3"] = fm(hy_f_b3, 8)
    shared["fdec"] = fm(hy_decay.reshape(2, 512), 4)
    shared["hyb"] = fm(hy_bias.reshape(2, 512), 4)

    in_maps = []
    for r in range(8):
        sb = r % 4
        xs = x_sample[sb]
        xp = x_prompt[NPS * r:NPS * r + NPS].reshape(TPR, D)
        xall = np.concatenate([xs, xp], 0)
        m = dict(shared)
        m["xT"] = np.ascontiguousarray(xall.reshape(T, 8, 128).transpose(2, 1, 0))
        cd = np.stack([c[sb], c_ctx], -1)
        m["condT"] = np.ascontiguousarray(cd.reshape(8, 128, 2).transpose(1, 0, 2))
        m["ckT"] = np.ascontiguousarray(cache_k[sb].reshape(2, 256, 128).transpose(0, 2, 1))
        m["cv"] = np.ascontiguousarray(cache_v[sb].reshape(2, 256, 128))
        in_maps.append(m)

    if _DBG.get("maps_only"):
        return in_maps
    if _NC is None:
        _NC = build_program()
    res = run_bass_kernel_spmd(_NC, in_maps, core_ids=list(range(8)))
    y_prompt = np.zeros((16, 256, D), f32)
    y_sample = np.zeros((4, TS, D), f32)
    nk = np.zeros((16, 2, 256, 2, 64), f32)
    nv = np.zeros((16, 2, 256, 2, 64), f32)
    for r in range(8):
        o = res.results[r]
        yt = np.asarray(o["yT"]).transpose(2, 1, 0).reshape(T, D)
        if r < 4:
            y_sample[r] = yt[:TS]
        y_prompt[NPS * r:NPS * r + NPS] = yt[TS:].reshape(NPS, 256, D)
        okt = np.asarray(o["okT"])
        nk[NPS * r:NPS * r + NPS] = okt.transpose(2, 0, 1).reshape(NPS, 256, 2, 2, 64).transpose(0, 2, 1, 3, 4)
        ovv = np.asarray(o["ov"])
        nv[NPS * r:NPS * r + NPS] = ovv.reshape(2, NPS, 256, 2, 64).transpose(1, 0, 2, 3, 4)
    return (y_prompt, y_sample, nk, nv)
```

```python
import math
from contextlib import ExitStack
import numpy as np
import concourse.bass as bass
import concourse.mybir as mybir
from concourse.bass_utils import run_bass_kernel_spmd

F32 = mybir.dt.float32
BF16 = mybir.dt.bfloat16
ALU = mybir.AluOpType
AF = mybir.ActivationFunctionType
AX = mybir.AxisListType

D = 1024
DFF = 2816
NFF = 22
TS = 4096
TPR = 512
NPS = TPR // 256
T = TS + TPR
NBLK = T // 128
EPS = 1e-6
TWO_PI = 2.0 * math.pi


class Buf:
    def __init__(self, name=""):
        self.name = name
        self.w = None
        self.r = []
        self.dsem = None
        self.dsem_sw = None


class Eng:
    def __init__(self, kb, name, h, is_pe=False):
        self.kb, self.name, self.h, self.is_pe = kb, name, h, is_pe
        self.sem = kb.nc.alloc_semaphore("e_" + name)
        self.semid = id(self)
        self.n = 0
        self.waited = {}


class KB:
    def __init__(self, nc):
        self.nc = nc
        self.pe = Eng(self, "pe", nc.tensor, True)
        self.act = Eng(self, "act", nc.scalar)
        self.dve = Eng(self, "dve", nc.vector)
        self.pool = Eng(self, "pool", nc.gpsimd)
        self.sp = Eng(self, "sp", nc.sync)
        self.dsems = []
        self.free_dsems = {False: [], True: []}
        self.outstanding = []

    def _deps(self, reads, writes):
        toks = []
        for b in reads:
            if b.w is not None:
                toks.append(b.w)
        for b in writes:
            if b.w is not None:
                toks.append(b.w)
            toks.extend(b.r)
        return toks

    def _wait(self, eng, toks):
        mx = {}
        for (sem, key, val) in toks:
            if key == eng.semid and eng.is_pe:
                continue
            if key not in mx or mx[key][1] < val:
                mx[key] = (sem, val)
        for key, (sem, val) in mx.items():
            if eng.waited.get(key, 0) < val:
                eng.h.wait_ge(sem, val)
                eng.waited[key] = val

    def _update(self, tok, reads, writes):
        for b in reads:
            b.r.append(tok)
        for b in writes:
            b.w = tok
            b.r = []

    def op(self, eng, fn, reads=(), writes=(), sig=True):
        self._wait(eng, self._deps(reads, writes))
        ins = fn(eng.h)
        if sig or not eng.is_pe:
            eng.n += 1
            ins.then_inc(eng.sem, 1)
            tok = (eng.sem, eng.semid, eng.n)
        else:
            tok = (eng.sem, eng.semid, eng.n + 1)
        self._update(tok, reads, writes)
        return tok

    def get_dsem(self, b, sw):
        attr = "dsem_sw" if sw else "dsem"
        if getattr(b, attr, None) is None:
            if self.free_dsems[sw]:
                setattr(b, attr, self.free_dsems[sw].pop())
            else:
                sm = self.nc.alloc_semaphore("d%d" % len(self.dsems))
                ds = [sm, 0, len(self.dsems) + 1000]
                self.dsems.append(ds)
                setattr(b, attr, ds)
        return getattr(b, attr)

    def lasttok(self, b, sw):
        ds = b.dsem_sw if sw else b.dsem
        return (ds[0], ds[2], ds[1])

    def dma(self, q, out, in_, reads=(), writes=(), sembuf=None):
        self._wait(q, self._deps(reads, writes))
        ds = self.get_dsem(sembuf, q is self.pool)
        ds[1] += 16
        q.h.dma_start(out=out, in_=in_).then_inc(ds[0], 16)
        tok = (ds[0], ds[2], ds[1])
        self._update(tok, reads, writes)
        self.outstanding.append(tok)
        return tok

    def dma2(self, q, out, in_, n, reads=(), writes=(), sembuf=None, maxel=2048):
        toks = None
        for a in range(0, n, maxel):
            b = min(n, a + maxel)
            toks = self.dma(q, out[:, a:b], in_[:, a:b], reads=reads, writes=(writes if a == 0 else ()), sembuf=sembuf)
        if writes:
            for bb in writes:
                bb.w = toks
        return toks

    def release(self, bufs):
        for b in bufs:
            if b.dsem is not None:
                self.free_dsems[False].append(b.dsem)
                b.dsem = None
            if b.dsem_sw is not None:
                self.free_dsems[True].append(b.dsem_sw)
                b.dsem_sw = None

    def barrier(self):
        toks = list(self.outstanding)
        for e in (self.pe, self.act, self.dve, self.pool, self.sp):
            if e.n > 0:
                toks.append((e.sem, e.semid, e.n))
        for e in (self.pe, self.act, self.dve, self.pool, self.sp):
            mx = {}
            for (sem, key, val) in toks:
                if key == e.semid:
                    continue
                if key not in mx or mx[key][2] < val:
                    mx[key] = (sem, key, val)
            for tk in mx.values():
                if e.waited.get(tk[1], 0) < tk[2]:
                    e.h.wait_ge(tk[0], tk[2])
                    e.waited[tk[1]] = tk[2]
        self.outstanding = []


def build_program(stages=None, debug=False):
    nc = bass.Bass("TRN2", target_bir_lowering=False)
    kb = KB(nc)
    PE, ACT, DVE, POOL, SP = kb.pe, kb.act, kb.dve, kb.pool, kb.sp

    def din(name, shape, dt=F32):
        return nc.dram_tensor(name, list(shape), dt, kind="ExternalInput")

    def dout(name, shape, dt=F32):
        return nc.dram_tensor(name, list(shape), dt, kind="ExternalOutput")

    def dscr(name, shape, dt):
        return nc.dram_tensor(name, list(shape), dt, kind="ExternalOutput" if debug else "Internal")

    xT_in = din("xT", [128, 8, T])
    condT = din("condT", [128, 8, 2])
    adaw = din("adaw", [2, 72, 128, 8 * 128])
    adab = din("adab", [2, 128, 72])
    normw = din("normw", [2, 128, 24])
    wg = din("wg", [2, 2, NFF, 128, 8 * 128])
    wu = din("wu", [2, 2, NFF, 128, 8 * 128])
    wd = din("wd", [2, 2, 128, NFF * 1024])
    win = din("win", [2, 128, 8 * 1664])
    winv = din("winv", [2, 128, 8 * 128])
    wout = din("wout", [2, 128, 8 * 1024])
    poolw = din("poolw", [2, 2, 128, 128])
    poolsc = din("poolsc", [2, 128, 2])
    qkn = din("qkn", [2, 128, 2])
    sinkb = din("sinkb", [2, 2, 64, 4])
    ckT = din("ckT", [2, 128, 256])
    cv = din("cv", [2, 256, 128])
    shw = din("shw", [2, 128, 6, 4])
    fw1 = din("fw1", [2, 33, 64])
    fw2 = din("fw2", [2, 64, 64])
    fw3 = din("fw3", [2, 64, 1024])
    fsm = din("fsm", [2, 64, 8])
    fb3 = din("fb3", [2, 128, 8])
    fdec = din("fdec", [2, 128, 4])
    hyb = din("hyb", [2, 128, 4])
    c_zT_s = din("c_zT_s", [33, TS])
    c_zT_p = din("c_zT_p", [33, 256])
    c_tl_s = din("c_tl_s", [128, TS])
    c_tl_p = din("c_tl_p", [128, 256])
    c_cos = din("c_cos", [128, TS])
    c_sin = din("c_sin", [128, TS])
    c_pm = din("c_pm", [128, 128])
    c_bd = din("c_bd", [128, 128])
    c_id = din("c_id", [128, 128])
    c_mask = din("c_mask", [2, 128, 128])
    c_invc_s = din("c_invc_s", [2, 128, TS])
    c_invc_p = din("c_invc_p", [2, 128, 256])

    yT = dout("yT", [128, 8, T])
    okT = dout("okT", [2, 128, TPR])
    ov = dout("ov", [2, TPR, 128])

    qT_s = dscr("qT_s", [4, 128, T], BF16)
    kT_s = dscr("kT_s", [128, T], BF16)
    v_s = dscr("v_s", [T, 128], BF16)
    up_s = dscr("up_s", [2, 128, T], F32)
    uh_s = dscr("uh_s", [6, 128, T], F32)
    mix_s = dscr("mix_s", [8, 128, T], BF16)
    G_s = dscr("G_s", [2, 256, 2 * TS - 1], BF16)
    G_p = dscr("G_p", [2, 256, 2 * 256 - 1], BF16)

    psb = [nc.alloc_psum_tensor("ps%d" % i, [128, 512], F32) for i in range(8)]
    PSB = [Buf("ps%d" % i) for i in range(8)]

    def A(t):
        return t.ap() if hasattr(t, "ap") and callable(getattr(t, "ap")) else t

    def sbp(name, shape, dt):
        return nc.alloc_sbuf_tensor(name, list(shape), dt)

    ones_bf = sbp("ones_bf", [128, 128], BF16)
    ones64 = sbp("ones64", [128, 64], BF16)
    bd_bf = sbp("bd_bf", [128, 128], BF16)
    pm_bf = sbp("pm_bf", [128, 128], BF16)
    id_f = sbp("id_f", [128, 128], F32)
    mask_f = sbp("mask_f", [128, 2, 128], F32)
    scT = sbp("scT", [128, 8, 2], F32)
    mod = sbp("mod", [128, 72, 2], F32)
    nrm = sbp("nrm", [128, 24], F32)
    Aab = sbp("Aab", [128, 3, 8, 2], F32)
    Gg = sbp("Gg", [128, 3, 8, 2], F32)
    adab_sb = sbp("adab_sb", [128, 72], F32)
    B_const = Buf("const")
    B_mod = Buf("mod")

    kb.op(DVE, lambda e: e.memset(ones_bf[:, :], 1.0 / 1024.0), writes=[B_const])
    kb.op(DVE, lambda e: e.memset(ones64[:, :], 1.0), writes=[B_const])
    kb.dma(POOL, bd_bf[:, :], A(c_bd)[:, :], writes=[B_const], sembuf=B_const)
    kb.dma(POOL, pm_bf[:, :], A(c_pm)[:, :], writes=[B_const], sembuf=B_const)
    kb.dma(SP, id_f[:, :], A(c_id)[:, :], writes=[B_const], sembuf=B_const)
    kb.dma(SP, mask_f[:, 0, :], A(c_mask)[0], writes=[B_const], sembuf=B_const)
    kb.dma(SP, mask_f[:, 1, :], A(c_mask)[1], writes=[B_const], sembuf=B_const)
    kb.dma(SP, scT[:, :, :], A(condT)[:, :, :], writes=[B_const], sembuf=B_const)
    kb.op(ACT, lambda e: e.activation(out=scT[:, :, :], in_=scT[:, :, :], func=AF.Silu),
          reads=[], writes=[B_const])

    _ctr = [0]

    def stage_sb(es, name, shape, dt):
        _ctr[0] += 1
        return es.enter_context(nc.sbuf_tensor("%s_%d" % (name, _ctr[0]), list(shape), dt))

    def mod_stage(l, pump=None):
        with ExitStack() as es:
            ring = [stage_sb(es, "adar%d" % i, [128, 8, 128], F32) for i in range(3)]
            RB = [Buf("adar%d" % i) for i in range(3)]
            kb.dma(SP, adab_sb[:, :], A(adab)[l], writes=[B_mod], sembuf=B_mod)
            kb.dma(SP, nrm[:, :], A(normw)[l], writes=[B_mod], sembuf=B_mod)
            for fc in range(72):
                if pump is not None:
                    pump()
                s = fc % 3
                kb.dma(SP, ring[s][:, :, :], A(adaw)[l, fc].rearrange("p (k j) -> p k j", j=128),
                       writes=[RB[s]], sembuf=RB[s])
                bank = 6 + (fc % 2)
                for kc in range(8):
                    kb.op(PE, lambda e, kc=kc, s=s, bank=bank: e.matmul(
                        psb[bank][:, 0:2], lhsT=ring[s][:, kc, :], rhs=scT[:, kc, :],
                        start=(kc == 0), stop=(kc == 7)),
                        reads=[RB[s], B_const], writes=[PSB[bank]], sig=(kc == 7))
                kb.op(DVE, lambda e, fc=fc, bank=bank: e.tensor_scalar(
                    out=mod[:, fc, :], in0=psb[bank][:, 0:2], scalar1=adab_sb[:, fc:fc + 1],
                    scalar2=None, op0=ALU.add), reads=[PSB[bank], B_mod], writes=[B_mod])
            for i in range(3):
                for ci in range(2):
                    kb.op(DVE, lambda e, i=i, ci=ci: e.scalar_tensor_tensor(
                        out=Aab[:, i, :, ci], in0=mod[:, (3 * i + 1) * 8:(3 * i + 2) * 8, ci], scalar=1.0,
                        in1=nrm[:, i * 8:(i + 1) * 8], op0=ALU.add, op1=ALU.mult),
                        reads=[B_mod], writes=[B_mod])
                    gs = 0.5 if i != 1 else 1.0
                    kb.op(DVE, lambda e, i=i, ci=ci, gs=gs: e.tensor_scalar(
                        out=Gg[:, i, :, ci], in0=mod[:, (3 * i + 2) * 8:(3 * i + 3) * 8, ci], scalar1=gs,
                        scalar2=None, op0=ALU.mult), reads=[B_mod], writes=[B_mod])
            kb.barrier()
            kb.release(RB)

    def norm_mod(xap, hap, i, ci, sq, rstd, tmpn, Bx, Bh, Bsq, Brs, Btmp, bank):
        for kc in range(8):
            kb.op(ACT, lambda e, kc=kc: e.activation(out=sq[:, kc, :], in_=xap(kc), func=AF.Square),
                  reads=[Bx[kc]], writes=[Bsq])
        for kc in range(8):
            kb.op(PE, lambda e, kc=kc: e.matmul(psb[bank][:, :], lhsT=ones_bf[:, :], rhs=sq[:, kc, :],
                                               start=(kc == 0), stop=(kc == 7)),
                  reads=[Bsq, B_const], writes=[PSB[bank]], sig=(kc == 7))
        kb.op(ACT, lambda e: e.activation(out=rstd[:, :], in_=psb[bank][:, :], func=AF.Sqrt, bias=EPS_AP[:, 0:1]),
              reads=[PSB[bank]], writes=[Brs])
        kb.op(DVE, lambda e: e.reciprocal(out=rstd[:, :], in_=rstd[:, :]), reads=[Brs], writes=[Brs])
        for kc in range(8):
            tb = kc % 2
            kb.op(DVE, lambda e, kc=kc, tb=tb: e.scalar_tensor_tensor(
                out=tmpn[tb][:, :], in0=xap(kc), scalar=Aab[:, i, kc, ci:ci + 1], in1=rstd[:, :],
                op0=ALU.mult, op1=ALU.mult), reads=[Bx[kc], Brs, B_mod], writes=[Btmp[tb]])
            kb.op(ACT, lambda e, kc=kc, tb=tb: e.activation(
                out=hap(kc), in_=tmpn[tb][:, :], func=AF.Identity,
                bias=mod[:, (3 * i) * 8 + kc, ci:ci + 1]), reads=[Btmp[tb], B_mod], writes=[Bh])

    eps_t = sbp("eps_t", [128, 1], F32)
    EPS_AP = eps_t
    kb.op(DVE, lambda e: e.memset(eps_t[:, :], EPS), writes=[B_const])

    def ffn_stage(l, f, src):
        i = 0 if f == 0 else 2
        NT = (T + 1023) // 1024

        def nhalf(tile):
            return min(2, (T - tile * 1024) // 512)
        with ExitStack() as es:
            xts = [stage_sb(es, "f_xt%d" % k, [128, 8, 1024], F32) for k in range(2)]
            hT = stage_sb(es, "f_hT", [128, 8, 1024], BF16)
            aT = stage_sb(es, "f_aT", [128, NFF, 1024], BF16)
            wds = stage_sb(es, "f_wd", [128, NFF, 1024], BF16)
            ring = [stage_sb(es, "f_r%d" % k, [128, 2, 8, 128], BF16) for k in range(3)]
            sq = stage_sb(es, "f_sq", [128, 8, 512], BF16)
            rstd = stage_sb(es, "f_rstd", [128, 512], F32)
            tmpn = [stage_sb(es, "f_tmp%d" % k, [128, 512], F32) for k in range(2)]
            sg = [stage_sb(es, "f_sg%d" % k, [128, 512], F32) for k in range(2)]
            Bxs = [[Buf("x%d_%d" % (b_, k)) for k in range(8)] for b_ in range(2)]
            BhT = [Buf("hT0"), Buf("hT1")]
            BaT = [[Buf() for _ in range(2)] for _ in range(NFF)]
            Bwd = [Buf("wd%d" % k) for k in range(NFF)]
            RB = [Buf("r%d" % k) for k in range(3)]
            Bsq, Brs = Buf("sq"), Buf("rs")
            Btmp = [Buf(), Buf()]
            Bsg = [Buf(), Buf()]

            def load_x(tile):
                xt, Bx = xts[tile % 2], Bxs[tile % 2]
                t0 = tile * 1024
                w_ = nhalf(tile) * 512
                for kc in range(8):
                    kb.dma(SP, xt[:, kc, 0:w_], A(src)[:, kc, t0:t0 + w_], writes=[Bx[kc]], sembuf=Bx[kc])

            def norm_tile(tile):
                xt, Bx = xts[tile % 2], Bxs[tile % 2]
                ci = 0 if tile < TS // 1024 else 1
                for h in range(nhalf(tile)):
                    norm_mod(lambda kc, h=h: xt[:, kc, h * 512:(h + 1) * 512],
                             lambda kc, h=h: hT[:, kc, h * 512:(h + 1) * 512],
                             i, ci, sq, rstd, tmpn, Bx, BhT[h], Bsq, Brs, Btmp, 7)

            load_x(0)
            norm_tile(0)
            rcount = 0
            for tile in range(NT):
                xt, Bx = xts[tile % 2], Bxs[tile % 2]
                ci = 0 if tile < TS // 1024 else 1
                t0 = tile * 1024
                if tile + 1 < NT:
                    load_x(tile + 1)
                for c in range(NFF):
                    s = rcount % 3
                    rcount += 1
                    kb.dma(POOL, ring[s][:, 0, :, :], A(wg)[l, f, c].rearrange("p (k j) -> p k j", j=128),
                           writes=[RB[s]], sembuf=RB[s])
                    kb.dma(POOL, ring[s][:, 1, :, :], A(wu)[l, f, c].rearrange("p (k j) -> p k j", j=128),
                           reads=[], writes=[], sembuf=RB[s])
                    RB[s].w = kb.lasttok(RB[s], True)
                    if tile == 0:
                        kb.dma(POOL, wds[:, c, :], A(wd)[l, f][:, c * 1024:(c + 1) * 1024], writes=[Bwd[c]], sembuf=Bwd[c])
                    for h in range(nhalf(tile)):
                        pb = 2 * ((2 * c + h) % 2)
                        for gu in range(2):
                            for kc in range(8):
                                kb.op(PE, lambda e, gu=gu, kc=kc, s=s, h=h, pb=pb: e.matmul(
                                    psb[pb + gu][:, :], lhsT=ring[s][:, gu, kc, :],
                                    rhs=hT[:, kc, h * 512:(h + 1) * 512], start=(kc == 0), stop=(kc == 7)),
                                    reads=[RB[s], BhT[h]], writes=[PSB[pb + gu]], sig=(kc == 7))
                        k2 = (2 * c + h) % 2
                        kb.op(ACT, lambda e, pb=pb, k2=k2: e.activation(out=sg[k2][:, :], in_=psb[pb][:, :], func=AF.Silu),
                              reads=[PSB[pb]], writes=[Bsg[k2]])
                        kb.op(DVE, lambda e, pb=pb, k2=k2, c=c, h=h: e.tensor_tensor(
                            out=aT[:, c, h * 512:(h + 1) * 512], in0=sg[k2][:, :], in1=psb[pb + 1][:, :], op=ALU.mult),
                            reads=[Bsg[k2], PSB[pb + 1]], writes=[BaT[c][h]])
                if tile + 1 < NT:
                    norm_tile(tile + 1)
                dcount = 0
                for oc in range(8):
                    for h in range(nhalf(tile)):
                        pb = 4 + (dcount % 3)
                        dcount += 1
                        for c in range(NFF):
                            kb.op(PE, lambda e, c=c, oc=oc, h=h, pb=pb: e.matmul(
                                psb[pb][:, :], lhsT=wds[:, c, oc * 128:(oc + 1) * 128],
                                rhs=aT[:, c, h * 512:(h + 1) * 512], start=(c == 0), stop=(c == NFF - 1)),
                                reads=[Bwd[c], BaT[c][h]], writes=[PSB[pb]], sig=(c == NFF - 1))
                        kb.op(DVE, lambda e, oc=oc, h=h, pb=pb, ci=ci, xt=xt: e.scalar_tensor_tensor(
                            out=xt[:, oc, h * 512:(h + 1) * 512], in0=psb[pb][:, :], scalar=Gg[:, i, oc, ci:ci + 1],
                            in1=xt[:, oc, h * 512:(h + 1) * 512], op0=ALU.mult, op1=ALU.add),
                            reads=[PSB[pb], B_mod], writes=[Bx[oc]])
                w_ = nhalf(tile) * 512
                for kc in range(8):
                    kb.dma(SP, A(yT)[:, kc, t0:t0 + w_], xt[:, kc, 0:w_], reads=[Bx[kc]], sembuf=Bx[kc])
            kb.barrier()
            kb.release(Bxs[0] + Bxs[1] + Bwd + RB)

    def mixa_stage(l):
        with ExitStack() as es:
            xts_ = [stage_sb(es, "a_xt%d" % k, [128, 8, 512], F32) for k in range(2)]
            hTs_ = [stage_sb(es, "a_hT%d" % k, [128, 8, 512], BF16) for k in range(2)]
            w_sb = stage_sb(es, "a_w", [128, 8, 1664], BF16)
            wv_sb = stage_sb(es, "a_wv", [128, 8, 128], BF16)
            sq = stage_sb(es, "a_sq", [128, 8, 512], BF16)
            rstd = stage_sb(es, "a_rstd", [128, 512], F32)
            tmpn = [stage_sb(es, "a_tmp%d" % k, [128, 512], F32) for k in range(2)]
            cos_sb = stage_sb(es, "a_cos", [128, TS], F32)
            sin_sb = stage_sb(es, "a_sin", [128, TS], F32)
            qkn_sb = stage_sb(es, "a_qkn", [128, 2], F32)
            sqh_ = [stage_sb(es, "a_sqh%d" % k, [128, 512], BF16) for k in range(2)]
            rs_ = [stage_sb(es, "a_rs%d" % k, [128, 512], F32) for k in range(2)]
            qn_ = [stage_sb(es, "a_qn%d" % k, [128, 512], F32) for k in range(2)]
            qnb_ = [stage_sb(es, "a_qnb%d" % k, [128, 512], BF16) for k in range(2)]
            t1_ = [stage_sb(es, "a_t1%d" % k, [128, 512], F32) for k in range(2)]
            t2_ = [stage_sb(es, "a_t2%d" % k, [128, 512], F32) for k in range(2)]
            ob = [stage_sb(es, "a_ob%d" % k, [128, 512], BF16) for k in range(2)]
            of = [stage_sb(es, "a_of%d" % k, [128, 512], F32) for k in range(2)]
            vb = [stage_sb(es, "a_vb%d" % k, [128, 128], BF16) for k in range(2)]
            vf = [stage_sb(es, "a_vf%d" % k, [128, 128], F32) for k in range(2)]
            Bxs_ = [[Buf() for _ in range(8)] for _ in range(2)]
            Bhs_ = [Buf(), Buf()]
            Bsq, Brs, Bw, Bc = Buf(), Buf(), Buf(), Buf()
            Btmp = [Buf(), Buf()]
            Bsqh_, Brs2_, Bqn_, Bqnb_, Bt1_, Bt2_ = [[Buf(), Buf()] for _ in range(6)]
            chain = [0]
            Bob = [Buf(), Buf()]
            Bof = [Buf(), Buf()]
            Bvb = [Buf(), Buf()]
            Bvf = [Buf(), Buf()]
            for kc in range(8):
                kb.dma(POOL, w_sb[:, kc, :], A(win)[l][:, kc * 1664:(kc + 1) * 1664], writes=([Bw] if kc == 0 else []), sembuf=Bw)
            kb.dma(POOL, wv_sb[:, :, :], A(winv)[l].rearrange("p (k j) -> p k j", j=128), writes=[], sembuf=Bw)
            Bw.w = kb.lasttok(Bw, True)
            kb.dma2(SP, cos_sb, A(c_cos), TS, writes=[Bc], sembuf=Bc)
            kb.dma2(SP, sin_sb, A(c_sin), TS, writes=[Bc], sembuf=Bc)
            kb.dma(SP, qkn_sb[:, :], A(qkn)[l], writes=[Bc], sembuf=Bc)
            oc_ = 0
            vc_ = 0
            NBK = T // 512

            def a_load(blk):
                xt_, Bx_ = xts_[blk % 2], Bxs_[blk % 2]
                for kc in range(8):
                    kb.dma(SP, xt_[:, kc, :], A(yT)[:, kc, blk * 512:(blk + 1) * 512], writes=[Bx_[kc]], sembuf=Bx_[kc])

            def a_norm(blk):
                xt_, Bx_, hT_, Bh_ = xts_[blk % 2], Bxs_[blk % 2], hTs_[blk % 2], Bhs_[blk % 2]
                norm_mod(lambda kc: xt_[:, kc, :], lambda kc: hT_[:, kc, :], 1, (0 if blk * 512 < TS else 1), sq, rstd, tmpn,
                         Bx_, Bh_, Bsq, Brs, Btmp, 7)

            a_load(0)
            a_load(1)
            a_norm(0)
            for blk in range(NBK):
                t0 = blk * 512
                isS = t0 < TS
                ci = 0 if isS else 1
                hT, Bh = hTs_[blk % 2], Bhs_[blk % 2]
                for ch in range(13):
                    pb = ch % 4
                    for kc in range(8):
                        kb.op(PE, lambda e, ch=ch, kc=kc, pb=pb: e.matmul(
                            psb[pb][:, :], lhsT=w_sb[:, kc, ch * 128:(ch + 1) * 128], rhs=hT[:, kc, :],
                            start=(kc == 0), stop=(kc == 7)), reads=[Bw, Bh], writes=[PSB[pb]], sig=(kc == 7))
                    if ch < 2 or ch >= 7:
                        k2 = oc_ % 2
                        oc_ += 1
                        kb.op(ACT, lambda e, pb=pb, k2=k2: e.activation(out=of[k2][:, :], in_=psb[pb][:, :], func=AF.Identity),
                              reads=[PSB[pb]], writes=[Bof[k2]])
                        dst = A(up_s)[ch, :, t0:t0 + 512] if ch < 2 else A(uh_s)[ch - 7, :, t0:t0 + 512]
                        kb.dma(SP, dst, of[k2][:, :], reads=[Bof[k2]], sembuf=Bof[k2])
                        continue
                    cp = chain[0] % 2
                    chain[0] += 1
                    sqh, rs, qn, qnb, t1, t2 = sqh_[cp], rs_[cp], qn_[cp], qnb_[cp], t1_[cp], t2_[cp]
                    Bsqh, Brs2, Bqn, Bqnb, Bt1, Bt2 = Bsqh_[cp], Brs2_[cp], Bqn_[cp], Bqnb_[cp], Bt1_[cp], Bt2_[cp]
                    bk = 4 + cp
                    isq = ch < 6
                    gcol = 0 if isq else 1
                    kb.op(ACT, lambda e, pb=pb, sqh=sqh: e.activation(out=sqh[:, :], in_=psb[pb][:, :], func=AF.Square),
                          reads=[PSB[pb]], writes=[Bsqh])
                    kb.op(PE, lambda e, bk=bk, sqh=sqh: e.matmul(psb[bk][:, :], lhsT=bd_bf[:, :], rhs=sqh[:, :], start=True, stop=True),
                          reads=[Bsqh, B_const], writes=[PSB[bk]])
                    kb.op(ACT, lambda e, bk=bk, rs=rs: e.activation(out=rs[:, :], in_=psb[bk][:, :], func=AF.Sqrt, bias=EPS_AP[:, 0:1]),
                          reads=[PSB[bk]], writes=[Brs2])
                    kb.op(DVE, lambda e, rs=rs: e.reciprocal(out=rs[:, :], in_=rs[:, :]), reads=[Brs2], writes=[Brs2])
                    kb.op(DVE, lambda e, pb=pb, gcol=gcol, qn=qn, rs=rs: e.scalar_tensor_tensor(
                        out=qn[:, :], in0=psb[pb][:, :], scalar=qkn_sb[:, gcol:gcol + 1], in1=rs[:, :],
                        op0=ALU.mult, op1=ALU.mult), reads=[PSB[pb], Brs2, Bc], writes=[Bqn])
                    k2 = oc_ % 2
                    oc_ += 1
                    if isS:
                        kb.op(ACT, lambda e, qnb=qnb, qn=qn: e.activation(out=qnb[:, :], in_=qn[:, :], func=AF.Identity),
                              reads=[Bqn], writes=[Bqnb])
                        kb.op(PE, lambda e, bk=bk, qnb=qnb: e.matmul(psb[bk][:, :], lhsT=pm_bf[:, :], rhs=qnb[:, :], start=True, stop=True),
                              reads=[Bqnb, B_const], writes=[PSB[bk]])
                        kb.op(DVE, lambda e, t0=t0, t1=t1, qn=qn: e.tensor_tensor(out=t1[:, :], in0=qn[:, :], in1=cos_sb[:, t0:t0 + 512], op=ALU.mult),
                              reads=[Bqn, Bc], writes=[Bt1])
                        kb.op(DVE, lambda e, t0=t0, t2=t2, bk=bk: e.tensor_tensor(out=t2[:, :], in0=psb[bk][:, :], in1=sin_sb[:, t0:t0 + 512], op=ALU.mult),
                              reads=[PSB[bk], Bc], writes=[Bt2])
                        kb.op(DVE, lambda e, k2=k2, t1=t1, t2=t2: e.tensor_tensor(out=ob[k2][:, :], in0=t1[:, :], in1=t2[:, :], op=ALU.add),
                              reads=[Bt1, Bt2], writes=[Bob[k2]])
                    else:
                        kb.op(ACT, lambda e, k2=k2, qn=qn: e.activation(out=ob[k2][:, :], in_=qn[:, :], func=AF.Identity),
                              reads=[Bqn], writes=[Bob[k2]])
                        if not isq:
                            k3 = oc_ % 2
                            oc_ += 1
                            kb.op(ACT, lambda e, k3=k3, qn=qn: e.activation(out=of[k3][:, :], in_=qn[:, :], func=AF.Identity),
                                  reads=[Bqn], writes=[Bof[k3]])
                            kb.dma(SP, A(okT)[l, :, t0 - TS:t0 - TS + 512], of[k3][:, :], reads=[Bof[k3]], sembuf=Bof[k3])
                    dst = A(qT_s)[ch - 2, :, t0:t0 + 512] if isq else A(kT_s)[:, t0:t0 + 512]
                    kb.dma(SP, dst, ob[k2][:, :], reads=[Bob[k2]], sembuf=Bob[k2])
                for tb in range(4):
                    for kc in range(8):
                        kb.op(PE, lambda e, tb=tb, kc=kc: e.matmul(
                            psb[6][:, 0:128], lhsT=hT[:, kc, tb * 128:(tb + 1) * 128], rhs=wv_sb[:, kc, :],
                            start=(kc == 0), stop=(kc == 7)), reads=[Bw, Bh], writes=[PSB[6]], sig=(kc == 7))
                    k2 = vc_ % 2
                    vc_ += 1
                    if not isS:
                        kb.op(ACT, lambda e, k2=k2: e.activation(out=vf[k2][:, :], in_=psb[6][:, 0:128], func=AF.Identity),
                              reads=[PSB[6]], writes=[Bvf[k2]])
                        r0 = t0 - TS + tb * 128
                        kb.dma(SP, A(ov)[l, r0:r0 + 128, :], vf[k2][:, :], reads=[Bvf[k2]], sembuf=Bvf[k2])
                    if not isS:
                        kb.op(DVE, lambda e, k2=k2: e.tensor_copy(out=vb[k2][:, :], in_=vf[k2][:, :]),
                              reads=[Bvf[k2]], writes=[Bvb[k2]])
                    else:
                        kb.op(DVE, lambda e, k2=k2: e.tensor_copy(out=vb[k2][:, :], in_=psb[6][:, 0:128]),
                              reads=[PSB[6]], writes=[Bvb[k2]])
                    r0 = t0 + tb * 128
                    kb.dma(SP, A(v_s)[r0:r0 + 128, :], vb[k2][:, :], reads=[Bvb[k2]], sembuf=Bvb[k2])
                if blk + 1 < NBK:
                    a_norm(blk + 1)
                if blk + 2 < NBK:
                    a_load(blk + 2)
            kb.barrier()
            kb.release(Bxs_[0] + Bxs_[1] + [Bw, Bc] + Bob + Bof + Bvb + Bvf)

    def attn_gen(l, es, Q, relbufs):
        if True:
            q_sb = [stage_sb(es, "b_q%d" % k, [128, 4, 128], BF16) for k in range(2)]
            k_sb = [stage_sb(es, "b_k%d" % k, [128, 3, 128], BF16) for k in range(2)]
            v_sb = [stage_sb(es, "b_v%d" % k, [128, 3, 128], BF16) for k in range(2)]
            ck_sb = stage_sb(es, "b_ck", [128, 256], BF16)
            cv_sb = stage_sb(es, "b_cv", [128, 2, 128], BF16)
            sk = stage_sb(es, "b_sk", [64, 2, 4], F32)
            pT = [stage_sb(es, "b_pT%d" % k, [128, 512], BF16) for k in range(3)]
            d2 = stage_sb(es, "b_d2", [64, 512], F32)
            o_sb = [stage_sb(es, "b_o%d" % k, [64, 512], BF16) for k in range(2)]
            Bq = [Buf(), Buf()]
            Bk = [Buf(), Buf()]
            Bv = [Buf(), Buf()]
            Bck, Bsk, Bd2 = Buf(), Buf(), Buf()
            BpT = [Buf(), Buf(), Buf()]
            Bo = [Buf(), Buf()]
            relbufs.extend(Bq + Bk + Bv + [Bck, Bsk] + Bo)
            kb.dma(POOL, ck_sb[:, :], A(ckT)[l], writes=[Bck], sembuf=Bck)
            kb.dma(POOL, cv_sb[:, :, :], A(cv)[l].rearrange("(b p) d -> p b d", p=128), writes=[Bck], sembuf=Bck)
            for kh in range(2):
                kb.dma(Q, sk[:, kh, :], A(sinkb)[l, kh], writes=[Bsk], sembuf=Bsk)
            kb.op(ACT, lambda e: e.activation(out=sk[:, :, :], in_=sk[:, :, :], func=AF.Exp), reads=[], writes=[Bsk])
            pcount = 0
            ocount = 0
            for qb in range(NBLK):
                isS = qb < TS // 128
                s = qb % 2
                t0 = qb * 128
                if isS:
                    kbl = [b for b in (qb - 1, qb, qb + 1) if 0 <= b < TS // 128]
                else:
                    sq0 = (qb - TS // 128) // 2 * 2 + TS // 128
                    kbl = [sq0, sq0 + 1]
                kb.dma(Q, q_sb[s][:, :, :], A(qT_s)[:, :, t0:t0 + 128].rearrange("c p t -> p c t"),
                       writes=[Bq[s]], sembuf=Bq[s])
                k0 = kbl[0] * 128
                nk = len(kbl)
                kb.dma(Q, k_sb[s][:, 0:nk, :], A(kT_s)[:, k0:k0 + nk * 128].rearrange("p (b t) -> p b t", t=128),
                       writes=[Bk[s]], sembuf=Bk[s])
                kb.dma(Q, v_sb[s][:, 0:nk, :], A(v_s)[k0:k0 + nk * 128, :].rearrange("(b p) d -> p b d", p=128),
                       writes=[Bv[s]], sembuf=Bv[s])
                for kh in range(2):
                    pr = slice(64 * kh, 64 * kh + 64)
                    keys = [("l", j, kbl[j]) for j in range(nk)]
                    if isS:
                        keys += [("c", 0, 0), ("c", 1, 0)]
                    nkeys = len(keys)
                    for idx, (kind, j, gb) in enumerate(keys):
                        if kind == "l":
                            lk = k_sb[s][pr, j, :]
                            lv = v_sb[s][:, j, 64 * kh:64 * kh + 64]
                            rk = [Bk[s]]
                            rv = [Bv[s]]
                        else:
                            lk = ck_sb[pr, j * 128:(j + 1) * 128]
                            lv = cv_sb[:, j, 64 * kh:64 * kh + 64]
                            rk = [Bck]
                            rv = [Bck]
                        sb_ = 3 + idx % 2
                        kb.op(PE, lambda e, lk=lk, s=s, pr=pr, sb_=sb_: e.matmul(
                            psb[sb_][:, :], lhsT=lk, rhs=q_sb[s][pr, :, :], start=True, stop=True),
                            reads=rk + [Bq[s]], writes=[PSB[sb_]])
                        pi = pcount % 3
                        pcount += 1
                        kb.op(ACT, lambda e, pi=pi, sb_=sb_: e.activation(out=pT[pi][:, :], in_=psb[sb_][:, :], func=AF.Exp, scale=0.125),
                              reads=[PSB[sb_]], writes=[BpT[pi]])
                        if kind == "l" and isS and gb != qb:
                            mi = 0 if gb < qb else 1
                            mh = mask_f[:, mi, :]
                            map_ = bass.AP(mask_f, mh.offset, [list(mh.ap[0]), [0, 4], [1, 128]])
                            kb.op(DVE, lambda e, pi=pi, map_=map_: e.tensor_tensor(
                                out=pT[pi][:, :].rearrange("p (c t) -> p c t", c=4),
                                in0=pT[pi][:, :].rearrange("p (c t) -> p c t", c=4), in1=map_, op=ALU.mult),
                                reads=[B_const], writes=[BpT[pi]])
                        kb.op(PE, lambda e, lv=lv, pi=pi, idx=idx, nkeys=nkeys: e.matmul(
                            psb[5][0:64, :], lhsT=lv, rhs=pT[pi][:, :], start=(idx == 0), stop=(idx == nkeys - 1)),
                            reads=rv + [BpT[pi]], writes=[PSB[5]], sig=False)
                        kb.op(PE, lambda e, pi=pi, idx=idx, nkeys=nkeys: e.matmul(
                            psb[6][0:64, :], lhsT=ones64[:, :], rhs=pT[pi][:, :], start=(idx == 0), stop=(idx == nkeys - 1)),
                            reads=[BpT[pi], B_const], writes=[PSB[6]], sig=True)
                    skb = sk[:, kh, :]
                    skap = bass.AP(sk, skb.offset, [list(skb.ap[0]), [1, 4], [0, 128]])
                    kb.op(DVE, lambda e, skap=skap: e.tensor_tensor(
                        out=d2[:, :].rearrange("p (c t) -> p c t", c=4),
                        in0=psb[6][0:64, :].rearrange("p (c t) -> p c t", c=4), in1=skap, op=ALU.add),
                        reads=[PSB[6], Bsk], writes=[Bd2])
                    kb.op(DVE, lambda e: e.reciprocal(out=d2[:, :], in_=d2[:, :]), reads=[Bd2], writes=[Bd2])
                    oi = ocount % 2
                    ocount += 1
                    kb.op(DVE, lambda e, oi=oi: e.tensor_tensor(out=o_sb[oi][:, :], in0=psb[5][0:64, :], in1=d2[:, :], op=ALU.mult),
                          reads=[PSB[5], Bd2], writes=[Bo[oi]])
                    dst = A(mix_s)[2:6, 64 * kh:64 * kh + 64, t0:t0 + 128].rearrange("c p t -> p c t")
                    kb.dma(Q, dst, o_sb[oi][:, :].rearrange("p (c t) -> p c t", c=4), reads=[Bo[oi]], sembuf=Bo[oi])
                    yield

    def attn_stage(l):
        with ExitStack() as es:
            rel = []
            for _ in attn_gen(l, es, SP, rel):
                pass
            kb.barrier()
            kb.release(rel)

    def pool_stage(l):
        for (nseq, L, tbase, invc_d) in ((1, TS, 0, c_invc_s), (NPS, 256, TS, c_invc_p)):
            with ExitStack() as es:
                LP = L + 16
                u = stage_sb(es, "c_u", [128, nseq, LP], F32)
                wa = stage_sb(es, "c_wa", [128, nseq, LP], F32)
                wb_ = stage_sb(es, "c_wb", [128, nseq, LP], F32)
                sel = stage_sb(es, "c_sel", [128, nseq, L], F32)
                invc = stage_sb(es, "c_invc", [128, L], F32)
                dT = stage_sb(es, "c_dT", [128, nseq, L], BF16)
                pw = stage_sb(es, "c_pw", [128, 128], BF16)
                psc = stage_sb(es, "c_psc", [128, 2], F32)
                yo = [stage_sb(es, "c_yo%d" % k, [128, 512], BF16) for k in range(2)]
                Bu, Bwa, Bwb, Bsel, Binv, Bd, Bpw = Buf(), Buf(), Buf(), Buf(), Buf(), Buf(), Buf()
                Byo = [Buf(), Buf()]
                Bpsc = Buf()
                kb.dma(SP, psc[:, :], A(poolsc)[l], writes=[Bpsc], sembuf=Bpsc)
                yc = 0
                for ch in range(2):
                    kb.op(DVE, lambda e: e.memset(u[:, :, :], 0.0), writes=[Bu])
                    kb.op(DVE, lambda e: e.memset(wa[:, :, :], 0.0), writes=[Bwa])
                    kb.op(DVE, lambda e: e.memset(wb_[:, :, :], 0.0), writes=[Bwb])
                    for sq_ in range(nseq):
                        kb.dma2(SP, u[:, sq_, 8:8 + L], A(up_s)[ch, :, tbase + sq_ * L:tbase + (sq_ + 1) * L], L,
                                writes=[Bu], sembuf=Bu)
                    kb.dma2(SP, invc[:, :], A(invc_d)[ch], L, writes=[Binv], sembuf=Binv)
                    kb.dma(POOL, pw[:, :], A(poolw)[l, ch], writes=[Bpw], sembuf=Bpw)
                    kb.op(DVE, lambda e: e.tensor_tensor(out=wa[:, :, 1:LP], in0=u[:, :, 0:LP - 1], in1=u[:, :, 1:LP], op=ALU.add),
                          reads=[Bu], writes=[Bwa])
                    kb.op(DVE, lambda e: e.tensor_tensor(out=wb_[:, :, 2:LP - 1], in0=wa[:, :, 1:LP - 2], in1=wa[:, :, 3:LP], op=ALU.add),
                          reads=[Bwa], writes=[Bwb])
                    if ch == 0:
                        kb.op(DVE, lambda e: e.tensor_copy(out=sel[0:64, :, :], in_=wa[0:64, :, 8:8 + L]), reads=[Bwa], writes=[Bsel])
                        kb.op(DVE, lambda e: e.tensor_copy(out=sel[64:128, :, :], in_=wb_[64:128, :, 8:8 + L]), reads=[Bwb], writes=[Bsel])
                    else:
                        kb.op(DVE, lambda e: e.tensor_tensor(out=wa[:, :, 4:LP - 3], in0=wb_[:, :, 2:LP - 5], in1=wb_[:, :, 6:LP - 1], op=ALU.add),
                              reads=[Bwb], writes=[Bwa])
                        kb.op(DVE, lambda e: e.tensor_tensor(out=wb_[:, :, 8:LP - 7], in0=wa[:, :, 4:LP - 11], in1=wa[:, :, 12:LP - 3], op=ALU.add),
                              reads=[Bwa], writes=[Bwb])
                        kb.op(DVE, lambda e: e.tensor_copy(out=sel[0:64, :, :], in_=wa[0:64, :, 8:8 + L]), reads=[Bwa], writes=[Bsel])
                        kb.op(DVE, lambda e: e.tensor_copy(out=sel[64:128, :, :], in_=wb_[64:128, :, 8:8 + L]), reads=[Bwb], writes=[Bsel])
                    for sq_ in range(nseq):
                        kb.op(DVE, lambda e, sq_=sq_: e.tensor_tensor(out=sel[:, sq_, :], in0=sel[:, sq_, :], in1=invc[:, :], op=ALU.mult),
                              reads=[Binv], writes=[Bsel])
                    kb.op(DVE, lambda e: e.tensor_tensor(out=dT[:, :, :], in0=sel[:, :, :], in1=u[:, :, 8:8 + L], op=ALU.subtract),
                          reads=[Bsel, Bu], writes=[Bd])
                    dflat = dT[:, :, :].rearrange("p s t -> p (s t)")
                    for cb in range(nseq * L // 512):
                        pb = cb % 2
                        kb.op(PE, lambda e, cb=cb, pb=pb: e.matmul(psb[pb][:, :], lhsT=pw[:, :], rhs=dflat[:, cb * 512:(cb + 1) * 512],
                                                                 start=True, stop=True), reads=[Bpw, Bd], writes=[PSB[pb]])
                        k2 = yc % 2
                        yc += 1
                        kb.op(ACT, lambda e, pb=pb, k2=k2, ch=ch: e.activation(out=yo[k2][:, :], in_=psb[pb][:, :], func=AF.Identity,
                                                                             scale=psc[:, ch:ch + 1]),
                              reads=[PSB[pb], Bpsc], writes=[Byo[k2]])
                        kb.dma(SP, A(mix_s)[ch, :, tbase + cb * 512:tbase + (cb + 1) * 512], yo[k2][:, :], reads=[Byo[k2]], sembuf=Byo[k2])
                    kb.barrier()
                kb.release([Bu, Binv, Bpw, Bpsc] + Byo)

    def pool_gen(l, es, rel, Q, bank):
        CL = 1024
        LP = CL + 16
        u = stage_sb(es, "g_u", [128, LP], F32)
        wa = stage_sb(es, "g_wa", [128, LP], F32)
        wb_ = stage_sb(es, "g_wb", [128, LP], F32)
        sel = stage_sb(es, "g_sel", [128, CL], F32)
        invc = stage_sb(es, "g_invc", [128, CL], F32)
        dT = stage_sb(es, "g_dT", [128, CL], BF16)
        pw = [stage_sb(es, "g_pw%d" % k, [128, 128], BF16) for k in range(2)]
        psc = stage_sb(es, "g_psc", [128, 2], F32)
        yo = [stage_sb(es, "g_yo%d" % k, [128, 512], BF16) for k in range(2)]
        Bu, Bwa, Bwb, Bsel, Binv, Bd, Bpw, Bpsc = Buf(), Buf(), Buf(), Buf(), Buf(), Buf(), Buf(), Buf()
        Byo = [Buf(), Buf()]
        rel.extend([Bu, Binv, Bpw, Bpsc] + Byo)
        kb.dma(Q, psc[:, :], A(poolsc)[l], writes=[Bpsc], sembuf=Bpsc)
        kb.dma(Q, pw[0][:, :], A(poolw)[l, 0], writes=[Bpw], sembuf=Bpw)
        kb.dma(Q, pw[1][:, :], A(poolw)[l, 1], writes=[Bpw], sembuf=Bpw)
        yc = 0
        units = []
        for c0 in range(0, TS, CL):
            units.append((0, TS, c0, CL, c_invc_s))
        for sq_ in range(NPS):
            units.append((TS + sq_ * 256, 256, 0, 256, c_invc_p))
        for (sbase, L, c0, cl, invc_d) in units:
            lp = cl + 16
            for ch in range(2):
                kb.op(DVE, lambda e: e.memset(u[:, :], 0.0), writes=[Bu])
                kb.op(DVE, lambda e: e.memset(wa[:, :], 0.0), writes=[Bwa])
                kb.op(DVE, lambda e: e.memset(wb_[:, :], 0.0), writes=[Bwb])
                lo, hi = max(0, c0 - 8), min(L, c0 + cl + 8)
                kb.dma(Q, u[:, lo - (c0 - 8):hi - (c0 - 8)], A(up_s)[ch, :, sbase + lo:sbase + hi], writes=[Bu], sembuf=Bu)
                kb.dma(Q, invc[:, 0:cl], A(invc_d)[ch][:, c0:c0 + cl], writes=[Binv], sembuf=Binv)
                kb.op(DVE, lambda e: e.tensor_tensor(out=wa[:, 1:lp], in0=u[:, 0:lp - 1], in1=u[:, 1:lp], op=ALU.add), reads=[Bu], writes=[Bwa])
                kb.op(DVE, lambda e: e.tensor_tensor(out=wb_[:, 2:lp - 1], in0=wa[:, 1:lp - 2], in1=wa[:, 3:lp], op=ALU.add), reads=[Bwa], writes=[Bwb])
                if ch == 1:
                    kb.op(DVE, lambda e: e.tensor_tensor(out=wa[:, 4:lp - 3], in0=wb_[:, 2:lp - 5], in1=wb_[:, 6:lp - 1], op=ALU.add), reads=[Bwb], writes=[Bwa])
                    kb.op(DVE, lambda e: e.tensor_tensor(out=wb_[:, 8:lp - 7], in0=wa[:, 4:lp - 11], in1=wa[:, 12:lp - 3], op=ALU.add), reads=[Bwa], writes=[Bwb])
                kb.op(DVE, lambda e: e.tensor_copy(out=sel[0:64, 0:cl], in_=wa[0:64, 8:8 + cl]), reads=[Bwa], writes=[Bsel])
                kb.op(DVE, lambda e: e.tensor_copy(out=sel[64:128, 0:cl], in_=wb_[64:128, 8:8 + cl]), reads=[Bwb], writes=[Bsel])
                kb.op(DVE, lambda e: e.tensor_tensor(out=sel[:, 0:cl], in0=sel[:, 0:cl], in1=invc[:, 0:cl], op=ALU.mult), reads=[Binv], writes=[Bsel])
                kb.op(DVE, lambda e: e.tensor_tensor(out=dT[:, 0:cl], in0=sel[:, 0:cl], in1=u[:, 8:8 + cl], op=ALU.subtract), reads=[Bsel, Bu], writes=[Bd])
                W = min(cl, 512)
                for cb in range(cl // W):
                    kb.op(PE, lambda e, cb=cb, ch=ch: e.matmul(psb[bank][:, 0:W], lhsT=pw[ch][:, :], rhs=dT[:, cb * W:(cb + 1) * W], start=True, stop=True),
                          reads=[Bpw, Bd], writes=[PSB[bank]])
                    k2 = yc % 2
                    yc += 1
                    kb.op(ACT, lambda e, k2=k2, ch=ch: e.activation(out=yo[k2][:, 0:W], in_=psb[bank][:, 0:W], func=AF.Identity, scale=psc[:, ch:ch + 1]),
                          reads=[PSB[bank], Bpsc], writes=[Byo[k2]])
                    t_ = sbase + c0 + cb * W
                    kb.dma(Q, A(mix_s)[ch, :, t_:t_ + W], yo[k2][:, 0:W], reads=[Byo[k2]], sembuf=Byo[k2])
                yield

    def sin_act(ps_ap, out_ap, fr_ap, bb_ap, tmp, Btmp_, rd, wr, tmp2=None, Btmp2_=None):
        kb.op(DVE, lambda e: e.tensor_scalar(out=tmp, in0=ps_ap, scalar1=fr_ap, scalar2=bb_ap, op0=ALU.mult, op1=ALU.add),
              reads=rd, writes=[Btmp_])
        for _r in range(2):
            kb.op(DVE, lambda e: e.tensor_scalar(out=tmp2, in0=tmp, scalar1=math.pi, scalar2=-TWO_PI, op0=ALU.is_gt, op1=ALU.mult),
                  reads=[Btmp_], writes=[Btmp2_])
            kb.op(DVE, lambda e: e.tensor_tensor(out=tmp, in0=tmp, in1=tmp2, op=ALU.add), reads=[Btmp2_], writes=[Btmp_])
            kb.op(DVE, lambda e: e.tensor_scalar(out=tmp2, in0=tmp, scalar1=-math.pi, scalar2=TWO_PI, op0=ALU.is_lt, op1=ALU.mult),
                  reads=[Btmp_], writes=[Btmp2_])
            kb.op(DVE, lambda e: e.tensor_tensor(out=tmp, in0=tmp, in1=tmp2, op=ALU.add), reads=[Btmp2_], writes=[Btmp_])
        kb.op(DVE, lambda e: e.tensor_scalar(out=tmp, in0=tmp, scalar1=-3.1415925, scalar2=3.1415925, op0=ALU.max, op1=ALU.min),
              reads=[], writes=[Btmp_])
        kb.op(ACT, lambda e: e.activation(out=out_ap, in_=tmp, func=AF.Sin), reads=[Btmp_], writes=wr)

    def filt_gen(l, L, zT_d, tl_d, G_d, es, rel, tag):
        if True:
            zT = stage_sb(es, "d_zT" + tag, [33, L], F32)
            tl = stage_sb(es, "d_tl" + tag, [128, L], F32)
            w1 = stage_sb(es, "d_w1" + tag, [33, 64], F32)
            w2 = stage_sb(es, "d_w2" + tag, [64, 64], F32)
            w3 = stage_sb(es, "d_w3" + tag, [64, 1024], F32)
            sm = stage_sb(es, "d_sm" + tag, [64, 8], F32)
            b3 = stage_sb(es, "d_b3" + tag, [128, 8], F32)
            dec = stage_sb(es, "d_dec" + tag, [128, 4], F32)
            h1 = stage_sb(es, "d_h1" + tag, [64, L], F32)
            h2 = stage_sb(es, "d_h2" + tag, [64, L], F32)
            tmp = stage_sb(es, "d_tmp" + tag, [128, 512], F32)
            tmp2 = stage_sb(es, "d_tmp2" + tag, [128, 512], F32)
            Btmp2_ = Buf()
            dk = stage_sb(es, "d_dk" + tag, [128, 512], F32)
            fl = [stage_sb(es, "d_fl%d"  % k + tag, [128, L], F32) for k in range(2)]
            flb = [stage_sb(es, "d_flb%d"  % k + tag, [128, L], BF16) for k in range(2)]
            ssum = stage_sb(es, "d_ssum" + tag, [128, 4], F32)
            Bc, Bh1, Bh2, Btmp_, Bdk, Bss = Buf(), Buf(), Buf(), Buf(), Buf(), Buf()
            Bfl = [Buf(), Buf()]
            Bflb = [Buf(), Buf()]
            for (dst, srcap, nn) in ((zT[:, :], A(zT_d)[:, :], L), (tl[:, :], A(tl_d)[:, :], L), (w1[:, :], A(fw1)[l], 64),
                                     (w2[:, :], A(fw2)[l], 64), (w3[:, :], A(fw3)[l], 1024), (sm[:, :], A(fsm)[l], 8),
                                     (b3[:, :], A(fb3)[l], 8), (dec[:, :], A(fdec)[l], 4)):
                kb.dma2(SP, dst, srcap, nn, writes=[Bc], sembuf=Bc)
            for k in range(2):
                kb.op(DVE, lambda e, k=k: e.tensor_scalar(out=sm[:, 4 + k:5 + k], in0=sm[:, k:k + 1], scalar1=sm[:, 2 + k:3 + k],
                                                         scalar2=0.0, op0=ALU.mult, op1=ALU.add), reads=[Bc], writes=[Bc])
            kb.op(ACT, lambda e: e.activation(out=dec[:, :], in_=dec[:, :], func=AF.Abs), reads=[Bc], writes=[Bc])
            kb.op(DVE, lambda e: e.tensor_scalar(out=dec[:, :], in0=dec[:, :], scalar1=-1.0, scalar2=None, op0=ALU.mult),
                  reads=[Bc], writes=[Bc])
            nb5 = L // 512 if L >= 512 else 1
            W = min(L, 512)
            for cb in range(nb5):
                cs = slice(cb * W, (cb + 1) * W)
                kb.op(PE, lambda e, cs=cs: e.matmul(psb[0][0:64, 0:W], lhsT=w1[:, :], rhs=zT[:, cs], start=True, stop=True),
                      reads=[Bc], writes=[PSB[0]])
                sin_act(psb[0][0:64, 0:W], h1[:, cs], sm[:, 2:3], sm[:, 4:5], tmp[0:64, 0:W], Btmp_, [PSB[0], Bc], [Bh1], tmp2[0:64, 0:W], Btmp2_)
                yield
            for cb in range(nb5):
                cs = slice(cb * W, (cb + 1) * W)
                kb.op(PE, lambda e, cs=cs: e.matmul(psb[1][0:64, 0:W], lhsT=w2[:, :], rhs=h1[:, cs], start=True, stop=True),
                      reads=[Bc, Bh1], writes=[PSB[1]])
                sin_act(psb[1][0:64, 0:W], h2[:, cs], sm[:, 3:4], sm[:, 5:6], tmp[0:64, 0:W], Btmp_, [PSB[1], Bc], [Bh2], tmp2[0:64, 0:W], Btmp2_)
                yield
            for o in range(2):
                for half in range(2):
                    oh = o * 2 + half
                    for d in range(2):
                        fc = d * 4 + oh
                        for cb in range(nb5):
                            cs = slice(cb * W, (cb + 1) * W)
                            pb = 2 + cb % 2
                            kb.op(PE, lambda e, cs=cs, fc=fc, pb=pb: e.matmul(psb[pb][:, 0:W], lhsT=w3[:, fc * 128:(fc + 1) * 128],
                                                                          rhs=h2[:, cs], start=True, stop=True),
                                  reads=[Bc, Bh2], writes=[PSB[pb]])
                            kb.op(ACT, lambda e, cs=cs, oh=oh: e.activation(out=dk[:, 0:W], in_=tl[:, cs], func=AF.Exp, scale=dec[:, oh:oh + 1]),
                                  reads=[Bc], writes=[Bdk])
                            kb.op(DVE, lambda e: e.tensor_scalar(out=dk[:, 0:W], in0=dk[:, 0:W], scalar1=0.05, scalar2=None, op0=ALU.add),
                                  reads=[], writes=[Bdk])
                            kb.op(DVE, lambda e, cs=cs, fc=fc, pb=pb, d=d: e.scalar_tensor_tensor(
                                out=fl[d][:, cs], in0=psb[pb][:, 0:W], scalar=b3[:, fc:fc + 1], in1=dk[:, 0:W], op0=ALU.add, op1=ALU.mult),
                                reads=[PSB[pb], Bdk, Bc], writes=[Bfl[d]])
                            yield
                        kb.op(DVE, lambda e, d=d: e.tensor_reduce(out=ssum[:, d:d + 1], in_=fl[d][:, :], axis=AX.X, op=ALU.add,
                                                                 apply_absolute_value=True), reads=[Bfl[d]], writes=[Bss])
                    kb.op(ACT, lambda e: e.activation(out=ssum[:, 2:3], in_=fl[1][:, 0:1], func=AF.Abs), reads=[Bfl[1]], writes=[Bss])
                    kb.op(DVE, lambda e: e.tensor_scalar(out=ssum[:, 2:3], in0=ssum[:, 2:3], scalar1=-1.0, scalar2=None, op0=ALU.mult),
                          reads=[], writes=[Bss])
                    kb.op(DVE, lambda e: e.tensor_tensor(out=ssum[:, 3:4], in0=ssum[:, 0:1], in1=ssum[:, 1:2], op=ALU.add), reads=[], writes=[Bss])
                    kb.op(DVE, lambda e: e.scalar_tensor_tensor(out=ssum[:, 3:4], in0=ssum[:, 3:4], scalar=1e-6, in1=ssum[:, 2:3], op0=ALU.add, op1=ALU.add),
                          reads=[], writes=[Bss])
                    kb.op(DVE, lambda e: e.reciprocal(out=ssum[:, 3:4], in_=ssum[:, 3:4]), reads=[], writes=[Bss])
                    kb.op(ACT, lambda e: e.activation(out=flb[0][:, :], in_=fl[0][:, :], func=AF.Identity, scale=ssum[:, 3:4]),
                          reads=[Bfl[0], Bss], writes=[Bflb[0]])
                    f1 = flb[1][:, :]
                    rev = bass.AP(flb[1], f1.offset + L - 1, [list(f1.ap[0]), [-1, L]])
                    kb.op(ACT, lambda e, rev=rev: e.activation(out=rev, in_=fl[1][:, :], func=AF.Identity, scale=ssum[:, 3:4]),
                          reads=[Bfl[1], Bss], writes=[Bflb[1]])
                    kb.dma2(SP, A(G_d)[o, half * 128:(half + 1) * 128, L - 1:2 * L - 1], flb[0][:, :], L, reads=[Bflb[0]], sembuf=Bflb[0])
                    kb.dma2(SP, A(G_d)[o, half * 128:(half + 1) * 128, 0:L - 1], flb[1][:, 0:L - 1], L - 1, reads=[Bflb[1]], sembuf=Bflb[1])
                    yield
            rel.extend([Bc] + Bflb)

    def filt_stage(l, L, zT_d, tl_d, G_d):
        with ExitStack() as es:
            rel = []
            for _ in filt_gen(l, L, zT_d, tl_d, G_d, es, rel, "X"):
                pass
            kb.barrier()
            kb.release(rel)

    def hyena_gen(l, nseq, L, tbase, G_d, es, rel, cbanks, tbank, NST, pump=None, tag="S"):
        BS = 128
        nb = L // BS
        nblk = nseq * nb
        SW = BS * (2 * nb - 1)
        GW = 2 * L - 1
        cpb = 512 // nblk
        TPB = 4
        raw = stage_sb(es, "h_raw" + tag, [128, nseq, L + 2], F32)
        cv_ = [stage_sb(es, "h_c%d" % k + tag, [128, nseq, L], F32) for k in range(3)]
        zrev = stage_sb(es, "h_zrev" + tag, [128, nblk, BS], F32)
        zf = stage_sb(es, "h_zf" + tag, [BS, nblk, 128], BF16)
        ytm = raw[:, :, :].rearrange("p s t -> p (s t)")[:, 0:nblk * 128].rearrange("p (b c) -> p b c", c=128)
        strips = [stage_sb(es, "h_st%d" % k + tag, [BS, SW], BF16) for k in range(NST)]
        swt = stage_sb(es, "h_sw" + tag, [128, 6, 4], F32)
        hb = stage_sb(es, "h_hb" + tag, [128, 4], F32)
        yo = zf[:, :, :].rearrange("p b c -> p (b c)").rearrange("p (s t) -> p s t", s=nseq)
        z1 = cv_[1]
        Braw, Bzrev, Bzf, Bsw = Buf(), Buf(), Buf(), Buf()
        Bytm, BcT, Byo = Braw, Bzrev, Bzf
        convT = zrev[:, :, :].rearrange("p (s b) t -> p s (b t)", s=nseq)
        cflat = zrev[:, :, :].rearrange("p b t -> p (b t)")
        Bcv = [Buf(), Buf(), Buf()]
        Bz1 = Bcv[1]
        Bst = [Buf() for _ in range(NST)]
        rel.extend([Braw, Bsw, Bzf] + Bst)
        kb.dma(SP, swt[:, :, :], A(shw)[l], writes=[Bsw], sembuf=Bsw)
        kb.dma(SP, hb[:, :], A(hyb)[l], writes=[Bsw], sembuf=Bsw)
        Dl = [0] + [d for d in range(-(nb - 1), nb) if d != 0]
        scount = 0
        gcount = 0
        for half in range(2):
            for part in range(3):
                chn = part * 2 + half
                kb.op(DVE, lambda e: e.memset(raw[:, :, :], 0.0), writes=[Braw])
                for sq_ in range(nseq):
                    kb.dma2(SP, raw[:, sq_, 1:L + 1], A(uh_s)[chn, :, tbase + sq_ * L:tbase + (sq_ + 1) * L], L, writes=[Braw], sembuf=Braw)
                kb.op(ACT, lambda e, part=part, chn=chn: e.activation(out=cv_[part][:, :, :], in_=raw[:, :, 1:L + 1], func=AF.Identity,
                                                                   scale=swt[:, chn, 1:2], bias=swt[:, chn, 3:4]),
                      reads=[Braw, Bsw], writes=[Bcv[part]])
                kb.op(DVE, lambda e, part=part, chn=chn: e.scalar_tensor_tensor(out=cv_[part][:, :, :], in0=raw[:, :, 0:L], scalar=swt[:, chn, 0:1],
                                                                             in1=cv_[part][:, :, :], op0=ALU.mult, op1=ALU.add),
                      reads=[Braw, Bsw], writes=[Bcv[part]])
                kb.op(DVE, lambda e, part=part, chn=chn: e.scalar_tensor_tensor(out=cv_[part][:, :, :], in0=raw[:, :, 2:L + 2], scalar=swt[:, chn, 2:3],
                                                                             in1=cv_[part][:, :, :], op0=ALU.mult, op1=ALU.add),
                      reads=[Braw, Bsw], writes=[Bcv[part]])
                yield
            for o in range(2):
                zsrc = cv_[0] if o == 0 else z1
                Bzs = Bcv[0] if o == 0 else Bz1
                zs = zsrc[:, :, :]
                pstep = list(zs.ap[0])
                revap = bass.AP(zsrc, zs.offset + BS - 1, [pstep, [BS, nblk], [-1, BS]])
                kb.op(POOL, lambda e, revap=revap: e.tensor_copy(out=zrev[:, :, :], in_=revap), reads=[Bzs], writes=[Bzrev])
                for b0 in range(0, nblk, TPB):
                    for b in range(b0, b0 + TPB):
                        kb.op(PE, lambda e, b=b, b0=b0: e.transpose(psb[tbank][0:BS, (b - b0) * 128:(b - b0 + 1) * 128], zrev[:, b, :], id_f[:, :]),
                              reads=[Bzrev, B_const], writes=[PSB[tbank]], sig=(b == b0 + TPB - 1))
                    kb.op(ACT, lambda e, b0=b0: e.activation(out=zf[:, b0:b0 + TPB, :].rearrange("p b c -> p (b c)"), in_=psb[tbank][0:BS, :], func=AF.Identity),
                          reads=[PSB[tbank]], writes=[Bzf])
                    yield
                for c0 in range(0, 128, cpb):
                    pb = cbanks[gcount % len(cbanks)]
                    gcount += 1
                    for c in range(c0, c0 + cpb):
                        s = scount % NST
                        scount += 1
                        if _DBG.get("split64") and tag == "S":
                            wfrac = _DBG.get("wfrac", 1.0)
                            SWx = int(SW * wfrac)
                            for hp in range(2):
                                src = bass.AP(G_d, (o * 256 + half * 128 + c) * GW + 64 * hp, [[1, 64], [1, SWx]])
                                kb.dma2(SP, strips[s][64 * hp:64 * hp + 64, 0:SWx], src, SWx, writes=([Bst[s]] if hp == 0 else []), sembuf=Bst[s], maxel=4096)
                            Bst[s].w = kb.lasttok(Bst[s], False)
                        else:
                            src = bass.AP(G_d, (o * 256 + half * 128 + c) * GW, [[1, BS], [1, SW]])
                            kb.dma2(SP, strips[s][:, :], src, SW, writes=[Bst[s]], sembuf=Bst[s], maxel=_DBG.get("smax", 4096))
                        col0 = (c - c0) * nblk
                        for di, Dd in enumerate(Dl):
                            J0, J1 = max(0, -Dd), min(nb, nb - Dd)
                            zfa = zf[:, :, :].rearrange("p (s b) c -> p s b c", s=nseq)[:, :, J0:J1, c]
                            oa = psb[pb][0:BS, col0:col0 + nblk].rearrange("p (s b) -> p s b", s=nseq)[:, :, J0 + Dd:J1 + Dd]
                            kb.op(PE, lambda e, s=s, Dd=Dd, zfa=zfa, oa=oa, di=di: e.matmul(
                                oa, lhsT=strips[s][:, BS * (Dd + nb - 1):BS * (Dd + nb)], rhs=zfa,
                                start=(di == 0), stop=(di == len(Dl) - 1)),
                                reads=[Bst[s], Bzf], writes=[PSB[pb]], sig=(di == len(Dl) - 1))
                        if pump is not None:
                            pump()
                        if c != c0 + cpb - 1:
                            yield
                    kb.op(ACT, lambda e, c0=c0, pb=pb: e.activation(
                        out=ytm[:, :, c0:c0 + cpb], in_=psb[pb][0:BS, :].rearrange("p (c b) -> p b c", b=nblk), func=AF.Identity),
                        reads=[PSB[pb]], writes=[Bytm])
                    yield
                for b0 in range(0, nblk, TPB):
                    for b in range(b0, b0 + TPB):
                        kb.op(PE, lambda e, b=b, b0=b0: e.transpose(psb[tbank][:, (b - b0) * BS:(b - b0 + 1) * BS], ytm[:, b, :], id_f[0:BS, 0:BS]),
                              reads=[Bytm, B_const], writes=[PSB[tbank]], sig=(b == b0 + TPB - 1))
                    kb.op(ACT, lambda e, b0=b0: e.activation(out=cflat[:, b0 * BS:(b0 + TPB) * BS], in_=psb[tbank][:, :], func=AF.Identity),
                          reads=[PSB[tbank]], writes=[BcT])
                    yield
                oh = o * 2 + half
                kb.op(DVE, lambda e, zsrc=zsrc, oh=oh: e.scalar_tensor_tensor(out=convT, in0=zsrc[:, :, :], scalar=hb[:, oh:oh + 1],
                                                                           in1=convT, op0=ALU.mult, op1=ALU.add),
                      reads=[Bzs, Bsw], writes=[BcT])
                if o == 0:
                    kb.op(DVE, lambda e: e.tensor_tensor(out=z1[:, :, :], in0=convT, in1=cv_[1][:, :, :], op=ALU.mult),
                          reads=[BcT], writes=[Bz1])
                else:
                    kb.op(DVE, lambda e: e.tensor_tensor(out=yo, in0=convT, in1=cv_[2][:, :, :], op=ALU.mult),
                          reads=[BcT, Bcv[2]], writes=[Byo])
                    for sq_ in range(nseq):
                        kb.dma2(SP, A(mix_s)[6 + half, :, tbase + sq_ * L:tbase + (sq_ + 1) * L], yo[:, sq_, :], L, reads=[Byo], sembuf=Byo)
                yield

    def hyena_all(l, with_attn, with_pool=False):
        with ExitStack() as es:
            rel = []
            ag = attn_gen(l, es, POOL, rel) if with_attn else iter(())
            next(ag, None)
            pg = hyena_gen(l, NPS, 256, TS, G_p, es, rel, [2], 7, 3, tag="P")
            next(pg, None)
            og = pool_gen(l, es, rel, POOL, 7) if with_pool else iter(())
            next(og, None)
            cnt = [0]

            def pump():
                cnt[0] += 1
                next(pg, None)
                if cnt[0] % 3 == 0:
                    next(pg, None)
                if cnt[0] % 5 == 0:
                    next(ag, None)
                if cnt[0] % 16 == 0:
                    next(og, None)
            with ExitStack() as es2:
                for _ in hyena_gen(l, 1, TS, 0, G_s, es2, rel, [0, 1], 7, _DBG.get("nst", 4), pump=pump, tag="S"):
                    pass
                for _ in pg:
                    pass
                for _ in ag:
                    pass
                for _ in og:
                    pass
                kb.barrier()
            kb.release(rel)

    def mixe_stage(l):
        NB = T // 512
        with ExitStack() as es:
            xts = [stage_sb(es, "e_xt%d" % k, [128, 8, 512], F32) for k in range(2)]
            mts = [stage_sb(es, "e_mt%d" % k, [128, 8, 512], BF16) for k in range(2)]
            wo = stage_sb(es, "e_wo", [128, 8, 1024], BF16)
            Bxs = [[Buf() for _ in range(8)] for _ in range(2)]
            Bms = [Buf(), Buf()]
            Bw = Buf()
            for kc in range(8):
                kb.dma(POOL, wo[:, kc, :], A(wout)[l][:, kc * 1024:(kc + 1) * 1024], writes=([Bw] if kc == 0 else []), sembuf=Bw)
            Bw.w = kb.lasttok(Bw, True)

            def load(blk):
                xt, Bx, mt, Bm = xts[blk % 2], Bxs[blk % 2], mts[blk % 2], Bms[blk % 2]
                t0 = blk * 512
                for kc in range(8):
                    kb.dma(SP, xt[:, kc, :], A(yT)[:, kc, t0:t0 + 512], writes=[Bx[kc]], sembuf=Bx[kc])
                kb.dma(SP, mt[:, :, :], A(mix_s)[:, :, t0:t0 + 512].rearrange("c p t -> p c t"), writes=[Bm], sembuf=Bm)

            load(0)
            for blk in range(NB):
                xt, Bx, mt, Bm = xts[blk % 2], Bxs[blk % 2], mts[blk % 2], Bms[blk % 2]
                t0 = blk * 512
                ci = 0 if t0 < TS else 1
                if blk + 1 < NB:
                    load(blk + 1)
                for oc in range(8):
                    pb = oc % 4
                    for kc in range(8):
                        kb.op(PE, lambda e, oc=oc, kc=kc, pb=pb, mt=mt: e.matmul(psb[pb][:, :], lhsT=wo[:, kc, oc * 128:(oc + 1) * 128], rhs=mt[:, kc, :],
                                                                              start=(kc == 0), stop=(kc == 7)), reads=[Bw, Bm], writes=[PSB[pb]], sig=(kc == 7))
                    kb.op(DVE, lambda e, oc=oc, pb=pb, ci=ci, xt=xt: e.scalar_tensor_tensor(out=xt[:, oc, :], in0=psb[pb][:, :], scalar=Gg[:, 1, oc, ci:ci + 1],
                                                                                         in1=xt[:, oc, :], op0=ALU.mult, op1=ALU.add),
                          reads=[PSB[pb], B_mod], writes=[Bx[oc]])
                for kc in range(8):
                    kb.dma(SP, A(yT)[:, kc, t0:t0 + 512], xt[:, kc, :], reads=[Bx[kc]], sembuf=Bx[kc])
            kb.barrier()
            kb.release(Bxs[0] + Bxs[1] + Bms + [Bw])

    def want(name):
        return stages is None or name in stages
    kb.barrier()
    for l in range(2):
        fused_filt = want("mod%d" % l) and want("filt%d" % l) and _DBG.get("ffilt", 1)
        if fused_filt:
            with ExitStack() as fes:
                frel = []
                fgs = filt_gen(l, TS, c_zT_s, c_tl_s, G_s, fes, frel, "S")
                next(fgs, None)
                fgp = filt_gen(l, 256, c_zT_p, c_tl_p, G_p, fes, frel, "P")
                next(fgp, None)

                def fpump(fgs=fgs, fgp=fgp):
                    for _ in range(2):
                        if next(fgs, "done") == "done":
                            next(fgp, None)
                mod_stage(l, fpump)
                for _ in fgs:
                    pass
                for _ in fgp:
                    pass
                kb.barrier()
                kb.release(frel)
        elif want("mod%d" % l):
            mod_stage(l)
        if want("ffn%d0" % l):
            ffn_stage(l, 0, xT_in if l == 0 else yT)
        if want("mixa%d" % l):
            mixa_stage(l)
        fused_pool = want("pool%d" % l) and want("hy%d" % l) and _DBG.get("fpool", 1)
        if want("pool%d" % l) and not fused_pool:
            pool_stage(l)
        if want("filt%d" % l) and not fused_filt:
            filt_stage(l, TS, c_zT_s, c_tl_s, G_s)
            filt_stage(l, 256, c_zT_p, c_tl_p, G_p)
        if want("hy%d" % l):
            hyena_all(l, want("attn%d" % l), fused_pool)
        elif want("attn%d" % l):
            attn_stage(l)
        if want("mixe%d" % l):
            mixe_stage(l)
        if want("ffn%d1" % l):
            ffn_stage(l, 1, yT)
    kb.barrier()
    return nc


def _consts():
    c = {}
    for (L, nm) in ((TS, "s"), (256, "p")):
        t = np.linspace(0.0, 1.0, L, dtype=np.float32)[:, None]
        bands = 16
        f = np.linspace(1e-4, bands - 1, bands, dtype=np.float32)[None, :]
        w = (2.0 * math.pi * np.arange(L, dtype=np.float32)[:, None] / L).astype(np.float32)
        z = np.concatenate([t, np.cos(f * w), -np.sin(f * w)], axis=-1).astype(np.float32)
        c["c_zT_" + nm] = np.ascontiguousarray(z.T)
        c["c_tl_" + nm] = np.ascontiguousarray(np.broadcast_to(t[:, 0][None, :], (128, L))).astype(np.float32)
        tt = np.arange(L)
        invc = np.zeros((2, 128, L), np.float32)
        for gi, win in enumerate((2, 4, 8, 16)):
            lo = np.clip(tt - win // 2, 0, L)
            hi = np.clip(tt + win // 2, 0, L)
            invc[gi // 2, (gi % 2) * 64:(gi % 2) * 64 + 64, :] = (1.0 / (hi - lo).astype(np.float32))[None, :]
        c["c_invc_" + nm] = invc
    tt = np.arange(TS)
    pos_row, pos_col = tt // 64, tt % 64
    inv = (10000.0 ** (-np.arange(16, dtype=np.float32) / 16)).astype(np.float32)
    cos = np.zeros((64, TS), np.float32)
    sin = np.zeros((64, TS), np.float32)
    for d in range(64):
        pos = pos_row if d < 32 else pos_col
        dd = d % 32
        ang = pos.astype(np.float32) * inv[dd % 16]
        cos[d] = np.cos(ang)
        sin[d] = -np.sin(ang) if dd < 16 else np.sin(ang)
    c["c_cos"] = np.concatenate([cos, cos], 0)
    c["c_sin"] = np.concatenate([sin, sin], 0)
    pm = np.zeros((128, 128), np.float32)
    for m in range(128):
        dd = m % 32
        k = m + 16 if dd < 16 else m - 16
        pm[k, m] = 1.0
    c["c_pm"] = pm
    bd = np.zeros((128, 128), np.float32)
    bd[:64, :64] = 1.0 / 64
    bd[64:, 64:] = 1.0 / 64
    c["c_bd"] = bd
    c["c_id"] = np.eye(128, dtype=np.float32)
    j = np.arange(128)[:, None]
    i = np.arange(128)[None, :]
    c["c_mask"] = np.stack([(j >= i), (j <= i)]).astype(np.float32)
    return c


_NC = None
_DBG = {}


def kernel(x_prompt, x_sample, cache_k, cache_v, c, c_ctx, ada_w, ada_b, norm_w,
           ffn_wg, ffn_wu, ffn_wd, w_in, w_out, pool_w, pool_scale, q_norm, k_norm,
           attn_sink, hy_short_w, hy_short_b, hy_f_w1, hy_f_b1, hy_f_w2, hy_f_b2,
           hy_f_w3, hy_f_b3, hy_sin_freq, hy_decay, hy_bias):
    global _NC
    f32 = np.float32
    g = lambda a: np.asarray(a, dtype=f32)
    x_prompt, x_sample, cache_k, cache_v, c, c_ctx = map(g, (x_prompt, x_sample, cache_k, cache_v, c, c_ctx))
    ada_w, ada_b, norm_w, ffn_wg, ffn_wu, ffn_wd, w_in, w_out = map(g, (ada_w, ada_b, norm_w, ffn_wg, ffn_wu, ffn_wd, w_in, w_out))
    pool_w, pool_scale, q_norm, k_norm, attn_sink = map(g, (pool_w, pool_scale, q_norm, k_norm, attn_sink))
    hy_short_w, hy_short_b, hy_f_w1, hy_f_b1, hy_f_w2, hy_f_b2 = map(g, (hy_short_w, hy_short_b, hy_f_w1, hy_f_b1, hy_f_w2, hy_f_b2))
    hy_f_w3, hy_f_b3, hy_sin_freq, hy_decay, hy_bias = map(g, (hy_f_w3, hy_f_b3, hy_sin_freq, hy_decay, hy_bias))

    def fm(v, n):
        return np.ascontiguousarray(np.swapaxes(v.reshape(v.shape[:-1] + (n, 128)), -1, -2))

    shared = dict(_consts())
    shared["adaw"] = np.ascontiguousarray(ada_w.reshape(2, 8, 128, 72, 128).transpose(0, 3, 2, 1, 4)).reshape(2, 72, 128, 1024)
    shared["adab"] = fm(ada_b, 72)
    shared["normw"] = np.ascontiguousarray(norm_w.reshape(2, 3, 8, 128).transpose(0, 3, 1, 2)).reshape(2, 128, 24)
    shared["wg"] = np.ascontiguousarray(ffn_wg.reshape(2, 2, 8, 128, NFF, 128).transpose(0, 1, 4, 3, 2, 5)).reshape(2, 2, NFF, 128, 1024)
    shared["wu"] = np.ascontiguousarray(ffn_wu.reshape(2, 2, 8, 128, NFF, 128).transpose(0, 1, 4, 3, 2, 5)).reshape(2, 2, NFF, 128, 1024)
    shared["wd"] = np.ascontiguousarray(ffn_wd.reshape(2, 2, NFF, 128, 1024).transpose(0, 1, 3, 2, 4)).reshape(2, 2, 128, NFF * 1024)
    qcols = []
    for cc in range(4):
        qcols += list(range(256 + 64 * cc, 256 + 64 * cc + 64)) + list(range(256 + 64 * (4 + cc), 256 + 64 * (4 + cc) + 64))
    colsFM = list(range(256)) + qcols + list(range(768, 896)) + list(range(1024, 1792))
    shared["win"] = np.ascontiguousarray(w_in[:, :, colsFM].reshape(2, 8, 128, 1664).transpose(0, 2, 1, 3)).reshape(2, 128, 8 * 1664)
    shared["winv"] = np.ascontiguousarray(w_in[:, :, 896:1024].reshape(2, 8, 128, 128).transpose(0, 2, 1, 3)).reshape(2, 128, 1024)
    rows = list(range(256))
    for cc in range(4):
        for kh in range(2):
            rows += list(range(256 + 64 * (4 * kh + cc), 256 + 64 * (4 * kh + cc) + 64))
    rows += list(range(768, 1024))
    shared["wout"] = np.ascontiguousarray(w_out[:, rows, :].reshape(2, 8, 128, 1024).transpose(0, 2, 1, 3)).reshape(2, 128, 8192)
    pw = np.zeros((2, 2, 128, 128), f32)
    for l in range(2):
        for ch in range(2):
            pw[l, ch, :64, :64] = pool_w[l, 2 * ch]
            pw[l, ch, 64:, 64:] = pool_w[l, 2 * ch + 1]
    shared["poolw"] = pw
    shared["poolsc"] = fm(pool_scale, 2)
    shared["qkn"] = np.ascontiguousarray(np.stack([np.concatenate([q_norm, q_norm], -1), np.concatenate([k_norm, k_norm], -1)], -1))
    shared["sinkb"] = np.ascontiguousarray(np.broadcast_to(attn_sink.reshape(2, 2, 1, 4), (2, 2, 64, 4)))
    shw = np.zeros((2, 128, 6, 4), f32)
    shw[:, :, :, 0:3] = hy_short_w.reshape(2, 3, 6, 128).transpose(0, 3, 2, 1)
    shw[:, :, :, 3] = hy_short_b.reshape(2, 6, 128).transpose(0, 2, 1)
    shared["shw"] = shw
    shared["fw1"] = hy_f_w1
    shared["fw2"] = hy_f_w2
    shared["fw3"] = hy_f_w3
    fsm = np.zeros((2, 64, 8), f32)
    fsm[:, :, 0] = hy_f_b1
    fsm[:, :, 1] = hy_f_b2
    fsm[:, :, 2] = hy_sin_freq[:, 0]
    fsm[:, :, 3] = hy_sin_freq[:, 1]
    shared["fsm"] = fsm
    shared["fb3"] = fm(hy_f_b3, 8)
    shared["fdec"] = fm(hy_decay.reshape(2, 512), 4)
    shared["hyb"] = fm(hy_bias.reshape(2, 512), 4)

    in_maps = []
    for r in range(8):
        sb = r % 4
        xs = x_sample[sb]
        xp = x_prompt[NPS * r:NPS * r + NPS].reshape(TPR, D)
        xall = np.concatenate([xs, xp], 0)
        m = dict(shared)
        m["xT"] = np.ascontiguousarray(xall.reshape(T, 8, 128).transpose(2, 1, 0))
        cd = np.stack([c[sb], c_ctx], -1)
        m["condT"] = np.ascontiguousarray(cd.reshape(8, 128, 2).transpose(1, 0, 2))
        m["ckT"] = np.ascontiguousarray(cache_k[sb].reshape(2, 256, 128).transpose(0, 2, 1))
        m["cv"] = np.ascontiguousarray(cache_v[sb].reshape(2, 256, 128))
        in_maps.append(m)

    if _DBG.get("maps_only"):
        return in_maps
    if _NC is None:
        _NC = build_program()
    res = run_bass_kernel_spmd(_NC, in_maps, core_ids=list(range(8)))
    y_prompt = np.zeros((16, 256, D), f32)
    y_sample = np.zeros((4, TS, D), f32)
    nk = np.zeros((16, 2, 256, 2, 64), f32)
    nv = np.zeros((16, 2, 256, 2, 64), f32)
    for r in range(8):
        o = res.results[r]
        yt = np.asarray(o["yT"]).transpose(2, 1, 0).reshape(T, D)
        if r < 4:
            y_sample[r] = yt[:TS]
        y_prompt[NPS * r:NPS * r + NPS] = yt[TS:].reshape(NPS, 256, D)
        okt = np.asarray(o["okT"])
        nk[NPS * r:NPS * r + NPS] = okt.transpose(2, 0, 1).reshape(NPS, 256, 2, 2, 64).transpose(0, 2, 1, 3, 4)
        ovv = np.asarray(o["ov"])
        nv[NPS * r:NPS * r + NPS] = ovv.reshape(2, NPS, 256, 2, 64).transpose(1, 0, 2, 3, 4)
    return (y_prompt, y_sample, nk, nv)
```

```python
import math
from contextlib import ExitStack
import numpy as np
import concourse.bass as bass
import concourse.mybir as mybir
from concourse.bass_utils import run_bass_kernel_spmd

F32 = mybir.dt.float32
BF16 = mybir.dt.bfloat16
ALU = mybir.AluOpType
AF = mybir.ActivationFunctionType
AX = mybir.AxisListType

D = 1024
DFF = 2816
NFF = 22
TS = 4096
TPR = 512
NPS = TPR // 256
T = TS + TPR
NBLK = T // 128
EPS = 1e-6
TWO_PI = 2.0 * math.pi


class Buf:
    def __init__(self, name=""):
        self.name = name
        self.w = None
        self.r = []
        self.dsem = None
        self.dsem_sw = None


class Eng:
    def __init__(self, kb, name, h, is_pe=False):
        self.kb, self.name, self.h, self.is_pe = kb, name, h, is_pe
        self.sem = kb.nc.alloc_semaphore("e_" + name)
        self.semid = id(self)
        self.n = 0
        self.waited = {}


class KB:
    def __init__(self, nc):
        self.nc = nc
        self.pe = Eng(self, "pe", nc.tensor, True)
        self.act = Eng(self, "act", nc.scalar)
        self.dve = Eng(self, "dve", nc.vector)
        self.pool = Eng(self, "pool", nc.gpsimd)
        self.sp = Eng(self, "sp", nc.sync)
        self.dsems = []
        self.free_dsems = {False: [], True: []}
        self.outstanding = []

    def _deps(self, reads, writes):
        toks = []
        for b in reads:
            if b.w is not None:
                toks.append(b.w)
        for b in writes:
            if b.w is not None:
                toks.append(b.w)
            toks.extend(b.r)
        return toks

    def _wait(self, eng, toks):
        mx = {}
        for (sem, key, val) in toks:
            if key == eng.semid and eng.is_pe:
                continue
            if key not in mx or mx[key][1] < val:
                mx[key] = (sem, val)
        for key, (sem, val) in mx.items():
            if eng.waited.get(key, 0) < val:
                eng.h.wait_ge(sem, val)
                eng.waited[key] = val

    def _update(self, tok, reads, writes):
        for b in reads:
            b.r.append(tok)
        for b in writes:
            b.w = tok
            b.r = []

    def op(self, eng, fn, reads=(), writes=(), sig=True):
        self._wait(eng, self._deps(reads, writes))
        ins = fn(eng.h)
        if sig or not eng.is_pe:
            eng.n += 1
            ins.then_inc(eng.sem, 1)
            tok = (eng.sem, eng.semid, eng.n)
        else:
            tok = (eng.sem, eng.semid, eng.n + 1)
        self._update(tok, reads, writes)
        return tok

    def get_dsem(self, b, sw):
        attr = "dsem_sw" if sw else "dsem"
        if getattr(b, attr, None) is None:
            if self.free_dsems[sw]:
                setattr(b, attr, self.free_dsems[sw].pop())
            else:
                sm = self.nc.alloc_semaphore("d%d" % len(self.dsems))
                ds = [sm, 0, len(self.dsems) + 1000]
                self.dsems.append(ds)
                setattr(b, attr, ds)
        return getattr(b, attr)

    def lasttok(self, b, sw):
        ds = b.dsem_sw if sw else b.dsem
        return (ds[0], ds[2], ds[1])

    def dma(self, q, out, in_, reads=(), writes=(), sembuf=None):
        self._wait(q, self._deps(reads, writes))
        ds = self.get_dsem(sembuf, q is self.pool)
        ds[1] += 16
        q.h.dma_start(out=out, in_=in_).then_inc(ds[0], 16)
        tok = (ds[0], ds[2], ds[1])
        self._update(tok, reads, writes)
        self.outstanding.append(tok)
        return tok

    def dma2(self, q, out, in_, n, reads=(), writes=(), sembuf=None, maxel=2048):
        toks = None
        for a in range(0, n, maxel):
            b = min(n, a + maxel)
            toks = self.dma(q, out[:, a:b], in_[:, a:b], reads=reads, writes=(writes if a == 0 else ()), sembuf=sembuf)
        if writes:
            for bb in writes:
                bb.w = toks
        return toks

    def release(self, bufs):
        for b in bufs:
            if b.dsem is not None:
                self.free_dsems[False].append(b.dsem)
                b.dsem = None
            if b.dsem_sw is not None:
                self.free_dsems[True].append(b.dsem_sw)
                b.dsem_sw = None

    def barrier(self):
        toks = list(self.outstanding)
        for e in (self.pe, self.act, self.dve, self.pool, self.sp):
            if e.n > 0:
                toks.append((e.sem, e.semid, e.n))
        for e in (self.pe, self.act, self.dve, self.pool, self.sp):
            mx = {}
            for (sem, key, val) in toks:
                if key == e.semid:
                    continue
                if key not in mx or mx[key][2] < val:
                    mx[key] = (sem, key, val)
            for tk in mx.values():
                if e.waited.get(tk[1], 0) < tk[2]:
                    e.h.wait_ge(tk[0], tk[2])
                    e.waited[tk[1]] = tk[2]
        self.outstanding = []


def build_program(stages=None, debug=False):
    nc = bass.Bass("TRN2", target_bir_lowering=False)
    kb = KB(nc)
    PE, ACT, DVE, POOL, SP = kb.pe, kb.act, kb.dve, kb.pool, kb.sp

    def din(name, shape, dt=F32):
        return nc.dram_tensor(name, list(shape), dt, kind="ExternalInput")

    def dout(name, shape, dt=F32):
        return nc.dram_tensor(name, list(shape), dt, kind="ExternalOutput")

    def dscr(name, shape, dt):
        return nc.dram_tensor(name, list(shape), dt, kind="ExternalOutput" if debug else "Internal")

    xT_in = din("xT", [128, 8, T])
    condT = din("condT", [128, 8, 2])
    adaw = din("adaw", [2, 72, 128, 8 * 128])
    adab = din("adab", [2, 128, 72])
    normw = din("normw", [2, 128, 24])
    wg = din("wg", [2, 2, NFF, 128, 8 * 128])
    wu = din("wu", [2, 2, NFF, 128, 8 * 128])
    wd = din("wd", [2, 2, 128, NFF * 1024])
    win = din("win", [2, 128, 8 * 1664])
    winv = din("winv", [2, 128, 8 * 128])
    wout = din("wout", [2, 128, 8 * 1024])
    poolw = din("poolw", [2, 2, 128, 128])
    poolsc = din("poolsc", [2, 128, 2])
    qkn = din("qkn", [2, 128, 2])
    sinkb = din("sinkb", [2, 2, 64, 4])
    ckT = din("ckT", [2, 128, 256])
    cv = din("cv", [2, 256, 128])
    shw = din("shw", [2, 128, 6, 4])
    fw1 = din("fw1", [2, 33, 64])
    fw2 = din("fw2", [2, 64, 64])
    fw3 = din("fw3", [2, 64, 1024])
    fsm = din("fsm", [2, 64, 8])
    fb3 = din("fb3", [2, 128, 8])
    fdec = din("fdec", [2, 128, 4])
    hyb = din("hyb", [2, 128, 4])
    c_zT_s = din("c_zT_s", [33, TS])
    c_zT_p = din("c_zT_p", [33, 256])
    c_tl_s = din("c_tl_s", [128, TS])
    c_tl_p = din("c_tl_p", [128, 256])
    c_cos = din("c_cos", [128, TS])
    c_sin = din("c_sin", [128, TS])
    c_pm = din("c_pm", [128, 128])
    c_bd = din("c_bd", [128, 128])
    c_id = din("c_id", [128, 128])
    c_mask = din("c_mask", [2, 128, 128])
    c_invc_s = din("c_invc_s", [2, 128, TS])
    c_invc_p = din("c_invc_p", [2, 128, 256])

    yT = dout("yT", [128, 8, T])
    okT = dout("okT", [2, 128, TPR])
    ov = dout("ov", [2, TPR, 128])

    qT_s = dscr("qT_s", [4, 128, T], BF16)
    kT_s = dscr("kT_s", [128, T], BF16)
    v_s = dscr("v_s", [T, 128], BF16)
    up_s = dscr("up_s", [2, 128, T], F32)
    uh_s = dscr("uh_s", [6, 128, T], F32)
    mix_s = dscr("mix_s", [8, 128, T], BF16)
    G_s = dscr("G_s", [2, 256, 2 * TS - 1], BF16)
    G_p = dscr("G_p", [2, 256, 2 * 256 - 1], BF16)

    psb = [nc.alloc_psum_tensor("ps%d" % i, [128, 512], F32) for i in range(8)]
    PSB = [Buf("ps%d" % i) for i in range(8)]

    def A(t):
        return t.ap() if hasattr(t, "ap") and callable(getattr(t, "ap")) else t

    def sbp(name, shape, dt):
        return nc.alloc_sbuf_tensor(name, list(shape), dt)

    ones_bf = sbp("ones_bf", [128, 128], BF16)
    ones64 = sbp("ones64", [128, 64], BF16)
    bd_bf = sbp("bd_bf", [128, 128], BF16)
    pm_bf = sbp("pm_bf", [128, 128], BF16)
    id_f = sbp("id_f", [128, 128], F32)
    mask_f = sbp("mask_f", [128, 2, 128], F32)
    scT = sbp("scT", [128, 8, 2], F32)
    mod = sbp("mod", [128, 72, 2], F32)
    nrm = sbp("nrm", [128, 24], F32)
    Aab = sbp("Aab", [128, 3, 8, 2], F32)
    Gg = sbp("Gg", [128, 3, 8, 2], F32)
    adab_sb = sbp("adab_sb", [128, 72], F32)
    B_const = Buf("const")
    B_mod = Buf("mod")

    kb.op(DVE, lambda e: e.memset(ones_bf[:, :], 1.0 / 1024.0), writes=[B_const])
    kb.op(DVE, lambda e: e.memset(ones64[:, :], 1.0), writes=[B_const])
    kb.dma(POOL, bd_bf[:, :], A(c_bd)[:, :], writes=[B_const], sembuf=B_const)
    kb.dma(POOL, pm_bf[:, :], A(c_pm)[:, :], writes=[B_const], sembuf=B_const)
    kb.dma(SP, id_f[:, :], A(c_id)[:, :], writes=[B_const], sembuf=B_const)
    kb.dma(SP, mask_f[:, 0, :], A(c_mask)[0], writes=[B_const], sembuf=B_const)
    kb.dma(SP, mask_f[:, 1, :], A(c_mask)[1], writes=[B_const], sembuf=B_const)
    kb.dma(SP, scT[:, :, :], A(condT)[:, :, :], writes=[B_const], sembuf=B_const)
    kb.op(ACT, lambda e: e.activation(out=scT[:, :, :], in_=scT[:, :, :], func=AF.Silu),
          reads=[], writes=[B_const])

    _ctr = [0]

    def stage_sb(es, name, shape, dt):
        _ctr[0] += 1
        return es.enter_context(nc.sbuf_tensor("%s_%d" % (name, _ctr[0]), list(shape), dt))

    def mod_stage(l, pump=None):
        with ExitStack() as es:
            ring = [stage_sb(es, "adar%d" % i, [128, 8, 128], F32) for i in range(3)]
            RB = [Buf("adar%d" % i) for i in range(3)]
            kb.dma(SP, adab_sb[:, :], A(adab)[l], writes=[B_mod], sembuf=B_mod)
            kb.dma(SP, nrm[:, :], A(normw)[l], writes=[B_mod], sembuf=B_mod)
            for fc in range(72):
                if pump is not None:
                    pump()
                s = fc % 3
                kb.dma(SP, ring[s][:, :, :], A(adaw)[l, fc].rearrange("p (k j) -> p k j", j=128),
                       writes=[RB[s]], sembuf=RB[s])
                bank = 6 + (fc % 2)
                for kc in range(8):
                    kb.op(PE, lambda e, kc=kc, s=s, bank=bank: e.matmul(
                        psb[bank][:, 0:2], lhsT=ring[s][:, kc, :], rhs=scT[:, kc, :],
                        start=(kc == 0), stop=(kc == 7)),
                        reads=[RB[s], B_const], writes=[PSB[bank]], sig=(kc == 7))
                kb.op(DVE, lambda e, fc=fc, bank=bank: e.tensor_scalar(
                    out=mod[:, fc, :], in0=psb[bank][:, 0:2], scalar1=adab_sb[:, fc:fc + 1],
                    scalar2=None, op0=ALU.add), reads=[PSB[bank], B_mod], writes=[B_mod])
            for i in range(3):
                for ci in range(2):
                    kb.op(DVE, lambda e, i=i, ci=ci: e.scalar_tensor_tensor(
                        out=Aab[:, i, :, ci], in0=mod[:, (3 * i + 1) * 8:(3 * i + 2) * 8, ci], scalar=1.0,
                        in1=nrm[:, i * 8:(i + 1) * 8], op0=ALU.add, op1=ALU.mult),
                        reads=[B_mod], writes=[B_mod])
                    gs = 0.5 if i != 1 else 1.0
                    kb.op(DVE, lambda e, i=i, ci=ci, gs=gs: e.tensor_scalar(
                        out=Gg[:, i, :, ci], in0=mod[:, (3 * i + 2) * 8:(3 * i + 3) * 8, ci], scalar1=gs,
                        scalar2=None, op0=ALU.mult), reads=[B_mod], writes=[B_mod])
            kb.barrier()
            kb.release(RB)

    def norm_mod(xap, hap, i, ci, sq, rstd, tmpn, Bx, Bh, Bsq, Brs, Btmp, bank):
        for kc in range(8):
            kb.op(ACT, lambda e, kc=kc: e.activation(out=sq[:, kc, :], in_=xap(kc), func=AF.Square),
                  reads=[Bx[kc]], writes=[Bsq])
        for kc in range(8):
            kb.op(PE, lambda e, kc=kc: e.matmul(psb[bank][:, :], lhsT=ones_bf[:, :], rhs=sq[:, kc, :],
                                               start=(kc == 0), stop=(kc == 7)),
                  reads=[Bsq, B_const], writes=[PSB[bank]], sig=(kc == 7))
        kb.op(ACT, lambda e: e.activation(out=rstd[:, :], in_=psb[bank][:, :], func=AF.Sqrt, bias=EPS_AP[:, 0:1]),
              reads=[PSB[bank]], writes=[Brs])
        kb.op(DVE, lambda e: e.reciprocal(out=rstd[:, :], in_=rstd[:, :]), reads=[Brs], writes=[Brs])
        for kc in range(8):
            tb = kc % 2
            kb.op(DVE, lambda e, kc=kc, tb=tb: e.scalar_tensor_tensor(
                out=tmpn[tb][:, :], in0=xap(kc), scalar=Aab[:, i, kc, ci:ci + 1], in1=rstd[:, :],
                op0=ALU.mult, op1=ALU.mult), reads=[Bx[kc], Brs, B_mod], writes=[Btmp[tb]])
            kb.op(ACT, lambda e, kc=kc, tb=tb: e.activation(
                out=hap(kc), in_=tmpn[tb][:, :], func=AF.Identity,
                bias=mod[:, (3 * i) * 8 + kc, ci:ci + 1]), reads=[Btmp[tb], B_mod], writes=[Bh])

    eps_t = sbp("eps_t", [128, 1], F32)
    EPS_AP = eps_t
    kb.op(DVE, lambda e: e.memset(eps_t[:, :], EPS), writes=[B_const])

    def ffn_stage(l, f, src):
        i = 0 if f == 0 else 2
        NT = (T + 1023) // 1024

        def nhalf(tile):
            return min(2, (T - tile * 1024) // 512)
        with ExitStack() as es:
            xts = [stage_sb(es, "f_xt%d" % k, [128, 8, 1024], F32) for k in range(2)]
            hT = stage_sb(es, "f_hT", [128, 8, 1024], BF16)
            aT = stage_sb(es, "f_aT", [128, NFF, 1024], BF16)
            wds = stage_sb(es, "f_wd", [128, NFF, 1024], BF16)
            ring = [stage_sb(es, "f_r%d" % k, [128, 2, 8, 128], BF16) for k in range(3)]
            sq = stage_sb(es, "f_sq", [128, 8, 512], BF16)
            rstd = stage_sb(es, "f_rstd", [128, 512], F32)
            tmpn = [stage_sb(es, "f_tmp%d" % k, [128, 512], F32) for k in range(2)]
            sg = [stage_sb(es, "f_sg%d" % k, [128, 512], F32) for k in range(2)]
            Bxs = [[Buf("x%d_%d" % (b_, k)) for k in range(8)] for b_ in range(2)]
            BhT = [Buf("hT0"), Buf("hT1")]
            BaT = [[Buf() for _ in range(2)] for _ in range(NFF)]
            Bwd = [Buf("wd%d" % k) for k in range(NFF)]
            RB = [Buf("r%d" % k) for k in range(3)]
            Bsq, Brs = Buf("sq"), Buf("rs")
            Btmp = [Buf(), Buf()]
            Bsg = [Buf(), Buf()]

            def load_x(tile):
                xt, Bx = xts[tile % 2], Bxs[tile % 2]
                t0 = tile * 1024
                w_ = nhalf(tile) * 512
                for kc in range(8):
                    kb.dma(SP, xt[:, kc, 0:w_], A(src)[:, kc, t0:t0 + w_], writes=[Bx[kc]], sembuf=Bx[kc])

            def norm_tile(tile):
                xt, Bx = xts[tile % 2], Bxs[tile % 2]
                ci = 0 if tile < TS // 1024 else 1
                for h in range(nhalf(tile)):
                    norm_mod(lambda kc, h=h: xt[:, kc, h * 512:(h + 1) * 512],
                             lambda kc, h=h: hT[:, kc, h * 512:(h + 1) * 512],
                             i, ci, sq, rstd, tmpn, Bx, BhT[h], Bsq, Brs, Btmp, 7)

            load_x(0)
            norm_tile(0)
            rcount = 0
            for tile in range(NT):
                xt, Bx = xts[tile % 2], Bxs[tile % 2]
                ci = 0 if tile < TS // 1024 else 1
                t0 = tile * 1024
                if tile + 1 < NT:
                    load_x(tile + 1)
                for c in range(NFF):
                    s = rcount % 3
                    rcount += 1
                    kb.dma(POOL, ring[s][:, 0, :, :], A(wg)[l, f, c].rearrange("p (k j) -> p k j", j=128),
                           writes=[RB[s]], sembuf=RB[s])
                    kb.dma(POOL, ring[s][:, 1, :, :], A(wu)[l, f, c].rearrange("p (k j) -> p k j", j=128),
                           reads=[], writes=[], sembuf=RB[s])
                    RB[s].w = kb.lasttok(RB[s], True)
                    if tile == 0:
                        kb.dma(POOL, wds[:, c, :], A(wd)[l, f][:, c * 1024:(c + 1) * 1024], writes=[Bwd[c]], sembuf=Bwd[c])
                    for h in range(nhalf(tile)):
                        pb = 2 * ((2 * c + h) % 2)
                        for gu in range(2):
                            for kc in range(8):
                                kb.op(PE, lambda e, gu=gu, kc=kc, s=s, h=h, pb=pb: e.matmul(
                                    psb[pb + gu][:, :], lhsT=ring[s][:, gu, kc, :],
                                    rhs=hT[:, kc, h * 512:(h + 1) * 512], start=(kc == 0), stop=(kc == 7)),
                                    reads=[RB[s], BhT[h]], writes=[PSB[pb + gu]], sig=(kc == 7))
                        k2 = (2 * c + h) % 2
                        kb.op(ACT, lambda e, pb=pb, k2=k2: e.activation(out=sg[k2][:, :], in_=psb[pb][:, :], func=AF.Silu),
                              reads=[PSB[pb]], writes=[Bsg[k2]])
                        kb.op(DVE, lambda e, pb=pb, k2=k2, c=c, h=h: e.tensor_tensor(
                            out=aT[:, c, h * 512:(h + 1) * 512], in0=sg[k2][:, :], in1=psb[pb + 1][:, :], op=ALU.mult),
                            reads=[Bsg[k2], PSB[pb + 1]], writes=[BaT[c][h]])
                if tile + 1 < NT:
                    norm_tile(tile + 1)
                dcount = 0
                for oc in range(8):
                    for h in range(nhalf(tile)):
                        pb = 4 + (dcount % 3)
                        dcount += 1
                        for c in range(NFF):
                            kb.op(PE, lambda e, c=c, oc=oc, h=h, pb=pb: e.matmul(
                                psb[pb][:, :], lhsT=wds[:, c, oc * 128:(oc + 1) * 128],
                                rhs=aT[:, c, h * 512:(h + 1) * 512], start=(c == 0), stop=(c == NFF - 1)),
                                reads=[Bwd[c], BaT[c][h]], writes=[PSB[pb]], sig=(c == NFF - 1))
                        kb.op(DVE, lambda e, oc=oc, h=h, pb=pb, ci=ci, xt=xt: e.scalar_tensor_tensor(
                            out=xt[:, oc, h * 512:(h + 1) * 512], in0=psb[pb][:, :], scalar=Gg[:, i, oc, ci:ci + 1],
                            in1=xt[:, oc, h * 512:(h + 1) * 512], op0=ALU.mult, op1=ALU.add),
                            reads=[PSB[pb], B_mod], writes=[Bx[oc]])
                w_ = nhalf(tile) * 512
                for kc in range(8):
                    kb.dma(SP, A(yT)[:, kc, t0:t0 + w_], xt[:, kc, 0:w_], reads=[Bx[kc]], sembuf=Bx[kc])
            kb.barrier()
            kb.release(Bxs[0] + Bxs[1] + Bwd + RB)

    def mixa_stage(l):
        with ExitStack() as es:
            xts_ = [stage_sb(es, "a_xt%d" % k, [128, 8, 512], F32) for k in range(2)]
            hTs_ = [stage_sb(es, "a_hT%d" % k, [128, 8, 512], BF16) for k in range(2)]
            w_sb = stage_sb(es, "a_w", [128, 8, 1664], BF16)
            wv_sb = stage_sb(es, "a_wv", [128, 8, 128], BF16)
            sq = stage_sb(es, "a_sq", [128, 8, 512], BF16)
            rstd = stage_sb(es, "a_rstd", [128, 512], F32)
            tmpn = [stage_sb(es, "a_tmp%d" % k, [128, 512], F32) for k in range(2)]
            cos_sb = stage_sb(es, "a_cos", [128, TS], F32)
            sin_sb = stage_sb(es, "a_sin", [128, TS], F32)
            qkn_sb = stage_sb(es, "a_qkn", [128, 2], F32)
            sqh_ = [stage_sb(es, "a_sqh%d" % k, [128, 512], BF16) for k in range(2)]
            rs_ = [stage_sb(es, "a_rs%d" % k, [128, 512], F32) for k in range(2)]
            qn_ = [stage_sb(es, "a_qn%d" % k, [128, 512], F32) for k in range(2)]
            qnb_ = [stage_sb(es, "a_qnb%d" % k, [128, 512], BF16) for k in range(2)]
            t1_ = [stage_sb(es, "a_t1%d" % k, [128, 512], F32) for k in range(2)]
            t2_ = [stage_sb(es, "a_t2%d" % k, [128, 512], F32) for k in range(2)]
            ob = [stage_sb(es, "a_ob%d" % k, [128, 512], BF16) for k in range(2)]
            of = [stage_sb(es, "a_of%d" % k, [128, 512], F32) for k in range(2)]
            vb = [stage_sb(es, "a_vb%d" % k, [128, 128], BF16) for k in range(2)]
            vf = [stage_sb(es, "a_vf%d" % k, [128, 128], F32) for k in range(2)]
            Bxs_ = [[Buf() for _ in range(8)] for _ in range(2)]
            Bhs_ = [Buf(), Buf()]
            Bsq, Brs, Bw, Bc = Buf(), Buf(), Buf(), Buf()
            Btmp = [Buf(), Buf()]
            Bsqh_, Brs2_, Bqn_, Bqnb_, Bt1_, Bt2_ = [[Buf(), Buf()] for _ in range(6)]
            chain = [0]
            Bob = [Buf(), Buf()]
            Bof = [Buf(), Buf()]
            Bvb = [Buf(), Buf()]
            Bvf = [Buf(), Buf()]
            for kc in range(8):
                kb.dma(POOL, w_sb[:, kc, :], A(win)[l][:, kc * 1664:(kc + 1) * 1664], writes=([Bw] if kc == 0 else []), sembuf=Bw)
            kb.dma(POOL, wv_sb[:, :, :], A(winv)[l].rearrange("p (k j) -> p k j", j=128), writes=[], sembuf=Bw)
            Bw.w = kb.lasttok(Bw, True)
            kb.dma2(SP, cos_sb, A(c_cos), TS, writes=[Bc], sembuf=Bc)
            kb.dma2(SP, sin_sb, A(c_sin), TS, writes=[Bc], sembuf=Bc)
            kb.dma(SP, qkn_sb[:, :], A(qkn)[l], writes=[Bc], sembuf=Bc)
            oc_ = 0
            vc_ = 0
            NBK = T // 512

            def a_load(blk):
                xt_, Bx_ = xts_[blk % 2], Bxs_[blk % 2]
                for kc in range(8):
                    kb.dma(SP, xt_[:, kc, :], A(yT)[:, kc, blk * 512:(blk + 1) * 512], writes=[Bx_[kc]], sembuf=Bx_[kc])

            def a_norm(blk):
                xt_, Bx_, hT_, Bh_ = xts_[blk % 2], Bxs_[blk % 2], hTs_[blk % 2], Bhs_[blk % 2]
                norm_mod(lambda kc: xt_[:, kc, :], lambda kc: hT_[:, kc, :], 1, (0 if blk * 512 < TS else 1), sq, rstd, tmpn,
                         Bx_, Bh_, Bsq, Brs, Btmp, 7)

            a_load(0)
            a_load(1)
            a_norm(0)
            for blk in range(NBK):
                t0 = blk * 512
                isS = t0 < TS
                ci = 0 if isS else 1
                hT, Bh = hTs_[blk % 2], Bhs_[blk % 2]
                for ch in range(13):
                    pb = ch % 4
                    for kc in range(8):
                        kb.op(PE, lambda e, ch=ch, kc=kc, pb=pb: e.matmul(
                            psb[pb][:, :], lhsT=w_sb[:, kc, ch * 128:(ch + 1) * 128], rhs=hT[:, kc, :],
                            start=(kc == 0), stop=(kc == 7)), reads=[Bw, Bh], writes=[PSB[pb]], sig=(kc == 7))
                    if ch < 2 or ch >= 7:
                        k2 = oc_ % 2
                        oc_ += 1
                        kb.op(ACT, lambda e, pb=pb, k2=k2: e.activation(out=of[k2][:, :], in_=psb[pb][:, :], func=AF.Identity),
                              reads=[PSB[pb]], writes=[Bof[k2]])
                        dst = A(up_s)[ch, :, t0:t0 + 512] if ch < 2 else A(uh_s)[ch - 7, :, t0:t0 + 512]
                        kb.dma(SP, dst, of[k2][:, :], reads=[Bof[k2]], sembuf=Bof[k2])
                        continue
                    cp = chain[0] % 2
                    chain[0] += 1
                    sqh, rs, qn, qnb, t1, t2 = sqh_[cp], rs_[cp], qn_[cp], qnb_[cp], t1_[cp], t2_[cp]
                    Bsqh, Brs2, Bqn, Bqnb, Bt1, Bt2 = Bsqh_[cp], Brs2_[cp], Bqn_[cp], Bqnb_[cp], Bt1_[cp], Bt2_[cp]
                    bk = 4 + cp
                    isq = ch < 6
                    gcol = 0 if isq else 1
                    kb.op(ACT, lambda e, pb=pb, sqh=sqh: e.activation(out=sqh[:, :], in_=psb[pb][:, :], func=AF.Square),
                          reads=[PSB[pb]], writes=[Bsqh])
                    kb.op(PE, lambda e, bk=bk, sqh=sqh: e.matmul(psb[bk][:, :], lhsT=bd_bf[:, :], rhs=sqh[:, :], start=True, stop=True),
                          reads=[Bsqh, B_const], writes=[PSB[bk]])
                    kb.op(ACT, lambda e, bk=bk, rs=rs: e.activation(out=rs[:, :], in_=psb[bk][:, :], func=AF.Sqrt, bias=EPS_AP[:, 0:1]),
                          reads=[PSB[bk]], writes=[Brs2])
                    kb.op(DVE, lambda e, rs=rs: e.reciprocal(out=rs[:, :], in_=rs[:, :]), reads=[Brs2], writes=[Brs2])
                    kb.op(DVE, lambda e, pb=pb, gcol=gcol, qn=qn, rs=rs: e.scalar_tensor_tensor(
                        out=qn[:, :], in0=psb[pb][:, :], scalar=qkn_sb[:, gcol:gcol + 1], in1=rs[:, :],
                        op0=ALU.mult, op1=ALU.mult), reads=[PSB[pb], Brs2, Bc], writes=[Bqn])
                    k2 = oc_ % 2
                    oc_ += 1
                    if isS:
                        kb.op(ACT, lambda e, qnb=qnb, qn=qn: e.activation(out=qnb[:, :], in_=qn[:, :], func=AF.Identity),
                              reads=[Bqn], writes=[Bqnb])
                        kb.op(PE, lambda e, bk=bk, qnb=qnb: e.matmul(psb[bk][:, :], lhsT=pm_bf[:, :], rhs=qnb[:, :], start=True, stop=True),
                              reads=[Bqnb, B_const], writes=[PSB[bk]])
                        kb.op(DVE, lambda e, t0=t0, t1=t1, qn=qn: e.tensor_tensor(out=t1[:, :], in0=qn[:, :], in1=cos_sb[:, t0:t0 + 512], op=ALU.mult),
                              reads=[Bqn, Bc], writes=[Bt1])
                        kb.op(DVE, lambda e, t0=t0, t2=t2, bk=bk: e.tensor_tensor(out=t2[:, :], in0=psb[bk][:, :], in1=sin_sb[:, t0:t0 + 512], op=ALU.mult),
                              reads=[PSB[bk], Bc], writes=[Bt2])
                        kb.op(DVE, lambda e, k2=k2, t1=t1, t2=t2: e.tensor_tensor(out=ob[k2][:, :], in0=t1[:, :], in1=t2[:, :], op=ALU.add),
                              reads=[Bt1, Bt2], writes=[Bob[k2]])
                    else:
                        kb.op(ACT, lambda e, k2=k2, qn=qn: e.activation(out=ob[k2][:, :], in_=qn[:, :], func=AF.Identity),
                              reads=[Bqn], writes=[Bob[k2]])
                        if not isq:
                            k3 = oc_ % 2
                            oc_ += 1
                            kb.op(ACT, lambda e, k3=k3, qn=qn: e.activation(out=of[k3][:, :], in_=qn[:, :], func=AF.Identity),
                                  reads=[Bqn], writes=[Bof[k3]])
                            kb.dma(SP, A(okT)[l, :, t0 - TS:t0 - TS + 512], of[k3][:, :], reads=[Bof[k3]], sembuf=Bof[k3])
                    dst = A(qT_s)[ch - 2, :, t0:t0 + 512] if isq else A(kT_s)[:, t0:t0 + 512]
                    kb.dma(SP, dst, ob[k2][:, :], reads=[Bob[k2]], sembuf=Bob[k2])
                for tb in range(4):
                    for kc in range(8):
                        kb.op(PE, lambda e, tb=tb, kc=kc: e.matmul(
                            psb[6][:, 0:128], lhsT=hT[:, kc, tb * 128:(tb + 1) * 128], rhs=wv_sb[:, kc, :],
                            start=(kc == 0), stop=(kc == 7)), reads=[Bw, Bh], writes=[PSB[6]], sig=(kc == 7))
                    k2 = vc_ % 2
                    vc_ += 1
                    if not isS:
                        kb.op(ACT, lambda e, k2=k2: e.activation(out=vf[k2][:, :], in_=psb[6][:, 0:128], func=AF.Identity),
                              reads=[PSB[6]], writes=[Bvf[k2]])
                        r0 = t0 - TS + tb * 128
                        kb.dma(SP, A(ov)[l, r0:r0 + 128, :], vf[k2][:, :], reads=[Bvf[k2]], sembuf=Bvf[k2])
                    if not isS:
                        kb.op(DVE, lambda e, k2=k2: e.tensor_copy(out=vb[k2][:, :], in_=vf[k2][:, :]),
                              reads=[Bvf[k2]], writes=[Bvb[k2]])
                    else:
                        kb.op(DVE, lambda e, k2=k2: e.tensor_copy(out=vb[k2][:, :], in_=psb[6][:, 0:128]),
                              reads=[PSB[6]], writes=[Bvb[k2]])
                    r0 = t0 + tb * 128
                    kb.dma(SP, A(v_s)[r0:r0 + 128, :], vb[k2][:, :], reads=[Bvb[k2]], sembuf=Bvb[k2])
                if blk + 1 < NBK:
                    a_norm(blk + 1)
                if blk + 2 < NBK:
                    a_load(blk + 2)
            kb.barrier()
            kb.release(Bxs_[0] + Bxs_[1] + [Bw, Bc] + Bob + Bof + Bvb + Bvf)

    def attn_gen(l, es, Q, relbufs):
        if True:
            q_sb = [stage_sb(es, "b_q%d" % k, [128, 4, 128], BF16) for k in range(2)]
            k_sb = [stage_sb(es, "b_k%d" % k, [128, 3, 128], BF16) for k in range(2)]
            v_sb = [stage_sb(es, "b_v%d" % k, [128, 3, 128], BF16) for k in range(2)]
            ck_sb = stage_sb(es, "b_ck", [128, 256], BF16)
            cv_sb = stage_sb(es, "b_cv", [128, 2, 128], BF16)
            sk = stage_sb(es, "b_sk", [64, 2, 4], F32)
            pT = [stage_sb(es, "b_pT%d" % k, [128, 512], BF16) for k in range(3)]
            d2 = stage_sb(es, "b_d2", [64, 512], F32)
            o_sb = [stage_sb(es, "b_o%d" % k, [64, 512], BF16) for k in range(2)]
            Bq = [Buf(), Buf()]
            Bk = [Buf(), Buf()]
            Bv = [Buf(), Buf()]
            Bck, Bsk, Bd2 = Buf(), Buf(), Buf()
            BpT = [Buf(), Buf(), Buf()]
            Bo = [Buf(), Buf()]
            relbufs.extend(Bq + Bk + Bv + [Bck, Bsk] + Bo)
            kb.dma(POOL, ck_sb[:, :], A(ckT)[l], writes=[Bck], sembuf=Bck)
            kb.dma(POOL, cv_sb[:, :, :], A(cv)[l].rearrange("(b p) d -> p b d", p=128), writes=[Bck], sembuf=Bck)
            for kh in range(2):
                kb.dma(Q, sk[:, kh, :], A(sinkb)[l, kh], writes=[Bsk], sembuf=Bsk)
            kb.op(ACT, lambda e: e.activation(out=sk[:, :, :], in_=sk[:, :, :], func=AF.Exp), reads=[], writes=[Bsk])
            pcount = 0
            ocount = 0
            for qb in range(NBLK):
                isS = qb < TS // 128
                s = qb % 2
                t0 = qb * 128
                if isS:
                    kbl = [b for b in (qb - 1, qb, qb + 1) if 0 <= b < TS // 128]
                else:
                    sq0 = (qb - TS // 128) // 2 * 2 + TS // 128
                    kbl = [sq0, sq0 + 1]
                kb.dma(Q, q_sb[s][:, :, :], A(qT_s)[:, :, t0:t0 + 128].rearrange("c p t -> p c t"),
                       writes=[Bq[s]], sembuf=Bq[s])
                k0 = kbl[0] * 128
                nk = len(kbl)
                kb.dma(Q, k_sb[s][:, 0:nk, :], A(kT_s)[:, k0:k0 + nk * 128].rearrange("p (b t) -> p b t", t=128),
                       writes=[Bk[s]], sembuf=Bk[s])
                kb.dma(Q, v_sb[s][:, 0:nk, :], A(v_s)[k0:k0 + nk * 128, :].rearrange("(b p) d -> p b d", p=128),
                       writes=[Bv[s]], sembuf=Bv[s])
                for kh in range(2):
                    pr = slice(64 * kh, 64 * kh + 64)
                    keys = [("l", j, kbl[j]) for j in range(nk)]
                    if isS:
                        keys += [("c", 0, 0), ("c", 1, 0)]
                    nkeys = len(keys)
                    for idx, (kind, j, gb) in enumerate(keys):
                        if kind == "l":
                            lk = k_sb[s][pr, j, :]
                            lv = v_sb[s][:, j, 64 * kh:64 * kh + 64]
                            rk = [Bk[s]]
                            rv = [Bv[s]]
                        else:
                            lk = ck_sb[pr, j * 128:(j + 1) * 128]
                            lv = cv_sb[:, j, 64 * kh:64 * kh + 64]
                            rk = [Bck]
                            rv = [Bck]
                        sb_ = 3 + idx % 2
                        kb.op(PE, lambda e, lk=lk, s=s, pr=pr, sb_=sb_: e.matmul(
                            psb[sb_][:, :], lhsT=lk, rhs=q_sb[s][pr, :, :], start=True, stop=True),
                            reads=rk + [Bq[s]], writes=[PSB[sb_]])
                        pi = pcount % 3
                        pcount += 1
                        kb.op(ACT, lambda e, pi=pi, sb_=sb_: e.activation(out=pT[pi][:, :], in_=psb[sb_][:, :], func=AF.Exp, scale=0.125),
                              reads=[PSB[sb_]], writes=[BpT[pi]])
                        if kind == "l" and isS and gb != qb:
                            mi = 0 if gb < qb else 1
                            mh = mask_f[:, mi, :]
                            map_ = bass.AP(mask_f, mh.offset, [list(mh.ap[0]), [0, 4], [1, 128]])
                            kb.op(DVE, lambda e, pi=pi, map_=map_: e.tensor_tensor(
                                out=pT[pi][:, :].rearrange("p (c t) -> p c t", c=4),
                                in0=pT[pi][:, :].rearrange("p (c t) -> p c t", c=4), in1=map_, op=ALU.mult),
                                reads=[B_const], writes=[BpT[pi]])
                        kb.op(PE, lambda e, lv=lv, pi=pi, idx=idx, nkeys=nkeys: e.matmul(
                            psb[5][0:64, :], lhsT=lv, rhs=pT[pi][:, :], start=(idx == 0), stop=(idx == nkeys - 1)),
                            reads=rv + [BpT[pi]], writes=[PSB[5]], sig=False)
                        kb.op(PE, lambda e, pi=pi, idx=idx, nkeys=nkeys: e.matmul(
                            psb[6][0:64, :], lhsT=ones64[:, :], rhs=pT[pi][:, :], start=(idx == 0), stop=(idx == nkeys - 1)),
                            reads=[BpT[pi], B_const], writes=[PSB[6]], sig=True)
                    skb = sk[:, kh, :]
                    skap = bass.AP(sk, skb.offset, [list(skb.ap[0]), [1, 4], [0, 128]])
                    kb.op(DVE, lambda e, skap=skap: e.tensor_tensor(
                        out=d2[:, :].rearrange("p (c t) -> p c t", c=4),
                        in0=psb[6][0:64, :].rearrange("p (c t) -> p c t", c=4), in1=skap, op=ALU.add),
                        reads=[PSB[6], Bsk], writes=[Bd2])
                    kb.op(DVE, lambda e: e.reciprocal(out=d2[:, :], in_=d2[:, :]), reads=[Bd2], writes=[Bd2])
                    oi = ocount % 2
                    ocount += 1
                    kb.op(DVE, lambda e, oi=oi: e.tensor_tensor(out=o_sb[oi][:, :], in0=psb[5][0:64, :], in1=d2[:, :], op=ALU.mult),
                          reads=[PSB[5], Bd2], writes=[Bo[oi]])
                    dst = A(mix_s)[2:6, 64 * kh:64 * kh + 64, t0:t0 + 128].rearrange("c p t -> p c t")
                    kb.dma(Q, dst, o_sb[oi][:, :].rearrange("p (c t) -> p c t", c=4), reads=[Bo[oi]], sembuf=Bo[oi])
                    yield

    def attn_stage(l):
        with ExitStack() as es:
            rel = []
            for _ in attn_gen(l, es, SP, rel):
                pass
            kb.barrier()
            kb.release(rel)

    def pool_stage(l):
        for (nseq, L, tbase, invc_d) in ((1, TS, 0, c_invc_s), (NPS, 256, TS, c_invc_p)):
            with ExitStack() as es:
                LP = L + 16
                u = stage_sb(es, "c_u", [128, nseq, LP], F32)
                wa = stage_sb(es, "c_wa", [128, nseq, LP], F32)
                wb_ = stage_sb(es, "c_wb", [128, nseq, LP], F32)
                sel = stage_sb(es, "c_sel", [128, nseq, L], F32)
                invc = stage_sb(es, "c_invc", [128, L], F32)
                dT = stage_sb(es, "c_dT", [128, nseq, L], BF16)
                pw = stage_sb(es, "c_pw", [128, 128], BF16)
                psc = stage_sb(es, "c_psc", [128, 2], F32)
                yo = [stage_sb(es, "c_yo%d" % k, [128, 512], BF16) for k in range(2)]
                Bu, Bwa, Bwb, Bsel, Binv, Bd, Bpw = Buf(), Buf(), Buf(), Buf(), Buf(), Buf(), Buf()
                Byo = [Buf(), Buf()]
                Bpsc = Buf()
                kb.dma(SP, psc[:, :], A(poolsc)[l], writes=[Bpsc], sembuf=Bpsc)
                yc = 0
                for ch in range(2):
                    kb.op(DVE, lambda e: e.memset(u[:, :, :], 0.0), writes=[Bu])
                    kb.op(DVE, lambda e: e.memset(wa[:, :, :], 0.0), writes=[Bwa])
                    kb.op(DVE, lambda e: e.memset(wb_[:, :, :], 0.0), writes=[Bwb])
                    for sq_ in range(nseq):
                        kb.dma2(SP, u[:, sq_, 8:8 + L], A(up_s)[ch, :, tbase + sq_ * L:tbase + (sq_ + 1) * L], L,
                                writes=[Bu], sembuf=Bu)
                    kb.dma2(SP, invc[:, :], A(invc_d)[ch], L, writes=[Binv], sembuf=Binv)
                    kb.dma(POOL, pw[:, :], A(poolw)[l, ch], writes=[Bpw], sembuf=Bpw)
                    kb.op(DVE, lambda e: e.tensor_tensor(out=wa[:, :, 1:LP], in0=u[:, :, 0:LP - 1], in1=u[:, :, 1:LP], op=ALU.add),
                          reads=[Bu], writes=[Bwa])
                    kb.op(DVE, lambda e: e.tensor_tensor(out=wb_[:, :, 2:LP - 1], in0=wa[:, :, 1:LP - 2], in1=wa[:, :, 3:LP], op=ALU.add),
                          reads=[Bwa], writes=[Bwb])
                    if ch == 0:
                        kb.op(DVE, lambda e: e.tensor_copy(out=sel[0:64, :, :], in_=wa[0:64, :, 8:8 + L]), reads=[Bwa], writes=[Bsel])
                        kb.op(DVE, lambda e: e.tensor_copy(out=sel[64:128, :, :], in_=wb_[64:128, :, 8:8 + L]), reads=[Bwb], writes=[Bsel])
                    else:
                        kb.op(DVE, lambda e: e.tensor_tensor(out=wa[:, :, 4:LP - 3], in0=wb_[:, :, 2:LP - 5], in1=wb_[:, :, 6:LP - 1], op=ALU.add),
                              reads=[Bwb], writes=[Bwa])
                        kb.op(DVE, lambda e: e.tensor_tensor(out=wb_[:, :, 8:LP - 7], in0=wa[:, :, 4:LP - 11], in1=wa[:, :, 12:LP - 3], op=ALU.add),
                              reads=[Bwa], writes=[Bwb])
                        kb.op(DVE, lambda e: e.tensor_copy(out=sel[0:64, :, :], in_=wa[0:64, :, 8:8 + L]), reads=[Bwa], writes=[Bsel])
                        kb.op(DVE, lambda e: e.tensor_copy(out=sel[64:128, :, :], in_=wb_[64:128, :, 8:8 + L]), reads=[Bwb], writes=[Bsel])
                    for sq_ in range(nseq):
                        kb.op(DVE, lambda e, sq_=sq_: e.tensor_tensor(out=sel[:, sq_, :], in0=sel[:, sq_, :], in1=invc[:, :], op=ALU.mult),
                              reads=[Binv], writes=[Bsel])
                    kb.op(DVE, lambda e: e.tensor_tensor(out=dT[:, :, :], in0=sel[:, :, :], in1=u[:, :, 8:8 + L], op=ALU.subtract),
                          reads=[Bsel, Bu], writes=[Bd])
                    dflat = dT[:, :, :].rearrange("p s t -> p (s t)")
                    for cb in range(nseq * L // 512):
                        pb = cb % 2
                        kb.op(PE, lambda e, cb=cb, pb=pb: e.matmul(psb[pb][:, :], lhsT=pw[:, :], rhs=dflat[:, cb * 512:(cb + 1) * 512],
                                                                 start=True, stop=True), reads=[Bpw, Bd], writes=[PSB[pb]])
                        k2 = yc % 2
                        yc += 1
                        kb.op(ACT, lambda e, pb=pb, k2=k2, ch=ch: e.activation(out=yo[k2][:, :], in_=psb[pb][:, :], func=AF.Identity,
                                                                             scale=psc[:, ch:ch + 1]),
                              reads=[PSB[pb], Bpsc], writes=[Byo[k2]])
                        kb.dma(SP, A(mix_s)[ch, :, tbase + cb * 512:tbase + (cb + 1) * 512], yo[k2][:, :], reads=[Byo[k2]], sembuf=Byo[k2])
                    kb.barrier()
                kb.release([Bu, Binv, Bpw, Bpsc] + Byo)

    def pool_gen(l, es, rel, Q, bank):
        CL = 1024
        LP = CL + 16
        u = stage_sb(es, "g_u", [128, LP], F32)
        wa = stage_sb(es, "g_wa", [128, LP], F32)
        wb_ = stage_sb(es, "g_wb", [128, LP], F32)
        sel = stage_sb(es, "g_sel", [128, CL], F32)
        invc = stage_sb(es, "g_invc", [128, CL], F32)
        dT = stage_sb(es, "g_dT", [128, CL], BF16)
        pw = [stage_sb(es, "g_pw%d" % k, [128, 128], BF16) for k in range(2)]
        psc = stage_sb(es, "g_psc", [128, 2], F32)
        yo = [stage_sb(es, "g_yo%d" % k, [128, 512], BF16) for k in range(2)]
        Bu, Bwa, Bwb, Bsel, Binv, Bd, Bpw, Bpsc = Buf(), Buf(), Buf(), Buf(), Buf(), Buf(), Buf(), Buf()
        Byo = [Buf(), Buf()]
        rel.extend([Bu, Binv, Bpw, Bpsc] + Byo)
        kb.dma(Q, psc[:, :], A(poolsc)[l], writes=[Bpsc], sembuf=Bpsc)
        kb.dma(Q, pw[0][:, :], A(poolw)[l, 0], writes=[Bpw], sembuf=Bpw)
        kb.dma(Q, pw[1][:, :], A(poolw)[l, 1], writes=[Bpw], sembuf=Bpw)
        yc = 0
        units = []
        for c0 in range(0, TS, CL):
            units.append((0, TS, c0, CL, c_invc_s))
        for sq_ in range(NPS):
            units.append((TS + sq_ * 256, 256, 0, 256, c_invc_p))
        for (sbase, L, c0, cl, invc_d) in units:
            lp = cl + 16
            for ch in range(2):
                kb.op(DVE, lambda e: e.memset(u[:, :], 0.0), writes=[Bu])
                kb.op(DVE, lambda e: e.memset(wa[:, :], 0.0), writes=[Bwa])
                kb.op(DVE, lambda e: e.memset(wb_[:, :], 0.0), writes=[Bwb])
                lo, hi = max(0, c0 - 8), min(L, c0 + cl + 8)
                kb.dma(Q, u[:, lo - (c0 - 8):hi - (c0 - 8)], A(up_s)[ch, :, sbase + lo:sbase + hi], writes=[Bu], sembuf=Bu)
                kb.dma(Q, invc[:, 0:cl], A(invc_d)[ch][:, c0:c0 + cl], writes=[Binv], sembuf=Binv)
                kb.op(DVE, lambda e: e.tensor_tensor(out=wa[:, 1:lp], in0=u[:, 0:lp - 1], in1=u[:, 1:lp], op=ALU.add), reads=[Bu], writes=[Bwa])
                kb.op(DVE, lambda e: e.tensor_tensor(out=wb_[:, 2:lp - 1], in0=wa[:, 1:lp - 2], in1=wa[:, 3:lp], op=ALU.add), reads=[Bwa], writes=[Bwb])
                if ch == 1:
                    kb.op(DVE, lambda e: e.tensor_tensor(out=wa[:, 4:lp - 3], in0=wb_[:, 2:lp - 5], in1=wb_[:, 6:lp - 1], op=ALU.add), reads=[Bwb], writes=[Bwa])
                    kb.op(DVE, lambda e: e.tensor_tensor(out=wb_[:, 8:lp - 7], in0=wa[:, 4:lp - 11], in1=wa[:, 12:lp - 3], op=ALU.add), reads=[Bwa], writes=[Bwb])
                kb.op(DVE, lambda e: e.tensor_copy(out=sel[0:64, 0:cl], in_=wa[0:64, 8:8 + cl]), reads=[Bwa], writes=[Bsel])
                kb.op(DVE, lambda e: e.tensor_copy(out=sel[64:128, 0:cl], in_=wb_[64:128, 8:8 + cl]), reads=[Bwb], writes=[Bsel])
                kb.op(DVE, lambda e: e.tensor_tensor(out=sel[:, 0:cl], in0=sel[:, 0:cl], in1=invc[:, 0:cl], op=ALU.mult), reads=[Binv], writes=[Bsel])
                kb.op(DVE, lambda e: e.tensor_tensor(out=dT[:, 0:cl], in0=sel[:, 0:cl], in1=u[:, 8:8 + cl], op=ALU.subtract), reads=[Bsel, Bu], writes=[Bd])
                W = min(cl, 512)
                for cb in range(cl // W):
                    kb.op(PE, lambda e, cb=cb, ch=ch: e.matmul(psb[bank][:, 0:W], lhsT=pw[ch][:, :], rhs=dT[:, cb * W:(cb + 1) * W], start=True, stop=True),
                          reads=[Bpw, Bd], writes=[PSB[bank]])
                    k2 = yc % 2
                    yc += 1
                    kb.op(ACT, lambda e, k2=k2, ch=ch: e.activation(out=yo[k2][:, 0:W], in_=psb[bank][:, 0:W], func=AF.Identity, scale=psc[:, ch:ch + 1]),
                          reads=[PSB[bank], Bpsc], writes=[Byo[k2]])
                    t_ = sbase + c0 + cb * W
                    kb.dma(Q, A(mix_s)[ch, :, t_:t_ + W], yo[k2][:, 0:W], reads=[Byo[k2]], sembuf=Byo[k2])
                yield

    def sin_act(ps_ap, out_ap, fr_ap, bb_ap, tmp, Btmp_, rd, wr, tmp2=None, Btmp2_=None):
        kb.op(DVE, lambda e: e.tensor_scalar(out=tmp, in0=ps_ap, scalar1=fr_ap, scalar2=bb_ap, op0=ALU.mult, op1=ALU.add),
              reads=rd, writes=[Btmp_])
        for _r in range(2):
            kb.op(DVE, lambda e: e.tensor_scalar(out=tmp2, in0=tmp, scalar1=math.pi, scalar2=-TWO_PI, op0=ALU.is_gt, op1=ALU.mult),
                  reads=[Btmp_], writes=[Btmp2_])
            kb.op(DVE, lambda e: e.tensor_tensor(out=tmp, in0=tmp, in1=tmp2, op=ALU.add), reads=[Btmp2_], writes=[Btmp_])
            kb.op(DVE, lambda e: e.tensor_scalar(out=tmp2, in0=tmp, scalar1=-math.pi, scalar2=TWO_PI, op0=ALU.is_lt, op1=ALU.mult),
                  reads=[Btmp_], writes=[Btmp2_])
            kb.op(DVE, lambda e: e.tensor_tensor(out=tmp, in0=tmp, in1=tmp2, op=ALU.add), reads=[Btmp2_], writes=[Btmp_])
        kb.op(DVE, lambda e: e.tensor_scalar(out=tmp, in0=tmp, scalar1=-3.1415925, scalar2=3.1415925, op0=ALU.max, op1=ALU.min),
              reads=[], writes=[Btmp_])
        kb.op(ACT, lambda e: e.activation(out=out_ap, in_=tmp, func=AF.Sin), reads=[Btmp_], writes=wr)

    def filt_gen(l, L, zT_d, tl_d, G_d, es, rel, tag):
        if True:
            zT = stage_sb(es, "d_zT" + tag, [33, L], F32)
            tl = stage_sb(es, "d_tl" + tag, [128, L], F32)
            w1 = stage_sb(es, "d_w1" + tag, [33, 64], F32)
            w2 = stage_sb(es, "d_w2" + tag, [64, 64], F32)
            w3 = stage_sb(es, "d_w3" + tag, [64, 1024], F32)
            sm = stage_sb(es, "d_sm" + tag, [64, 8], F32)
            b3 = stage_sb(es, "d_b3" + tag, [128, 8], F32)
            dec = stage_sb(es, "d_dec" + tag, [128, 4], F32)
            h1 = stage_sb(es, "d_h1" + tag, [64, L], F32)
            h2 = stage_sb(es, "d_h2" + tag, [64, L], F32)
            tmps = [stage_sb(es, "d_tmp%d" % k + tag, [128, 512], F32) for k in range(2)]
            tmp2s = [stage_sb(es, "d_tmq%d" % k + tag, [128, 512], F32) for k in range(2)]
            Btmp2s = [Buf(), Buf()]
            Btmps = [Buf(), Buf()]
            dks = [stage_sb(es, "d_dk%d" % k + tag, [128, 512], F32) for k in range(2)]
            Bdks = [Buf(), Buf()]
            fl = [stage_sb(es, "d_fl%d"  % k + tag, [128, L], F32) for k in range(2)]
            flb = [stage_sb(es, "d_flb%d"  % k + tag, [128, L], BF16) for k in range(2)]
            ssum = stage_sb(es, "d_ssum" + tag, [128, 4], F32)
            Bc, Bh1, Bh2, Bss = Buf(), Buf(), Buf(), Buf()
            Bfl = [Buf(), Buf()]
            Bflb = [Buf(), Buf()]
            for (dst, srcap, nn) in ((zT[:, :], A(zT_d)[:, :], L), (tl[:, :], A(tl_d)[:, :], L), (w1[:, :], A(fw1)[l], 64),
                                     (w2[:, :], A(fw2)[l], 64), (w3[:, :], A(fw3)[l], 1024), (sm[:, :], A(fsm)[l], 8),
                                     (b3[:, :], A(fb3)[l], 8), (dec[:, :], A(fdec)[l], 4)):
                kb.dma2(SP, dst, srcap, nn, writes=[Bc], sembuf=Bc)
            for k in range(2):
                kb.op(DVE, lambda e, k=k: e.tensor_scalar(out=sm[:, 4 + k:5 + k], in0=sm[:, k:k + 1], scalar1=sm[:, 2 + k:3 + k],
                                                         scalar2=0.0, op0=ALU.mult, op1=ALU.add), reads=[Bc], writes=[Bc])
            kb.op(ACT, lambda e: e.activation(out=dec[:, :], in_=dec[:, :], func=AF.Abs), reads=[Bc], writes=[Bc])
            kb.op(DVE, lambda e: e.tensor_scalar(out=dec[:, :], in0=dec[:, :], scalar1=-1.0, scalar2=None, op0=ALU.mult),
                  reads=[Bc], writes=[Bc])
            nb5 = L // 512 if L >= 512 else 1
            W = min(L, 512)
            for cb in range(nb5):
                cs = slice(cb * W, (cb + 1) * W)
                kb.op(PE, lambda e, cs=cs, cb=cb: e.matmul(psb[cb % 2][0:64, 0:W], lhsT=w1[:, :], rhs=zT[:, cs], start=True, stop=True),
                      reads=[Bc], writes=[PSB[cb % 2]])
                sin_act(psb[cb % 2][0:64, 0:W], h1[:, cs], sm[:, 2:3], sm[:, 4:5], tmps[cb % 2][0:64, 0:W], Btmps[cb % 2], [PSB[cb % 2], Bc], [Bh1], tmp2s[cb % 2][0:64, 0:W], Btmp2s[cb % 2])
                yield
            for cb in range(nb5):
                cs = slice(cb * W, (cb + 1) * W)
                kb.op(PE, lambda e, cs=cs, cb=cb: e.matmul(psb[cb % 2][0:64, 0:W], lhsT=w2[:, :], rhs=h1[:, cs], start=True, stop=True),
                      reads=[Bc, Bh1], writes=[PSB[cb % 2]])
                sin_act(psb[cb % 2][0:64, 0:W], h2[:, cs], sm[:, 3:4], sm[:, 5:6], tmps[cb % 2][0:64, 0:W], Btmps[cb % 2], [PSB[cb % 2], Bc], [Bh2], tmp2s[cb % 2][0:64, 0:W], Btmp2s[cb % 2])
                yield
            for o in range(2):
                for half in range(2):
                    oh = o * 2 + half
                    for d in range(2):
                        fc = d * 4 + oh
                        for cb in range(nb5):
                            cs = slice(cb * W, (cb + 1) * W)
                            pb = 2 + cb % 2
                            kb.op(PE, lambda e, cs=cs, fc=fc, pb=pb: e.matmul(psb[pb][:, 0:W], lhsT=w3[:, fc * 128:(fc + 1) * 128],
                                                                          rhs=h2[:, cs], start=True, stop=True),
                                  reads=[Bc, Bh2], writes=[PSB[pb]])
                            dk, Bdk = dks[cb % 2], Bdks[cb % 2]
                            kb.op(ACT, lambda e, cs=cs, oh=oh, dk=dk: e.activation(out=dk[:, 0:W], in_=tl[:, cs], func=AF.Exp, scale=dec[:, oh:oh + 1]),
                                  reads=[Bc], writes=[Bdk])
                            kb.op(DVE, lambda e, dk=dk: e.tensor_scalar(out=dk[:, 0:W], in0=dk[:, 0:W], scalar1=0.05, scalar2=None, op0=ALU.add),
                                  reads=[], writes=[Bdk])
                            kb.op(DVE, lambda e, cs=cs, fc=fc, pb=pb, d=d, dk=dk: e.scalar_tensor_tensor(
                                out=fl[d][:, cs], in0=psb[pb][:, 0:W], scalar=b3[:, fc:fc + 1], in1=dk[:, 0:W], op0=ALU.add, op1=ALU.mult),
                                reads=[PSB[pb], Bdk, Bc], writes=[Bfl[d]])
                            yield
                        kb.op(DVE, lambda e, d=d: e.tensor_reduce(out=ssum[:, d:d + 1], in_=fl[d][:, :], axis=AX.X, op=ALU.add,
                                                                 apply_absolute_value=True), reads=[Bfl[d]], writes=[Bss])
                    kb.op(ACT, lambda e: e.activation(out=ssum[:, 2:3], in_=fl[1][:, 0:1], func=AF.Abs), reads=[Bfl[1]], writes=[Bss])
                    kb.op(DVE, lambda e: e.tensor_scalar(out=ssum[:, 2:3], in0=ssum[:, 2:3], scalar1=-1.0, scalar2=None, op0=ALU.mult),
                          reads=[], writes=[Bss])
                    kb.op(DVE, lambda e: e.tensor_tensor(out=ssum[:, 3:4], in0=ssum[:, 0:1], in1=ssum[:, 1:2], op=ALU.add), reads=[], writes=[Bss])
                    kb.op(DVE, lambda e: e.scalar_tensor_tensor(out=ssum[:, 3:4], in0=ssum[:, 3:4], scalar=1e-6, in1=ssum[:, 2:3], op0=ALU.add, op1=ALU.add),
                          reads=[], writes=[Bss])
                    kb.op(DVE, lambda e: e.reciprocal(out=ssum[:, 3:4], in_=ssum[:, 3:4]), reads=[], writes=[Bss])
                    kb.op(ACT, lambda e: e.activation(out=flb[0][:, :], in_=fl[0][:, :], func=AF.Identity, scale=ssum[:, 3:4]),
                          reads=[Bfl[0], Bss], writes=[Bflb[0]])
                    f1 = flb[1][:, :]
                    rev = bass.AP(flb[1], f1.offset + L - 1, [list(f1.ap[0]), [-1, L]])
                    kb.op(ACT, lambda e, rev=rev: e.activation(out=rev, in_=fl[1][:, :], func=AF.Identity, scale=ssum[:, 3:4]),
                          reads=[Bfl[1], Bss], writes=[Bflb[1]])
                    kb.dma2(SP, A(G_d)[o, half * 128:(half + 1) * 128, L - 1:2 * L - 1], flb[0][:, :], L, reads=[Bflb[0]], sembuf=Bflb[0])
                    kb.dma2(SP, A(G_d)[o, half * 128:(half + 1) * 128, 0:L - 1], flb[1][:, 0:L - 1], L - 1, reads=[Bflb[1]], sembuf=Bflb[1])
                    yield
            rel.extend([Bc] + Bflb)

    def filt_stage(l, L, zT_d, tl_d, G_d):
        with ExitStack() as es:
            rel = []
            for _ in filt_gen(l, L, zT_d, tl_d, G_d, es, rel, "X"):
                pass
            kb.barrier()
            kb.release(rel)

    def hyena_gen(l, nseq, L, tbase, G_d, es, rel, cbanks, tbank, NST, pump=None, tag="S"):
        BS = 128
        nb = L // BS
        nblk = nseq * nb
        SW = BS * (2 * nb - 1)
        GW = 2 * L - 1
        cpb = 512 // nblk
        TPB = 4
        raw = stage_sb(es, "h_raw" + tag, [128, nseq, L + 2], F32)
        cv_ = [stage_sb(es, "h_c%d" % k + tag, [128, nseq, L], F32) for k in range(3)]
        zrev = stage_sb(es, "h_zrev" + tag, [128, nblk, BS], F32)
        zf = stage_sb(es, "h_zf" + tag, [BS, nblk, 128], BF16)
        ytm = raw[:, :, :].rearrange("p s t -> p (s t)")[:, 0:nblk * 128].rearrange("p (b c) -> p b c", c=128)
        strips = [stage_sb(es, "h_st%d" % k + tag, [BS, SW], BF16) for k in range(NST)]
        swt = stage_sb(es, "h_sw" + tag, [128, 6, 4], F32)
        hb = stage_sb(es, "h_hb" + tag, [128, 4], F32)
        yo = zf[:, :, :].rearrange("p b c -> p (b c)").rearrange("p (s t) -> p s t", s=nseq)
        z1 = cv_[1]
        Braw, Bzrev, Bzf, Bsw = Buf(), Buf(), Buf(), Buf()
        Bytm, BcT, Byo = Braw, Bzrev, Bzf
        convT = zrev[:, :, :].rearrange("p (s b) t -> p s (b t)", s=nseq)
        cflat = zrev[:, :, :].rearrange("p b t -> p (b t)")
        Bcv = [Buf(), Buf(), Buf()]
        Bz1 = Bcv[1]
        Bst = [Buf() for _ in range(NST)]
        rel.extend([Braw, Bsw, Bzf] + Bst)
        kb.dma(SP, swt[:, :, :], A(shw)[l], writes=[Bsw], sembuf=Bsw)
        kb.dma(SP, hb[:, :], A(hyb)[l], writes=[Bsw], sembuf=Bsw)
        Dl = [0] + [d for d in range(-(nb - 1), nb) if d != 0]
        scount = 0
        gcount = 0
        for half in range(2):
            for part in range(3):
                chn = part * 2 + half
                kb.op(DVE, lambda e: e.memset(raw[:, :, :], 0.0), writes=[Braw])
                for sq_ in range(nseq):
                    kb.dma2(SP, raw[:, sq_, 1:L + 1], A(uh_s)[chn, :, tbase + sq_ * L:tbase + (sq_ + 1) * L], L, writes=[Braw], sembuf=Braw)
                kb.op(ACT, lambda e, part=part, chn=chn: e.activation(out=cv_[part][:, :, :], in_=raw[:, :, 1:L + 1], func=AF.Identity,
                                                                   scale=swt[:, chn, 1:2], bias=swt[:, chn, 3:4]),
                      reads=[Braw, Bsw], writes=[Bcv[part]])
                kb.op(DVE, lambda e, part=part, chn=chn: e.scalar_tensor_tensor(out=cv_[part][:, :, :], in0=raw[:, :, 0:L], scalar=swt[:, chn, 0:1],
                                                                             in1=cv_[part][:, :, :], op0=ALU.mult, op1=ALU.add),
                      reads=[Braw, Bsw], writes=[Bcv[part]])
                kb.op(DVE, lambda e, part=part, chn=chn: e.scalar_tensor_tensor(out=cv_[part][:, :, :], in0=raw[:, :, 2:L + 2], scalar=swt[:, chn, 2:3],
                                                                             in1=cv_[part][:, :, :], op0=ALU.mult, op1=ALU.add),
                      reads=[Braw, Bsw], writes=[Bcv[part]])
                yield
            for o in range(2):
                zsrc = cv_[0] if o == 0 else z1
                Bzs = Bcv[0] if o == 0 else Bz1
                zs = zsrc[:, :, :]
                pstep = list(zs.ap[0])
                revap = bass.AP(zsrc, zs.offset + BS - 1, [pstep, [BS, nblk], [-1, BS]])
                kb.op(POOL, lambda e, revap=revap: e.tensor_copy(out=zrev[:, :, :], in_=revap), reads=[Bzs], writes=[Bzrev])
                for b0 in range(0, nblk, TPB):
                    for b in range(b0, b0 + TPB):
                        kb.op(PE, lambda e, b=b, b0=b0: e.transpose(psb[tbank][0:BS, (b - b0) * 128:(b - b0 + 1) * 128], zrev[:, b, :], id_f[:, :]),
                              reads=[Bzrev, B_const], writes=[PSB[tbank]], sig=(b == b0 + TPB - 1))
                    kb.op(ACT, lambda e, b0=b0: e.activation(out=zf[:, b0:b0 + TPB, :].rearrange("p b c -> p (b c)"), in_=psb[tbank][0:BS, :], func=AF.Identity),
                          reads=[PSB[tbank]], writes=[Bzf])
                    yield
                for c0 in range(0, 128, cpb):
                    pb = cbanks[gcount % len(cbanks)]
                    gcount += 1
                    for c in range(c0, c0 + cpb):
                        s = scount % NST
                        scount += 1
                        if _DBG.get("split64") and tag == "S":
                            wfrac = _DBG.get("wfrac", 1.0)
                            SWx = int(SW * wfrac)
                            for hp in range(2):
                                src = bass.AP(G_d, (o * 256 + half * 128 + c) * GW + 64 * hp, [[1, 64], [1, SWx]])
                                kb.dma2(SP, strips[s][64 * hp:64 * hp + 64, 0:SWx], src, SWx, writes=([Bst[s]] if hp == 0 else []), sembuf=Bst[s], maxel=4096)
                            Bst[s].w = kb.lasttok(Bst[s], False)
                        else:
                            src = bass.AP(G_d, (o * 256 + half * 128 + c) * GW, [[1, BS], [1, SW]])
                            kb.dma2(SP, strips[s][:, :], src, SW, writes=[Bst[s]], sembuf=Bst[s], maxel=_DBG.get("smax", 8192))
                        col0 = (c - c0) * nblk
                        for di, Dd in enumerate(Dl):
                            J0, J1 = max(0, -Dd), min(nb, nb - Dd)
                            zfa = zf[:, :, :].rearrange("p (s b) c -> p s b c", s=nseq)[:, :, J0:J1, c]
                            oa = psb[pb][0:BS, col0:col0 + nblk].rearrange("p (s b) -> p s b", s=nseq)[:, :, J0 + Dd:J1 + Dd]
                            kb.op(PE, lambda e, s=s, Dd=Dd, zfa=zfa, oa=oa, di=di: e.matmul(
                                oa, lhsT=strips[s][:, BS * (Dd + nb - 1):BS * (Dd + nb)], rhs=zfa,
                                start=(di == 0), stop=(di == len(Dl) - 1)),
                                reads=[Bst[s], Bzf], writes=[PSB[pb]], sig=(di == len(Dl) - 1))
                        if pump is not None:
                            pump()
                        if c != c0 + cpb - 1:
                            yield
                    kb.op(ACT, lambda e, c0=c0, pb=pb: e.activation(
                        out=ytm[:, :, c0:c0 + cpb], in_=psb[pb][0:BS, :].rearrange("p (c b) -> p b c", b=nblk), func=AF.Identity),
                        reads=[PSB[pb]], writes=[Bytm])
                    yield
                for b0 in range(0, nblk, TPB):
                    for b in range(b0, b0 + TPB):
                        kb.op(PE, lambda e, b=b, b0=b0: e.transpose(psb[tbank][:, (b - b0) * BS:(b - b0 + 1) * BS], ytm[:, b, :], id_f[0:BS, 0:BS]),
                              reads=[Bytm, B_const], writes=[PSB[tbank]], sig=(b == b0 + TPB - 1))
                    kb.op(ACT, lambda e, b0=b0: e.activation(out=cflat[:, b0 * BS:(b0 + TPB) * BS], in_=psb[tbank][:, :], func=AF.Identity),
                          reads=[PSB[tbank]], writes=[BcT])
                    yield
                oh = o * 2 + half
                kb.op(DVE, lambda e, zsrc=zsrc, oh=oh: e.scalar_tensor_tensor(out=convT, in0=zsrc[:, :, :], scalar=hb[:, oh:oh + 1],
                                                                           in1=convT, op0=ALU.mult, op1=ALU.add),
                      reads=[Bzs, Bsw], writes=[BcT])
                if o == 0:
                    kb.op(DVE, lambda e: e.tensor_tensor(out=z1[:, :, :], in0=convT, in1=cv_[1][:, :, :], op=ALU.mult),
                          reads=[BcT], writes=[Bz1])
                else:
                    kb.op(DVE, lambda e: e.tensor_tensor(out=yo, in0=convT, in1=cv_[2][:, :, :], op=ALU.mult),
                          reads=[BcT, Bcv[2]], writes=[Byo])
                    for sq_ in range(nseq):
                        kb.dma2(SP, A(mix_s)[6 + half, :, tbase + sq_ * L:tbase + (sq_ + 1) * L], yo[:, sq_, :], L, reads=[Byo], sembuf=Byo)
                yield

    def hyena_all(l, with_attn, with_pool=False):
        with ExitStack() as es:
            rel = []
            ag = attn_gen(l, es, POOL, rel) if with_attn else iter(())
            next(ag, None)
            pg = hyena_gen(l, NPS, 256, TS, G_p, es, rel, [2], 7, 3, tag="P")
            next(pg, None)
            og = pool_gen(l, es, rel, POOL, 7) if with_pool else iter(())
            next(og, None)
            cnt = [0]

            def pump():
                cnt[0] += 1
                next(pg, None)
                if cnt[0] % 3 == 0:
                    next(pg, None)
                if cnt[0] % 5 == 0:
                    next(ag, None)
                if cnt[0] % 16 == 0:
                    next(og, None)
            with ExitStack() as es2:
                for _ in hyena_gen(l, 1, TS, 0, G_s, es2, rel, [0, 1], 7, _DBG.get("nst", 4), pump=pump, tag="S"):
                    pass
                for _ in pg:
                    pass
                for _ in ag:
                    pass
                for _ in og:
                    pass
                kb.barrier()
            kb.release(rel)

    def mixe_stage(l):
        NB = T // 512
        with ExitStack() as es:
            xts = [stage_sb(es, "e_xt%d" % k, [128, 8, 512], F32) for k in range(2)]
            mts = [stage_sb(es, "e_mt%d" % k, [128, 8, 512], BF16) for k in range(2)]
            wo = stage_sb(es, "e_wo", [128, 8, 1024], BF16)
            Bxs = [[Buf() for _ in range(8)] for _ in range(2)]
            Bms = [Buf(), Buf()]
            Bw = Buf()
            for kc in range(8):
                kb.dma(POOL, wo[:, kc, :], A(wout)[l][:, kc * 1024:(kc + 1) * 1024], writes=([Bw] if kc == 0 else []), sembuf=Bw)
            Bw.w = kb.lasttok(Bw, True)

            def load(blk):
                xt, Bx, mt, Bm = xts[blk % 2], Bxs[blk % 2], mts[blk % 2], Bms[blk % 2]
                t0 = blk * 512
                for kc in range(8):
                    kb.dma(SP, xt[:, kc, :], A(yT)[:, kc, t0:t0 + 512], writes=[Bx[kc]], sembuf=Bx[kc])
                kb.dma(SP, mt[:, :, :], A(mix_s)[:, :, t0:t0 + 512].rearrange("c p t -> p c t"), writes=[Bm], sembuf=Bm)

            load(0)
            for blk in range(NB):
                xt, Bx, mt, Bm = xts[blk % 2], Bxs[blk % 2], mts[blk % 2], Bms[blk % 2]
                t0 = blk * 512
                ci = 0 if t0 < TS else 1
                if blk + 1 < NB:
                    load(blk + 1)
                for oc in range(8):
                    pb = oc % 4
                    for kc in range(8):
                        kb.op(PE, lambda e, oc=oc, kc=kc, pb=pb, mt=mt: e.matmul(psb[pb][:, :], lhsT=wo[:, kc, oc * 128:(oc + 1) * 128], rhs=mt[:, kc, :],
                                                                              start=(kc == 0), stop=(kc == 7)), reads=[Bw, Bm], writes=[PSB[pb]], sig=(kc == 7))
                    kb.op(DVE, lambda e, oc=oc, pb=pb, ci=ci, xt=xt: e.scalar_tensor_tensor(out=xt[:, oc, :], in0=psb[pb][:, :], scalar=Gg[:, 1, oc, ci:ci + 1],
                                                                                         in1=xt[:, oc, :], op0=ALU.mult, op1=ALU.add),
                          reads=[PSB[pb], B_mod], writes=[Bx[oc]])
                for kc in range(8):
                    kb.dma(SP, A(yT)[:, kc, t0:t0 + 512], xt[:, kc, :], reads=[Bx[kc]], sembuf=Bx[kc])
            kb.barrier()
            kb.release(Bxs[0] + Bxs[1] + Bms + [Bw])

    def want(name):
        return stages is None or name in stages
    kb.barrier()
    for l in range(2):
        fused_filt = want("mod%d" % l) and want("filt%d" % l) and _DBG.get("ffilt", 1)
        if fused_filt:
            with ExitStack() as fes:
                frel = []
                fgs = filt_gen(l, TS, c_zT_s, c_tl_s, G_s, fes, frel, "S")
                next(fgs, None)
                fgp = filt_gen(l, 256, c_zT_p, c_tl_p, G_p, fes, frel, "P")
                next(fgp, None)

                def fpump(fgs=fgs, fgp=fgp):
                    for _ in range(2):
                        if next(fgs, "done") == "done":
                            next(fgp, None)
                mod_stage(l, fpump)
                for _ in fgs:
                    pass
                for _ in fgp:
                    pass
                kb.barrier()
                kb.release(frel)
        elif want("mod%d" % l):
            mod_stage(l)
        if want("ffn%d0" % l):
            ffn_stage(l, 0, xT_in if l == 0 else yT)
        if want("mixa%d" % l):
            mixa_stage(l)
        fused_pool = want("pool%d" % l) and want("hy%d" % l) and _DBG.get("fpool", 1)
        if want("pool%d" % l) and not fused_pool:
            pool_stage(l)
        if want("filt%d" % l) and not fused_filt:
            filt_stage(l, TS, c_zT_s, c_tl_s, G_s)
            filt_stage(l, 256, c_zT_p, c_tl_p, G_p)
        if want("hy%d" % l):
            hyena_all(l, want("attn%d" % l), fused_pool)
        elif want("attn%d" % l):
            attn_stage(l)
        if want("mixe%d" % l):
            mixe_stage(l)
        if want("ffn%d1" % l):
            ffn_stage(l, 1, yT)
    kb.barrier()
    return nc


def _consts():
    c = {}
    for (L, nm) in ((TS, "s"), (256, "p")):
        t = np.linspace(0.0, 1.0, L, dtype=np.float32)[:, None]
        bands = 16
        f = np.linspace(1e-4, bands - 1, bands, dtype=np.float32)[None, :]
        w = (2.0 * math.pi * np.arange(L, dtype=np.float32)[:, None] / L).astype(np.float32)
        z = np.concatenate([t, np.cos(f * w), -np.sin(f * w)], axis=-1).astype(np.float32)
        c["c_zT_" + nm] = np.ascontiguousarray(z.T)
        c["c_tl_" + nm] = np.ascontiguousarray(np.broadcast_to(t[:, 0][None, :], (128, L))).astype(np.float32)
        tt = np.arange(L)
        invc = np.zeros((2, 128, L), np.float32)
        for gi, win in enumerate((2, 4, 8, 16)):
            lo = np.clip(tt - win // 2, 0, L)
            hi = np.clip(tt + win // 2, 0, L)
            invc[gi // 2, (gi % 2) * 64:(gi % 2) * 64 + 64, :] = (1.0 / (hi - lo).astype(np.float32))[None, :]
        c["c_invc_" + nm] = invc
    tt = np.arange(TS)
    pos_row, pos_col = tt // 64, tt % 64
    inv = (10000.0 ** (-np.arange(16, dtype=np.float32) / 16)).astype(np.float32)
    cos = np.zeros((64, TS), np.float32)
    sin = np.zeros((64, TS), np.float32)
    for d in range(64):
        pos = pos_row if d < 32 else pos_col
        dd = d % 32
        ang = pos.astype(np.float32) * inv[dd % 16]
        cos[d] = np.cos(ang)
        sin[d] = -np.sin(ang) if dd < 16 else np.sin(ang)
    c["c_cos"] = np.concatenate([cos, cos], 0)
    c["c_sin"] = np.concatenate([sin, sin], 0)
    pm = np.zeros((128, 128), np.float32)
    for m in range(128):
        dd = m % 32
        k = m + 16 if dd < 16 else m - 16
        pm[k, m] = 1.0
    c["c_pm"] = pm
    bd = np.zeros((128, 128), np.float32)
    bd[:64, :64] = 1.0 / 64
    bd[64:, 64:] = 1.0 / 64
    c["c_bd"] = bd
    c["c_id"] = np.eye(128, dtype=np.float32)
    j = np.arange(128)[:, None]
    i = np.arange(128)[None, :]
    c["c_mask"] = np.stack([(j >= i), (j <= i)]).astype(np.float32)
    return c


_NC = None
_DBG = {}


def kernel(x_prompt, x_sample, cache_k, cache_v, c, c_ctx, ada_w, ada_b, norm_w,
           ffn_wg, ffn_wu, ffn_wd, w_in, w_out, pool_w, pool_scale, q_norm, k_norm,
           attn_sink, hy_short_w, hy_short_b, hy_f_w1, hy_f_b1, hy_f_w2, hy_f_b2,
           hy_f_w3, hy_f_b3, hy_sin_freq, hy_decay, hy_bias):
    global _NC
    f32 = np.float32
    g = lambda a: np.asarray(a, dtype=f32)
    x_prompt, x_sample, cache_k, cache_v, c, c_ctx = map(g, (x_prompt, x_sample, cache_k, cache_v, c, c_ctx))
    ada_w, ada_b, norm_w, ffn_wg, ffn_wu, ffn_wd, w_in, w_out = map(g, (ada_w, ada_b, norm_w, ffn_wg, ffn_wu, ffn_wd, w_in, w_out))
    pool_w, pool_scale, q_norm, k_norm, attn_sink = map(g, (pool_w, pool_scale, q_norm, k_norm, attn_sink))
    hy_short_w, hy_short_b, hy_f_w1, hy_f_b1, hy_f_w2, hy_f_b2 = map(g, (hy_short_w, hy_short_b, hy_f_w1, hy_f_b1, hy_f_w2, hy_f_b2))
    hy_f_w3, hy_f_b3, hy_sin_freq, hy_decay, hy_bias = map(g, (hy_f_w3, hy_f_b3, hy_sin_freq, hy_decay, hy_bias))

    def fm(v, n):
        return np.ascontiguousarray(np.swapaxes(v.reshape(v.shape[:-1] + (n, 128)), -1, -2))

    shared = dict(_consts())
    shared["adaw"] = np.ascontiguousarray(ada_w.reshape(2, 8, 128, 72, 128).transpose(0, 3, 2, 1, 4)).reshape(2, 72, 128, 1024)
    shared["adab"] = fm(ada_b, 72)
    shared["normw"] = np.ascontiguousarray(norm_w.reshape(2, 3, 8, 128).transpose(0, 3, 1, 2)).reshape(2, 128, 24)
    shared["wg"] = np.ascontiguousarray(ffn_wg.reshape(2, 2, 8, 128, NFF, 128).transpose(0, 1, 4, 3, 2, 5)).reshape(2, 2, NFF, 128, 1024)
    shared["wu"] = np.ascontiguousarray(ffn_wu.reshape(2, 2, 8, 128, NFF, 128).transpose(0, 1, 4, 3, 2, 5)).reshape(2, 2, NFF, 128, 1024)
    shared["wd"] = np.ascontiguousarray(ffn_wd.reshape(2, 2, NFF, 128, 1024).transpose(0, 1, 3, 2, 4)).reshape(2, 2, 128, NFF * 1024)
    qcols = []
    for cc in range(4):
        qcols += list(range(256 + 64 * cc, 256 + 64 * cc + 64)) + list(range(256 + 64 * (4 + cc), 256 + 64 * (4 + cc) + 64))
    colsFM = list(range(256)) + qcols + list(range(768, 896)) + list(range(1024, 1792))
    shared["win"] = np.ascontiguousarray(w_in[:, :, colsFM].reshape(2, 8, 128, 1664).transpose(0, 2, 1, 3)).reshape(2, 128, 8 * 1664)
    shared["winv"] = np.ascontiguousarray(w_in[:, :, 896:1024].reshape(2, 8, 128, 128).transpose(0, 2, 1, 3)).reshape(2, 128, 1024)
    rows = list(range(256))
    for cc in range(4):
        for kh in range(2):
            rows += list(range(256 + 64 * (4 * kh + cc), 256 + 64 * (4 * kh + cc) + 64))
    rows += list(range(768, 1024))
    shared["wout"] = np.ascontiguousarray(w_out[:, rows, :].reshape(2, 8, 128, 1024).transpose(0, 2, 1, 3)).reshape(2, 128, 8192)
    pw = np.zeros((2, 2, 128, 128), f32)
    for l in range(2):
        for ch in range(2):
            pw[l, ch, :64, :64] = pool_w[l, 2 * ch]
            pw[l, ch, 64:, 64:] = pool_w[l, 2 * ch + 1]
    shared["poolw"] = pw
    shared["poolsc"] = fm(pool_scale, 2)
    shared["qkn"] = np.ascontiguousarray(np.stack([np.concatenate([q_norm, q_norm], -1), np.concatenate([k_norm, k_norm], -1)], -1))
    shared["sinkb"] = np.ascontiguousarray(np.broadcast_to(attn_sink.reshape(2, 2, 1, 4), (2, 2, 64, 4)))
    shw = np.zeros((2, 128, 6, 4), f32)
    shw[:, :, :, 0:3] = hy_short_w.reshape(2, 3, 6, 128).transpose(0, 3, 2, 1)
    shw[:, :, :, 3] = hy_short_b.reshape(2, 6, 128).transpose(0, 2, 1)
    shared["shw"] = shw
    shared["fw1"] = hy_f_w1
    shared["fw2"] = hy_f_w2
    shared["fw3"] = hy_f_w3
    fsm = np.zeros((2, 64, 8), f32)
    fsm[:, :, 0] = hy_f_b1
    fsm[:, :, 1] = hy_f_b2
    fsm[:, :, 2] = hy_sin_freq[:, 0]
    fsm[:, :, 3] = hy_sin_freq[:, 1]
    shared["fsm"] = fsm
    shared["fb3"] = fm(hy_f_b3, 8)
    shared["fdec"] = fm(hy_decay.reshape(2, 512), 4)
    shared["hyb"] = fm(hy_bias.reshape(2, 512), 4)

    in_maps = []
    for r in range(8):
        sb = r % 4
        xs = x_sample[sb]
        xp = x_prompt[NPS * r:NPS * r + NPS].reshape(TPR, D)
        xall = np.concatenate([xs, xp], 0)
        m = dict(shared)
        m["xT"] = np.ascontiguousarray(xall.reshape(T, 8, 128).transpose(2, 1, 0))
        cd = np.stack([c[sb], c_ctx], -1)
        m["condT"] = np.ascontiguousarray(cd.reshape(8, 128, 2).transpose(1, 0, 2))
        m["ckT"] = np.ascontiguousarray(cache_k[sb].reshape(2, 256, 128).transpose(0, 2, 1))
        m["cv"] = np.ascontiguousarray(cache_v[sb].reshape(2, 256, 128))
        in_maps.append(m)

    if _DBG.get("maps_only"):
        return in_maps
    if _NC is None:
        _NC = build_program()
    res = run_bass_kernel_spmd(_NC, in_maps, core_ids=list(range(8)))
    y_prompt = np.zeros((16, 256, D), f32)
    y_sample = np.zeros((4, TS, D), f32)
    nk = np.zeros((16, 2, 256, 2, 64), f32)
    nv = np.zeros((16, 2, 256, 2, 64), f32)
    for r in range(8):
        o = res.results[r]
        yt = np.asarray(o["yT"]).transpose(2, 1, 0).reshape(T, D)
        if r < 4:
            y_sample[r] = yt[:TS]
        y_prompt[NPS * r:NPS * r + NPS] = yt[TS:].reshape(NPS, 256, D)
        okt = np.asarray(o["okT"])
        nk[NPS * r:NPS * r + NPS] = okt.transpose(2, 0, 1).reshape(NPS, 256, 2, 2, 64).transpose(0, 2, 1, 3, 4)
        ovv = np.asarray(o["ov"])
        nv[NPS * r:NPS * r + NPS] = ovv.reshape(2, NPS, 256, 2, 64).transpose(1, 0, 2, 3, 4)
    return (y_prompt, y_sample, nk, nv)
```
